# Optimizing a Trainium2 kernel written in Bass

```python
import jax, jax.numpy as jnp
from jax import lax
import numpy as np

D_MODEL = 1024
BATCH = 8
SEQ = 2048
DEPTH = 1
DEC_BATCH = 128
DEC_SEQ = 1
PAST_LEN = 16384
PAGE_SIZE = 128

R_HEADS = 8
R_HEAD_DIM = 64
R_WIDTH = R_HEADS * R_HEAD_DIM
W_LORA = 64
A_LORA = 64
G_LORA = 128
R_GN_EPS = 64e-5
M_HEADS = 4
M_HEAD_DIM = 128
M_WIDTH = M_HEADS * M_HEAD_DIM
CONV_W = 4
MLSTM_CHUNK = 64
M_GN_EPS = 1e-5
D_FF = 4 * D_MODEL
RMS_EPS = 1e-6
R_SHIFT_COLS = 3 * R_WIDTH + W_LORA + A_LORA + G_LORA
M_COLS = 3 * M_WIDTH + 2 * M_HEADS
GATE_COLS = 2 * D_MODEL
IN_COLS = R_SHIFT_COLS + M_COLS + GATE_COLS

F32 = jnp.float32

kernel_name = 'rwkv7_mlstm_gated_hybrid_step'


def _rmsnorm(x, g):
    xf = x.astype(F32)
    y = xf * lax.rsqrt(jnp.mean(xf * xf, axis=-1, keepdims=True) + RMS_EPS)
    return (y * g.astype(F32)).astype(x.dtype)


def _head_norm(y, eps):
    yf = y.astype(F32)
    mu = jnp.mean(yf, axis=-1, keepdims=True)
    var = jnp.mean(jnp.square(yf - mu), axis=-1, keepdims=True)
    return (yf - mu) * lax.rsqrt(var + eps)


def _rwkv7_recurrence(r, log_w, k, v, kk, a, S0):
    def step(S, inp):
        r_t, lw_t, k_t, v_t, kk_t, a_t = inp
        s_kk = jnp.einsum('bhvk,bhk->bhv', S, kk_t)
        S = (S * jnp.exp(lw_t)[:, :, None, :]
             - s_kk[..., None] * (kk_t * a_t)[:, :, None, :]
             + v_t[..., None] * k_t[:, :, None, :])
        return S, jnp.einsum('bhvk,bhk->bhv', S, r_t)
    xs = tuple(jnp.swapaxes(t.astype(F32), 0, 1) for t in (r, log_w, k, v, kk, a))
    S, ys = lax.scan(step, S0.astype(F32), xs)
    return jnp.swapaxes(ys, 0, 1), S


def _rwkv7_branch(pr, pr_prev, S0, p):
    B, T, _ = pr.shape
    mixed = pr + (pr_prev - pr) * p['r_mu']
    r, k, v, xw, xa, xg = jnp.split(
        mixed, [R_WIDTH, 2 * R_WIDTH, 3 * R_WIDTH, 3 * R_WIDTH + W_LORA, 3 * R_WIDTH + W_LORA + A_LORA], axis=-1)
    w = (p['r_w0'] + jnp.tanh(xw) @ p['r_w2']).astype(F32)
    log_decay = -jnp.exp(-jax.nn.softplus(-w) - 0.5)
    a = jax.nn.sigmoid(p['r_a0'] + xa @ p['r_a2'])
    g = jax.nn.sigmoid(xg) @ p['r_g2']
    hs = lambda t: t.reshape(B, T, R_HEADS, R_HEAD_DIM)
    kk = hs((k * p['r_kk']).astype(F32))
    kk = kk / jnp.maximum(jnp.sqrt(jnp.sum(kk * kk, axis=-1, keepdims=True)), 1e-12)
    k = k * (1.0 + (a - 1.0) * p['r_ka'])
    r_h, k_h, v_h, a_h = hs(r), hs(k), hs(v), hs(a)
    y, S_new = _rwkv7_recurrence(r_h, hs(log_decay), k_h, v_h, kk, a_h, S0)
    y = _head_norm(y, R_GN_EPS) * p['r_gn_w'].reshape(R_HEADS, R_HEAD_DIM) + p['r_gn_b'].reshape(R_HEADS, R_HEAD_DIM)
    y = y + jnp.sum(r_h * k_h * p['r_rk'], axis=-1, keepdims=True) * v_h
    y = (y.reshape(B, T, R_WIDTH) * g).astype(pr.dtype)
    return y, S_new


def _mlstm_chunked(q, k, v, logi, logf, C0, n0, m0, chunk):
    B, T, H, Dh = q.shape
    nc = T // chunk
    c4 = lambda t: t.astype(F32).reshape(B, nc, chunk, H, Dh).transpose(1, 0, 3, 2, 4)
    c3 = lambda t: t.astype(F32).reshape(B, nc, chunk, H).transpose(1, 0, 3, 2)
    causal = jnp.tril(jnp.ones((chunk, chunk), dtype=bool))

    def step(carry, inp):
        C, n, m = carry
        qb, kb, vb, li, lf = inp
        bcum = jnp.cumsum(lf, axis=-1)
        dmat = bcum[..., :, None] - bcum[..., None, :] + li[..., None, :]
        dmat = jnp.where(causal, dmat, -jnp.inf)
        g = bcum + m[..., None]
        m_t = jnp.maximum(g, jnp.max(dmat, axis=-1))
        w_in = jnp.exp(dmat - m_t[..., None])
        w_st = jnp.exp(g - m_t)
        s = jnp.einsum('bhtd,bhsd->bhts', qb, kb) * w_in
        num = w_st[..., None] * jnp.einsum('bhtd,bhde->bhte', qb, C) + jnp.einsum('bhts,bhse->bhte', s, vb)
        den = w_st * jnp.einsum('bhtd,bhd->bht', qb, n) + jnp.sum(s, axis=-1)
        h = num / jnp.maximum(jnp.abs(den), jnp.exp(-m_t))[..., None]
        b_last = bcum[..., -1]
        lw_end = b_last[..., None] - bcum + li
        g_end = b_last + m
        m_new = jnp.maximum(g_end, jnp.max(lw_end, axis=-1))
        we = jnp.exp(lw_end - m_new[..., None])
        ge = jnp.exp(g_end - m_new)
        C_new = ge[..., None, None] * C + jnp.einsum('bhs,bhsd,bhse->bhde', we, kb, vb)
        n_new = ge[..., None] * n + jnp.einsum('bhs,bhsd->bhd', we, kb)
        return (C_new, n_new, m_new), h

    carry0 = (C0.astype(F32), n0.astype(F32), m0.astype(F32))
    (C, n, m), hs = lax.scan(step, carry0, (c4(q), c4(k), c4(v), c3(logi), c3(logf)))
    h = hs.transpose(1, 0, 3, 2, 4).reshape(B, T, H, Dh)
    return h, C, n, m


def _mlstm_branch(pm, C0, n0, m0, conv0, p):
    B, T, _ = pm.shape
    xm, v, o, ig, fg = jnp.split(pm, [M_WIDTH, 2 * M_WIDTH, 3 * M_WIDTH, 3 * M_WIDTH + M_HEADS], axis=-1)
    xp = jnp.concatenate([conv0.astype(xm.dtype), xm], axis=1)
    xc = p['m_conv_b'] + sum(xp[:, j:j + T] * p['m_conv_w'][j] for j in range(CONV_W))
    xc = jax.nn.silu(xc)
    conv_new = xp[:, xp.shape[1] - (CONV_W - 1):]
    xc_h = xc.reshape(B, T, M_HEADS, M_HEAD_DIM)
    q = jnp.einsum('bthd,hde->bthe', xc_h, p['m_wq']) * (M_HEAD_DIM ** -0.5)
    k = jnp.einsum('bthd,hde->bthe', xc_h, p['m_wk'])
    v_h = v.reshape(B, T, M_HEADS, M_HEAD_DIM)
    logi = (ig + p['m_i_b']).astype(F32)
    logf = jax.nn.log_sigmoid((fg + p['m_f_b']).astype(F32))
    chunk = MLSTM_CHUNK if T % MLSTM_CHUNK == 0 else T
    h, C, n, m = _mlstm_chunked(q, k, v_h, logi, logf, C0, n0, m0, chunk)
    h = (_head_norm(h, M_GN_EPS) * p['m_gn_w'].reshape(M_HEADS, M_HEAD_DIM)
         + p['m_skip'].reshape(M_HEADS, M_HEAD_DIM) * xc_h)
    y = (jax.nn.sigmoid(o) * h.reshape(B, T, M_WIDTH)).astype(pm.dtype)
    return y, C, n, m, conv_new


def _layer(x, st, p):
    shift0, S0, C0, n0, m0, conv0 = st
    dt = x.dtype
    xn = _rmsnorm(x, p['norm_mix_g'])
    proj = xn @ p['w_in']
    pr = proj[..., :R_SHIFT_COLS]
    pm = proj[..., R_SHIFT_COLS:R_SHIFT_COLS + M_COLS]
    pg = proj[..., R_SHIFT_COLS + M_COLS:] + p['gate_b']
    pr_first = shift0.astype(dt) @ p['w_in'][:, :R_SHIFT_COLS]
    pr_prev = jnp.concatenate([pr_first[:, None], pr[:, :-1]], axis=1)
    y_r, S_new = _rwkv7_branch(pr, pr_prev, S0, p)
    y_m, C_new, n_new, m_new, conv_new = _mlstm_branch(pm, C0, n0, m0, conv0, p)
    g_r, g_m = jnp.split(pg, 2, axis=-1)
    merged = jax.nn.sigmoid(g_r) * (y_r @ p['r_up']) + jax.nn.sigmoid(g_m) * (y_m @ p['m_up'])
    x = x + merged @ p['w_out']
    hn = _rmsnorm(x, p['norm_ffn_g'])
    x = x + jnp.square(jax.nn.relu(hn @ p['ffn_w1'])) @ p['ffn_w2']
    new = (xn[:, -1], S_new.astype(dt), C_new.astype(dt), n_new.astype(dt), m_new.astype(dt), conv_new.astype(dt))
    return x, new


def _trunk(x, states, layers, norm_final_g):
    acc = [[] for _ in states]
    for l in range(DEPTH):
        x, new = _layer(x, tuple(s[l] for s in states), layers[l])
        for a_list, s in zip(acc, new):
            a_list.append(s)
    y = _rmsnorm(x, norm_final_g)
    return y, [jnp.stack(a_list) for a_list in acc]


def setup_inputs(seed: int = 0) -> dict:
    key = jax.random.key(seed)
    ks = iter(jax.random.split(key, 64))
    nrm = lambda shape, scale: scale * jax.random.normal(next(ks), shape, F32)
    uni = lambda shape, lo, hi: jax.random.uniform(next(ks), shape, F32, lo, hi)
    L = DEPTH
    d = {}
    d['x_prompt'] = nrm((BATCH, SEQ, D_MODEL), 1.0)
    d['x_sample'] = nrm((DEC_BATCH, DEC_SEQ, D_MODEL), 1.0)
    d['state_rwkv_shift'] = nrm((L, DEC_BATCH, D_MODEL), 1.0)
    d['state_rwkv_wkv'] = nrm((L, DEC_BATCH, R_HEADS, R_HEAD_DIM, R_HEAD_DIM), 0.1)
    d['state_mlstm_C'] = nrm((L, DEC_BATCH, M_HEADS, M_HEAD_DIM, M_HEAD_DIM), 0.1)
    d['state_mlstm_n'] = nrm((L, DEC_BATCH, M_HEADS, M_HEAD_DIM), 0.1)
    d['state_mlstm_m'] = nrm((L, DEC_BATCH, M_HEADS), 1.0)
    d['state_mlstm_conv'] = nrm((L, DEC_BATCH, CONV_W - 1, M_WIDTH), 1.0)
    d['norm_mix_g'] = 1.0 + nrm((L, D_MODEL), 0.02)
    d['w_in'] = nrm((L, D_MODEL, IN_COLS), D_MODEL ** -0.5)
    d['r_mu'] = uni((L, R_SHIFT_COLS), 0.0, 1.0)
    d['r_w0'] = uni((L, R_WIDTH), -6.0, 2.0)
    d['r_w2'] = nrm((L, W_LORA, R_WIDTH), 0.1)
    d['r_a0'] = nrm((L, R_WIDTH), 0.1)
    d['r_a2'] = nrm((L, A_LORA, R_WIDTH), 0.1)
    d['r_g2'] = nrm((L, G_LORA, R_WIDTH), G_LORA ** -0.5)
    d['r_kk'] = 0.85 + nrm((L, R_WIDTH), 0.02)
    d['r_ka'] = 1.0 + nrm((L, R_WIDTH), 0.02)
    d['r_rk'] = nrm((L, R_HEADS, R_HEAD_DIM), 0.1)
    d['r_gn_w'] = 1.0 + nrm((L, R_WIDTH), 0.02)
    d['r_gn_b'] = nrm((L, R_WIDTH), 0.02)
    d['r_up'] = nrm((L, R_WIDTH, D_MODEL), R_WIDTH ** -0.5)
    d['m_conv_w'] = nrm((L, CONV_W, M_WIDTH), CONV_W ** -0.5)
    d['m_conv_b'] = nrm((L, M_WIDTH), 0.02)
    d['m_wq'] = nrm((L, M_HEADS, M_HEAD_DIM, M_HEAD_DIM), M_HEAD_DIM ** -0.5)
    d['m_wk'] = nrm((L, M_HEADS, M_HEAD_DIM, M_HEAD_DIM), M_HEAD_DIM ** -0.5)
    d['m_i_b'] = nrm((L, M_HEADS), 0.1)
    d['m_f_b'] = jnp.linspace(3.0, 6.0, M_HEADS, dtype=F32)[None, :] + nrm((L, M_HEADS), 0.1)
    d['m_gn_w'] = 1.0 + nrm((L, M_WIDTH), 0.02)
    d['m_skip'] = 1.0 + nrm((L, M_WIDTH), 0.02)
    d['m_up'] = nrm((L, M_WIDTH, D_MODEL), M_WIDTH ** -0.5)
    d['gate_b'] = nrm((L, GATE_COLS), 0.02)
    d['w_out'] = nrm((L, D_MODEL, D_MODEL), D_MODEL ** -0.5)
    d['norm_ffn_g'] = 1.0 + nrm((L, D_MODEL), 0.02)
    d['ffn_w1'] = nrm((L, D_MODEL, D_FF), D_MODEL ** -0.5)
    d['ffn_w2'] = nrm((L, D_FF, D_MODEL), D_FF ** -0.5)
    d['norm_final_g'] = 1.0 + nrm((D_MODEL,), 0.02)
    return d


def reference(x_prompt, x_sample, state_rwkv_shift, state_rwkv_wkv, state_mlstm_C, state_mlstm_n,
              state_mlstm_m, state_mlstm_conv, norm_mix_g, w_in, r_mu, r_w0, r_w2, r_a0, r_a2, r_g2,
              r_kk, r_ka, r_rk, r_gn_w, r_gn_b, r_up, m_conv_w, m_conv_b, m_wq, m_wk, m_i_b, m_f_b,
              m_gn_w, m_skip, m_up, gate_b, w_out, norm_ffn_g, ffn_w1, ffn_w2, norm_final_g):
    layers = [dict(norm_mix_g=norm_mix_g[l], w_in=w_in[l], r_mu=r_mu[l], r_w0=r_w0[l], r_w2=r_w2[l],
                   r_a0=r_a0[l], r_a2=r_a2[l], r_g2=r_g2[l], r_kk=r_kk[l], r_ka=r_ka[l], r_rk=r_rk[l],
                   r_gn_w=r_gn_w[l], r_gn_b=r_gn_b[l], r_up=r_up[l], m_conv_w=m_conv_w[l],
                   m_conv_b=m_conv_b[l], m_wq=m_wq[l], m_wk=m_wk[l], m_i_b=m_i_b[l], m_f_b=m_f_b[l],
                   m_gn_w=m_gn_w[l], m_skip=m_skip[l], m_up=m_up[l], gate_b=gate_b[l], w_out=w_out[l],
                   norm_ffn_g=norm_ffn_g[l], ffn_w1=ffn_w1[l], ffn_w2=ffn_w2[l])
              for l in range(DEPTH)]
    dt = x_prompt.dtype
    bp = x_prompt.shape[0]
    zero_states = (jnp.zeros((DEPTH, bp, D_MODEL), dt),
                   jnp.zeros((DEPTH, bp, R_HEADS, R_HEAD_DIM, R_HEAD_DIM), dt),
                   jnp.zeros((DEPTH, bp, M_HEADS, M_HEAD_DIM, M_HEAD_DIM), dt),
                   jnp.zeros((DEPTH, bp, M_HEADS, M_HEAD_DIM), dt),
                   jnp.zeros((DEPTH, bp, M_HEADS), dt),
                   jnp.zeros((DEPTH, bp, CONV_W - 1, M_WIDTH), dt))
    y_prompt, p_st = _trunk(x_prompt, zero_states, layers, norm_final_g)
    sample_states = (state_rwkv_shift, state_rwkv_wkv, state_mlstm_C, state_mlstm_n, state_mlstm_m, state_mlstm_conv)
    y_sample, s_st = _trunk(x_sample, sample_states, layers, norm_final_g)
    return (y_prompt, y_sample, p_st[0], p_st[1], p_st[2], p_st[3], p_st[4], p_st[5],
            s_st[0], s_st[1], s_st[2], s_st[3], s_st[4], s_st[5])
```

```python
import contextlib
import math
import numpy as np
import concourse.bass as bass
import concourse.mybir as mybir
from concourse.bass_utils import run_bass_kernel_spmd

F32 = mybir.dt.float32
BF16 = mybir.dt.bfloat16
I32 = mybir.dt.int32
AF = mybir.ActivationFunctionType
ALU = mybir.AluOpType
AX = mybir.AxisListType

NCORES = 8
T = 2048
TB = 512
NBLK = T // TB
D = 1024
L = 128
CC = math.exp(-0.5)
NDMA = 24
NSLOT = 4

PIECE_ORIG = [1536, 1664] + [base + j * 128 for j in range(4) for base in (0, 512, 1024)]
PERM = np.concatenate([np.arange(s, s + 128) for s in PIECE_ORIG] + [np.arange(1792, 5384)])
C_LAT = 0
C_R = 256
C_XM = 1792
C_MV = 2304
C_O = 2816
C_GT = 3328
C_GR = 3336
C_GM = 4360

P_MU, P_W0, P_A0, P_RKK, P_KA, P_RRK, P_GNW, P_GNB, P_CW, P_CB, P_MGN, P_MSK, P_GTB, P_GMIX, P_GFFN = \
    0, 14, 18, 22, 26, 30, 34, 38, 42, 58, 62, 66, 70, 86, 94
NPC = 102
Q_OMU, Q_HW0, Q_HA0, Q_OMKA, Q_HGTB = 0, 14, 18, 22, 26
NQC = 42


class Sched:
    def __init__(self, nc, stack):
        self.nc = nc
        self.eng = {'pe': nc.tensor, 'act': nc.scalar, 'dve': nc.vector, 'pool': nc.gpsimd, 'sp': nc.sync}
        self.sem = {e: stack.enter_context(nc.semaphore("s_" + e)) for e in self.eng}
        self.cnt = {e: 0 for e in self.eng}
        self.seen = {e: {} for e in self.eng}
        self.dsem = [stack.enter_context(nc.semaphore("s_dma%d" % i)) for i in range(NDMA)]
        self.dval = [0] * NDMA
        self.drr = {'sp': 0, 'pool': 0, 'act': 0}
        self.dpool = {'sp': list(range(0, 16)), 'act': list(range(0, 16)), 'pool': list(range(16, NDMA))}
        self.last_w = {}
        self.readers = {}
        self.nwaits = 0
        self.ninst = 0
        self.defer = True
        self.pend = []

    def _need(self, e, tok):
        semk, val, sem = tok
        if self.seen[e].get(semk, 0) >= val:
            return
        self.eng[e].wait_ge(sem, val)
        self.nwaits += 1
        self.seen[e][semk] = val

    def _deps(self, e, reads, writes):
        best = {}

        def add(t):
            if t is not None and (t[0] not in best or best[t[0]][1] < t[1]):
                best[t[0]] = t
        for k in reads:
            add(self.last_w.get(k))
        for k in writes:
            add(self.last_w.get(k))
            for t in self.readers.get(k, ()):
                add(t)
        for t in best.values():
            if e == 'pe' and t[0] == 'pe':
                continue
            self._need(e, t)

    def _record(self, tok, reads, writes):
        for k in reads:
            lst = self.readers.setdefault(k, [])
            for i, t in enumerate(lst):
                if t[0] == tok[0]:
                    lst[i] = tok
                    break
            else:
                lst.append(tok)
        for k in writes:
            self.last_w[k] = tok
            self.readers[k] = []

    def op(self, e, fn, reads=(), writes=(), cost=0.3):
        if self.defer:
            self.pend.append(('op', e, fn, tuple(reads), tuple(writes), cost))
            return None
        return self._op_now(e, fn, reads, writes)

    def dma(self, q, out, in_, reads=(), writes=(), cost=None):
        if cost is None:
            n_ = 1
            for d_ in out.shape:
                n_ *= d_
            cost = 2.0 + n_ * 4 / 250e3
        if self.defer:
            self.pend.append(('dma', q, (out, in_), tuple(reads), tuple(writes), cost))
            return None
        return self._dma_now(q, out, in_, reads, writes)

    def flush(self):
        import heapq
        ops = self.pend
        self.pend = []
        n = len(ops)
        if n == 0:
            return
        LAT = 0.25
        last_w, readers = {}, {}
        deps = [set() for _ in range(n)]
        for i, o in enumerate(ops):
            rd, wr = o[3], o[4]
            for k in rd:
                j = last_w.get(k)
                if j is not None:
                    deps[i].add(j)
            for k in wr:
                j = last_w.get(k)
                if j is not None:
                    deps[i].add(j)
                for r in readers.get(k, ()):
                    deps[i].add(r)
            for k in rd:
                readers.setdefault(k, []).append(i)
            for k in wr:
                last_w[k] = i
                readers[k] = []
            deps[i].discard(i)
        succ = [[] for _ in range(n)]
        indeg = [0] * n
        for i in range(n):
            indeg[i] = len(deps[i])
            for j in deps[i]:
                succ[j].append(i)
        blev = [0.0] * n
        for i in range(n - 1, -1, -1):
            b_ = 0.0
            for sc in succ[i]:
                lat_ = 0.0 if ops[sc][1] == ops[i][1] else LAT
                if blev[sc] + lat_ > b_:
                    b_ = blev[sc] + lat_
            blev[i] = ops[i][5] + b_
        engs = ('pe', 'act', 'dve', 'pool', 'sp')
        free_at = {e: 0.0 for e in engs}
        ready_t = [0.0] * n
        fin = [0.0] * n
        waiting = {e: [] for e in engs}
        avail = {e: [] for e in engs}
        for i in range(n):
            if indeg[i] == 0:
                heapq.heappush(waiting[ops[i][1]], (0.0, i))
        order = []
        done = 0
        while done < n:
            best = None
            for e in engs:
                w, a = waiting[e], avail[e]
                while w and w[0][0] <= free_at[e]:
                    rt, i = heapq.heappop(w)
                    heapq.heappush(a, (-blev[i], i))
                if a:
                    cand = (free_at[e], a[0][1], e, True)
                elif w:
                    cand = (w[0][0], w[0][1], e, False)
                else:
                    continue
                if best is None or cand[:2] < best[:2]:
                    best = cand
            st_, i, e, from_a = best
            if from_a:
                heapq.heappop(avail[e])
            else:
                heapq.heappop(waiting[e])
            start = max(free_at[e], ready_t[i])
            fin[i] = start + ops[i][5]
            free_at[e] = fin[i] if e != 'sp' and ops[i][0] != 'dma' else start + 0.1
            if ops[i][0] == 'dma':
                free_at[e] = start + (1.0 if e == 'pool' else 0.1)
            order.append(i)
            done += 1
            for sc in succ[i]:
                lat = 0.0 if (ops[sc][1] == e and ops[i][0] != 'dma') else LAT
                t = fin[i] + lat
                if t > ready_t[sc]:
                    ready_t[sc] = t
                indeg[sc] -= 1
                if indeg[sc] == 0:
                    heapq.heappush(waiting[ops[sc][1]], (ready_t[sc], sc))
        for i in order:
            o = ops[i]
            if o[0] == 'op':
                self._op_now(o[1], o[2], o[3], o[4])
            else:
                self._dma_now(o[1], o[2][0], o[2][1], o[3], o[4])

    def _op_now(self, e, fn, reads=(), writes=()):
        self._deps(e, reads, writes)
        inst = fn(self.eng[e])
        self.cnt[e] += 1
        inst.then_inc(self.sem[e], 1)
        tok = (e, self.cnt[e], self.sem[e])
        self._record(tok, reads, writes)
        self.ninst += 1
        return tok

    def _dma_now(self, q, out, in_, reads=(), writes=()):
        self._deps(q, reads, writes)
        pl = self.dpool[q]
        i = pl[self.drr[q] % len(pl)]
        self.drr[q] += 1
        semk = "d%d" % i
        if self.dval[i] > 0:
            self._need(q, (semk, self.dval[i], self.dsem[i]))
        inst = self.eng[q].dma_start(out=out, in_=in_)
        self.dval[i] += 16
        inst.then_inc(self.dsem[i], 16)
        tok = (semk, self.dval[i], self.dsem[i])
        self._record(tok, reads, writes)
        self.ninst += 1
        return tok

    def barrier(self):
        self.flush()
        for e in self.eng:
            for o in self.eng:
                if o != e and self.cnt[o] > 0:
                    self._need(e, (o, self.cnt[o], self.sem[o]))
            for i in range(NDMA):
                if self.dval[i] > 0:
                    self._need(e, ("d%d" % i, self.dval[i], self.dsem[i]))

    def finish(self, e='sp'):
        self.flush()
        for o in self.eng:
            if o != e and self.cnt[o] > 0:
                self._need(e, (o, self.cnt[o], self.sem[o]))
        for i in range(NDMA):
            if self.dval[i] > 0:
                self._need(e, ("d%d" % i, self.dval[i], self.dsem[i]))


class _Stop(Exception):
    pass


def build_program(debug=False, nblk=NBLK, stop=None, with_sample=True):
    nc = bass.Bass("TRN2", target_bir_lowering=False)
    dram_in = lambda name, shape: nc.dram_tensor(name, list(shape), F32, kind="ExternalInput").ap()
    dram_out = lambda name, shape: nc.dram_tensor(name, list(shape), F32, kind="ExternalOutput").ap()
    x_d = dram_in("x", [T, D])
    win_d = dram_in("w_in_l", [128, 8, 5384])
    rup_d = dram_in("r_up_l", [128, 4, 1024])
    mup_d = dram_in("m_up_l", [128, 4, 1024])
    wout_d = dram_in("w_out_l", [128, 8, 1024])
    w1_d = dram_in("w1_l", [128, 8, 4096])
    w2_d = dram_in("w2_l", [128, 32, 1024])
    w2a2_d = dram_in("w2a2", [128, 512])
    g2_d = dram_in("g2", [128, 512])
    wq_d = dram_in("wq_l", [128, 4, 128])
    wk_d = dram_in("wk_l", [128, 4, 128])
    pp_d = dram_in("ppack", [128, NPC])
    gbt_d = dram_in("gbias_t", [1, 8])
    gbf_d = dram_in("gbias_f", [4, 2])
    gfin_d = dram_in("g_final", [1, D])
    gmixrow_d = dram_in("g_mix_row", [1, D])

    y_d = dram_out("y", [T, D])
    shift_d = dram_out("p_shift", [1, D])
    wkv_d = dram_out("p_wkv", [8, 64, 64])
    C_d = dram_out("p_C", [4, 128, 128])
    n_d = dram_out("p_n", [4, 128])
    m_d = dram_out("p_m", [4, 1])
    conv_d = dram_out("p_conv", [3, 512])
    xs_d = dram_in("xs", [16, D])
    sh0_d = dram_in("sh0", [16, D])
    wkv0_d = dram_in("s_wkv0", [128, 4096])
    C0_d = dram_in("s_C0", [64, 128, 128])
    n0_d = dram_in("s_n0", [64, 128])
    m0_d = dram_in("s_m0", [16, 4])
    conv0_d = dram_in("s_conv0", [16, 3, 512])
    prow1_d = dram_in("prow1", [1, 5376])
    prow2_d = dram_in("prow2", [1, 3584])
    prow3_d = dram_in("prow3", [1, 4096])
    ys_d = dram_out("ys", [16, D])
    s_shift_d = dram_out("s_shift", [16, D])
    s_wkv_d = dram_out("s_wkv", [128, 4096])
    s_C_d = dram_out("s_C", [64, 128, 128])
    s_n_d = dram_out("s_n", [64, 128])
    s_m_d = dram_out("s_m", [16, 4])
    s_conv_d = dram_out("s_conv", [16, 3, 512])
    scr1 = nc.dram_tensor("scr1", [6, 16 * 512], F32).ap()
    scr2 = nc.dram_tensor("scr2", [128, 64], F32).ap()
    scrq = nc.dram_tensor("scrq", [64, 128], F32).ap()
    scrk = nc.dram_tensor("scrk", [64, 128], F32).ap()
    scrv = nc.dram_tensor("scrv", [64, 128], F32).ap()
    scrs = nc.dram_tensor("scrs", [64, 4], F32).ap()
    scrh = nc.dram_tensor("scrh", [64, 128], F32).ap()
    dbg_outs = {}

    with contextlib.ExitStack() as st:
        st.enter_context(nc.allow_non_contiguous_dma(reason="small strided state outputs"))
        S = Sched(nc, st)

        def SB(stack, name, shape, dt=F32):
            return stack.enter_context(nc.sbuf_tensor(name, list(shape), dt))

        def fsz(ap):
            n_ = 1
            for d_ in ap.shape[1:]:
                n_ *= d_
            return n_

        def tt(e, out, a, b, op, r, w):
            return S.op(e, lambda en: en.tensor_tensor(out=out, in0=a, in1=b, op=op), reads=r, writes=w, cost=0.12 + 0.00105 * fsz(out))

        def ts(e, out, a, s1, s2, op0, op1, r, w):
            if s2 is None:
                return S.op(e, lambda en: en.tensor_scalar(out=out, in0=a, scalar1=s1, scalar2=None, op0=op0), reads=r, writes=w, cost=0.12 + 0.00105 * fsz(out))
            return S.op(e, lambda en: en.tensor_scalar(out=out, in0=a, scalar1=s1, scalar2=s2, op0=op0, op1=op1), reads=r, writes=w, cost=0.12 + 0.00105 * fsz(out))

        def stt(e, out, a, s, b, op0, op1, r, w):
            return S.op(e, lambda en: en.scalar_tensor_tensor(out=out, in0=a, scalar=s, in1=b, op0=op0, op1=op1), reads=r, writes=w, cost=0.12 + 0.0022 * fsz(out))

        def act(out, in_, func, r, w, bias=None, scale=1.0, accum=None):
            def f(en):
                kw = {}
                if bias is not None:
                    kw['bias'] = bias
                if accum is not None:
                    kw['accum_out'] = accum
                return en.activation(out=out, in_=in_, func=func, scale=scale, **kw)
            return S.op('act', f, reads=r, writes=w, cost=0.22 + 0.00072 * fsz(out))

        def cp(e, out, in_, r, w):
            if e == 'act':
                return act(out, in_, AF.Copy, r, w)
            return S.op(e, lambda en: en.tensor_copy(out=out, in_=in_), reads=r, writes=w, cost=0.12 + 0.00105 * fsz(out))

        def fix05(out, in_, r, w, np_=128):
            return act(out, in_, AF.Identity, r + ['half_c'], w, bias=half_c[0:np_, 0:1], scale=0.5)

        def mm(out, lhsT, rhs, start, stop, r, w):
            return S.op('pe', lambda en: en.matmul(out, lhsT, rhs, start=start, stop=stop), reads=r, writes=w, cost=0.04 + 0.0005 * fsz(out))

        def tr(out, in_, ident, r, w):
            return S.op('pe', lambda en: en.transpose(out, in_, ident), reads=r, writes=w, cost=0.04 + 0.0005 * fsz(out))

        def dbg(name, ap, shape, keys):
            if not debug:
                return
            d = dram_out("dbg_" + name, shape)
            dbg_outs[name] = d
            S.dma('pool', d, ap, reads=keys)

        def bc_mid(ap2d, n):
            P_, X_ = ap2d.shape
            return ap2d.unsqueeze(1).to_broadcast([P_, n, X_])

        def bc_in(ap2d, n):
            P_, A_ = ap2d.shape
            return ap2d.unsqueeze(2).to_broadcast([P_, A_, n])

        ident_bf = SB(st, "ident_bf", [128, 128], BF16)
        ident_f = SB(st, "ident_f", [128, 128])
        bones = SB(st, "bones", [128, 128], BF16)
        su_f = SB(st, "su_f", [128, 128])
        ui_f = SB(st, "ui_f", [128, 128])
        sl_f = SB(st, "sl_f", [128, 128])
        ones_f = SB(st, "ones_f", [128, 128])
        zeros_f = SB(st, "zeros_f", [128, 128])
        pp = SB(st, "pp", [128, NPC])
        pq = SB(st, "pq", [128, NQC])
        w2a2_bf = SB(st, "w2a2_bf", [128, 512], BF16)
        g2_bf = SB(st, "g2_bf", [128, 512], BF16)
        wq_bf = SB(st, "wq_bf", [128, 4, 128], BF16)
        wk_bf = SB(st, "wk_bf", [128, 4, 128], BF16)
        gfin_bc = SB(st, "gfin_bc", [128, D])
        gbt = SB(st, "gbt", [128, 8])
        gbf = SB(st, "gbf", [4, 2])
        hfb = SB(st, "hfb", [4, 1])
        half_c = SB(st, "half_c", [128, 1])
        eps_c = SB(st, "eps_c", [128, 1])
        carry = SB(st, "carry", [128, 14])
        Hst = SB(st, "Hst", [128, 256])
        Hbf = [SB(st, "Hbf%d" % i, [128, 256], BF16) for i in range(2)]
        Caug = SB(st, "Caug", [128, 4, 129])
        Caug_bf = SB(st, "Caug_bf", [128, 4, 129], BF16)
        Mst = SB(st, "Mst", [4, 1])
        xmc = SB(st, "xmc", [128, 4, 3])
        wbuf = [SB(st, "wbuf%d" % i, [128, 4096], BF16) for i in range(NSLOT)]
        rs_in = SB(st, "rs_in", [128, 8])
        rs_out = SB(st, "rs_out", [128, 8])
        rs_t = SB(st, "rs_t", [128, 8])
        ps = [st.enter_context(nc.psum_tensor("ps%d" % i, [128, 512], F32)) for i in range(8)]
        psk = ["ps%d" % i for i in range(8)]
        bank_rr = [0]

        reserved = set()

        def nb():
            for _ in range(8):
                i = bank_rr[0]
                bank_rr[0] = (i + 1) % 8
                if i not in reserved:
                    return i
            raise RuntimeError("no psum bank")

        rsB_in = SB(st, "rsB_in", [128, 8])
        rsB_out = SB(st, "rsB_out", [128, 8])
        rsB_t = SB(st, "rsB_t", [128, 8])

        def rsqrt(n, rk, bgset=False):
            if bgset:
                r_in, r_out, r_t, k_in, k_out, k_t = rsB_in, rsB_out, rsB_t, 'rsB_in', 'rsB_out', 'rsB_t'
            else:
                r_in, r_out, r_t, k_in, k_out, k_t = rs_in, rs_out, rs_t, 'rs_in', 'rs_out', 'rs_t'
            xi = r_in[:, 0:n].bitcast(I32)
            ti = r_t[:, 0:n].bitcast(I32)
            oi = r_out[:, 0:n].bitcast(I32)
            S.op('dve', lambda en: en.tensor_single_scalar(out=ti, in_=xi, scalar=1, op=ALU.arith_shift_right), reads=[k_in] + rk, writes=[k_t])
            S.op('dve', lambda en: en.tensor_scalar(out=oi, in0=ti, scalar1=-1, scalar2=0x5f3759df, op0=ALU.mult, op1=ALU.add), reads=[k_t], writes=[k_out])
            for _ in range(2):
                tt('dve', r_t[:, 0:n], r_out[:, 0:n], r_out[:, 0:n], ALU.mult, [k_out], [k_t])
                tt('dve', r_t[:, 0:n], r_t[:, 0:n], r_in[:, 0:n], ALU.mult, [k_t, k_in], [k_t])
                ts('dve', r_t[:, 0:n], r_t[:, 0:n], -0.5, 1.5, ALU.mult, ALU.add, [k_t], [k_t])
                tt('dve', r_out[:, 0:n], r_out[:, 0:n], r_t[:, 0:n], ALU.mult, [k_out, k_t], [k_out])

        S.op('pool', lambda en: en.memset(ident_f[:], 0.0), writes=['ident_f'])
        S.op('pool', lambda en: en.affine_select(out=ident_f[:], in_=ident_f[:], pattern=[[-1, 128]], compare_op=ALU.not_equal, fill=1.0, base=0, channel_multiplier=1), reads=['ident_f'], writes=['ident_f'])
        cp('pool', ident_bf[:], ident_f[:], ['ident_f'], ['ident_bf'])
        for (tl, op, nm, sg) in ((su_f, ALU.is_gt, 'su_f', -1), (ui_f, ALU.is_ge, 'ui_f', -1), (sl_f, ALU.is_gt, 'sl_f', 1)):
            S.op('pool', lambda en, tl=tl: en.memset(tl[:], 1.0), writes=[nm])
            S.op('pool', lambda en, tl=tl, sg=sg, op=op: en.affine_select(out=tl[:], in_=tl[:], pattern=[[-sg, 128]], compare_op=op, fill=0.0, base=0, channel_multiplier=sg), reads=[nm], writes=[nm])
        S.op('pool', lambda en: en.memset(ones_f[:], 1.0), writes=['ones_f'])
        S.op('pool', lambda en: en.memset(half_c[:], 0.5), writes=['half_c'])
        S.op('pool', lambda en: en.memset(eps_c[:], 1e-24), writes=['eps_c'])
        S.op('pool', lambda en: en.memset(rs_in[:], 1.0), writes=['rs_in'])
        S.op('pool', lambda en: en.memset(rs_out[:], 1.0), writes=['rs_out'])
        S.op('pool', lambda en: en.memset(rs_t[:], 1.0), writes=['rs_t'])
        S.op('pool', lambda en: en.memset(rsB_in[:], 1.0), writes=['rsB_in'])
        S.op('pool', lambda en: en.memset(rsB_out[:], 1.0), writes=['rsB_out'])
        S.op('pool', lambda en: en.memset(rsB_t[:], 1.0), writes=['rsB_t'])
        S.op('pool', lambda en: en.memset(zeros_f[:], 0.0), writes=['zeros_f'])
        S.op('pool', lambda en: en.memset(bones[:], 0.0), writes=['bones'])
        S.op('pool', lambda en: en.memset(bones[0:64, 0:64], 1.0), reads=['bones'], writes=['bones'])
        S.op('pool', lambda en: en.memset(bones[64:128, 64:128], 1.0), reads=['bones'], writes=['bones'])
        S.op('pool', lambda en: en.memset(carry[:], 0.0), writes=['carry'])
        S.op('pool', lambda en: en.memset(Hst[:], 0.0), writes=['Hst'])
        S.op('pool', lambda en: en.memset(Hbf[0][:], 0.0), writes=['Hbf0'])
        S.op('pool', lambda en: en.memset(Caug[:], 0.0), writes=['Caug'])
        S.op('pool', lambda en: en.memset(Caug_bf[:], 0.0), writes=['Caug_bf'])
        S.op('pool', lambda en: en.memset(Mst[:], 1.0), writes=['Mst'])
        S.op('pool', lambda en: en.memset(xmc[:], 0.0), writes=['xmc'])
        S.dma('sp', pp[:], pp_d[:, :], writes=['pp'])
        S.dma('sp', gfin_bc[:], gfin_d[0:1, :].partition_broadcast(128), writes=['gfin_bc'])
        S.dma('sp', gbt[:], gbt_d[0:1, :].partition_broadcast(128), writes=['gbt'])
        S.dma('sp', gbf[:], gbf_d[:, :], writes=['gbf'])
        S.dma('pool', w2a2_bf[:], w2a2_d[:, :], writes=['w2a2_bf'])
        S.dma('pool', g2_bf[:], g2_d[:, :], writes=['g2_bf'])
        S.dma('pool', wq_bf[:], wq_d[:, :, :], writes=['wq_bf'])
        S.dma('pool', wk_bf[:], wk_d[:, :, :], writes=['wk_bf'])
        ts('dve', pq[:, Q_OMU:Q_OMU + 14], pp[:, P_MU:P_MU + 14], -1.0, 1.0, ALU.mult, ALU.add, ['pp'], ['pq'])
        ts('dve', pq[:, Q_HW0:Q_HW0 + 4], pp[:, P_W0:P_W0 + 4], 0.5, None, ALU.mult, None, ['pp'], ['pq'])
        ts('dve', pq[:, Q_HA0:Q_HA0 + 4], pp[:, P_A0:P_A0 + 4], 0.5, None, ALU.mult, None, ['pp'], ['pq'])
        ts('dve', pq[:, Q_OMKA:Q_OMKA + 4], pp[:, P_KA:P_KA + 4], -1.0, 1.0, ALU.mult, ALU.add, ['pp'], ['pq'])
        ts('dve', pq[:, Q_HGTB:Q_HGTB + 16], pp[:, P_GTB:P_GTB + 16], 0.5, None, ALU.mult, None, ['pp'], ['pq'])
        ts('dve', hfb[:], gbf[:, 1:2], 0.5, None, ALU.mult, None, ['gbf'], ['hfb'])
        PPK = ['pp', 'pq']

        wseq = []
        if with_sample:
            for g in range(11):
                c0, c1 = g * 512, min((g + 1) * 512, 5384)
                wseq.append(('s_in%d' % g, win_d[:, :, c0:c1], 8, c1 - c0))
            for hh in range(2):
                wseq.append(('s_ru%d' % hh, rup_d[:, :, hh * 512:(hh + 1) * 512], 4, 512))
            for hh in range(2):
                wseq.append(('s_mu%d' % hh, mup_d[:, :, hh * 512:(hh + 1) * 512], 4, 512))
            for hh in range(2):
                wseq.append(('s_wo%d' % hh, wout_d[:, :, hh * 512:(hh + 1) * 512], 8, 512))
        for b in range(nblk):
            wseq.append(('lat', win_d[:, :, C_LAT:C_LAT + 256], 8, 256))
            for j in range(4):
                wseq.append(('r%d' % j, win_d[:, :, C_R + j * 384:C_R + (j + 1) * 384], 8, 384))
            wseq.append(('xm', win_d[:, :, C_XM:C_XM + 512], 8, 512))
            wseq.append(('gt', win_d[:, :, C_GT:C_GT + 8], 8, 8))
            wseq.append(('mv', win_d[:, :, C_MV:C_MV + 512], 8, 512))
            wseq.append(('o', win_d[:, :, C_O:C_O + 512], 8, 512))
            for hh in range(2):
                wseq.append(('gr%d' % hh, win_d[:, :, C_GR + hh * 512:C_GR + (hh + 1) * 512], 8, 512))
                wseq.append(('gm%d' % hh, win_d[:, :, C_GM + hh * 512:C_GM + (hh + 1) * 512], 8, 512))
                wseq.append(('ru%d' % hh, rup_d[:, :, hh * 512:(hh + 1) * 512], 4, 512))
                wseq.append(('mu%d' % hh, mup_d[:, :, hh * 512:(hh + 1) * 512], 4, 512))
            for hh in range(2):
                wseq.append(('wo%d' % hh, wout_d[:, :, hh * 512:(hh + 1) * 512], 8, 512))
            for g in range(8):
                wseq.append(('w1_%d' % g, w1_d[:, :, g * 512:(g + 1) * 512], 8, 512))
            for hh in range(2):
                for kg in range(4):
                    wseq.append(('w2_%d_%d' % (hh, kg), w2_d[:, kg * 8:(kg + 1) * 8, hh * 512:(hh + 1) * 512], 8, 512))
        wstate = {'issued': 0, 'used': 0, 'done': 0}

        def w_pump():
            while wstate['issued'] < min(wstate['done'] + NSLOT, len(wseq)):
                i = wstate['issued']
                name, src, kc, ncol = wseq[i]
                slot = i % NSLOT
                dst = wbuf[slot][:, 0:kc * ncol].rearrange("p (k c) -> p k c", k=kc)
                S.dma('pool', dst, src, writes=[('wbuf', slot)])
                wstate['issued'] = i + 1

        def w_get(name):
            i = wstate['used']
            nm, src, kc, ncol = wseq[i]
            assert nm == name, (nm, name)
            w_pump()
            assert wstate['issued'] > i
            wstate['used'] = i + 1
            slot = i % NSLOT
            return wbuf[slot][:, 0:kc * ncol].rearrange("p (k c) -> p k c", k=kc), ('wbuf', slot)

        def w_done(n=1):
            wstate['done'] += n
            assert wstate['done'] <= wstate['used']
            w_pump()

        w_pump()

        def chk(name):
            if stop == name:
                raise _Stop()

        def sample_phase():
            with contextlib.ExitStack() as ss_:
                P_s = SB(ss_, "P_s", [16, 5384])
                big0 = SB(ss_, "big0", [128, 8192])
                xs_t = SB(ss_, "xs_t", [16, D])
                x1_s = x1_sP
                gmix_bc = SB(ss_, "gmix_bc", [16, D])
                junk_s = SB(ss_, "junk_s", [16, D], BF16)
                xnb = SB(ss_, "xnb", [16, D], BF16)
                shb = SB(ss_, "shb", [16, D], BF16)
                xnT_s = SB(ss_, "xnT_s", [128, 8, 16], BF16)
                shT_s = SB(ss_, "shT_s", [128, 8, 16], BF16)
                ss_s = SB(ss_, "ss_s", [16, 1])
                yrT_s = SB(ss_, "yrT_s", [128, 4, 16], BF16)
                ymT_s = SB(ss_, "ymT_s", [128, 4, 16], BF16)
                PK = [('P_s', g) for g in range(11)] + [('P_s', 3, 'b')]

                def rms16(src, srck, gain, gaink, outp, outk):
                    S.op('pool', lambda en: en.memset(ss_s[:], 0.0), reads=['rs_in'], writes=['ss_s'])
                    act(junk_s[:], src, AF.Square, srck, ['junk_s', 'ss_s'], accum=ss_s[:, 0:1])
                    ts('dve', rs_in[0:16, 0:1], ss_s[:], 1.0 / D, 1e-6, ALU.mult, ALU.add, ['ss_s', 'rs_out'], ['rs_in'])
                    rsqrt(1, [])
                    stt('dve', outp, src, rs_out[0:16, 0:1], gain, ALU.mult, ALU.mult, srck + ['rs_out'] + gaink, outk)

                def tr16(src_bf, srck, dst, dstk, nch):
                    bi = nb()
                    psb = ps[bi][:].bitcast(BF16)
                    for kc in range(nch):
                        tr(psb[:, kc * 16:(kc + 1) * 16], src_bf[0:16, kc * 128:(kc + 1) * 128], ident_bf[0:16, 0:16], srck + ['ident_bf'], [psk[bi]])
                    cp('dve', dst[:, 0:nch, :], psb[:, 0:nch * 16].rearrange("p (k t) -> p k t", k=nch), [psk[bi]], [dstk])

                def red(out, in3, r, w):
                    return S.op('dve', lambda en: en.tensor_reduce(out=out, in_=in3, axis=AX.X, op=ALU.add), reads=r, writes=w)

                Sb = big0[:, 0:4096]
                S.dma('sp', Sb, wkv0_d[:, :], writes=['Sb'])
                S.dma('sp', xs_t[:], xs_d[:, :], writes=['xs_t'])
                S.dma('pool', shb[:], sh0_d[:, :], writes=['shb'])
                S.dma('sp', gmix_bc[:], gmixrow_d[0:1, :].partition_broadcast(16), writes=['gmix_bc'])
                rms16(xs_t[:], ['xs_t'], gmix_bc[:], ['gmix_bc'], x1_s[:], ['x1_s'])
                S.dma('sp', s_shift_d[:, :], x1_s[:], reads=['x1_s'], writes=['o_sshift'])
                cp('act', xnb[:], x1_s[:], ['x1_s'], ['xnb'])
                tr16(xnb, ['xnb'], xnT_s, 'xnT_s', 8)
                tr16(shb, ['shb'], shT_s, 'shT_s', 8)

                with contextlib.ExitStack() as s1:
                    pr1 = SB(s1, "pr1", [16, 5376])
                    T1t = SB(s1, "T1t", [128, 4096])
                    T1 = T1t[:]
                    S.dma('sp', pr1[:], prow1_d[0:1, :].partition_broadcast(16), writes=['pr1'])
                    pf_sb = SB(s1, "pf_sb", [16, 512])
                    dtmp = SB(s1, "dtmp", [16, 512])
                    for g in range(11):
                        wg, wgk = w_get('s_in%d' % g)
                        c0, c1 = g * 512, min((g + 1) * 512, 5384)
                        ncl = c1 - c0
                        bi = nb()
                        for kc in range(8):
                            mm(ps[bi][0:16, 0:ncl], xnT_s[:, kc, :], wg[:, kc, :], kc == 0, kc == 7, [wgk, 'xnT_s'], [psk[bi]])
                        if c0 < 1792:
                            ncm = min(c1, 1792) - c0
                            b2 = nb()
                            for kc in range(8):
                                mm(ps[b2][0:16, 0:ncm], shT_s[:, kc, :], wg[:, kc, 0:ncm], kc == 0, kc == 7, [wgk, 'shT_s'], [psk[b2]])
                            cp('act', pf_sb[:, 0:ncm], ps[b2][0:16, 0:ncm], [psk[b2]], ['pf_sb'])
                            tt('dve', dtmp[:, 0:ncm], pf_sb[:, 0:ncm], ps[bi][0:16, 0:ncm], ALU.subtract, ['pf_sb', psk[bi]], ['dtmp'])
                            tt('dve', dtmp[:, 0:ncm], dtmp[:, 0:ncm], pr1[:, c0:c0 + ncm], ALU.mult, ['dtmp', 'pr1'], ['dtmp'])
                            tt('dve', P_s[:, c0:c0 + ncm], dtmp[:, 0:ncm], ps[bi][0:16, 0:ncm], ALU.add, ['dtmp', psk[bi]], [('P_s', g)])
                            if ncm < ncl:
                                cp('dve', P_s[:, c0 + ncm:c1], ps[bi][0:16, ncm:ncl], [psk[bi]], [('P_s', g, 'b')])
                        else:
                            cp('act' if g % 2 else 'dve', P_s[:, c0:c1], ps[bi][0:16, 0:ncl], [psk[bi]], [('P_s', g)])
                        w_done()
                    R_W0, R_A0, R_RKK, R_KA, R_RRK, R_GNW, R_GNB = 1792, 2304, 2816, 3328, 3840, 4352, 4864
                    pw = lambda off: pr1[:, off:off + 512]
                    rkv = P_s[:, 256:1792].rearrange("p (j w c) -> p j w c", j=4, w=3)
                    V6 = SB(s1, "V6", [16, 6, 512])
                    k_t = SB(s1, "k_t", [16, 512])
                    a_t = SB(s1, "a_t", [16, 512])
                    kr = SB(s1, "kr", [16, 512])
                    g_t = SB(s1, "g_t", [16, 512])
                    y_t = SB(s1, "y_t", [16, 512])
                    tS1 = SB(s1, "tS1", [16, 512])
                    tS2 = SB(s1, "tS2", [16, 512])
                    yr_bf = SB(s1, "yr_bf", [16, 512], BF16)
                    latb = SB(s1, "latb", [16, 256], BF16)
                    latT_s = SB(s1, "latT_s", [128, 2, 16], BF16)
                    ssk = SB(s1, "ssk", [16, 8])
                    st1_s = SB(s1, "st1_s", [16, 8])
                    st2_s = SB(s1, "st2_s", [16, 8])
                    bon = SB(s1, "bon", [16, 8])
                    pv = SB(s1, "pv", [128, 6, 64])
                    skk = SB(s1, "skk", [128, 64])
                    yp = SB(s1, "yp", [128, 64])
                    v4 = lambda ap: ap.rearrange("p (j c) -> p j c", j=4)
                    v8 = lambda ap: ap.rearrange("p (h c) -> p h c", h=8)
                    r_t, v_t = V6[:, 4, :], V6[:, 5, :]
                    cp('dve', v4(r_t), rkv[:, :, 0, :], PK, [('V6', 4)])
                    cp('dve', v4(k_t[:]), rkv[:, :, 1, :], PK, ['k_t'])
                    cp('dve', v4(v_t), rkv[:, :, 2, :], PK, [('V6', 5)])
                    act(latb[:, 0:64], P_s[:, 0:64], AF.Tanh, PK, [('latb', 0)])
                    cp('dve', latb[:, 64:128], P_s[:, 64:128], PK, [('latb', 1)])
                    act(tS1[:, 0:128], P_s[:, 128:256], AF.Tanh, PK, ['tS1'], scale=0.5)
                    ts('dve', latb[:, 128:256], tS1[:, 0:128], 0.5, 0.5, ALU.mult, ALU.add, ['tS1'], [('latb', 2)])
                    tr16(latb, [('latb', i) for i in range(3)], latT_s, 'latT_s', 2)
                    bw, ba, bg = nb(), nb(), nb()
                    mm(ps[bw][0:16, :], latT_s[0:64, 0, :], w2a2_bf[0:64, :], True, True, ['latT_s', 'w2a2_bf'], [psk[bw]])
                    mm(ps[ba][0:16, :], latT_s[64:128, 0, :], w2a2_bf[64:128, :], True, True, ['latT_s', 'w2a2_bf'], [psk[ba]])
                    mm(ps[bg][0:16, :], latT_s[:, 1, :], g2_bf[:, :], True, True, ['latT_s', 'g2_bf'], [psk[bg]])
                    cp('act', g_t[:], ps[bg][0:16, :], [psk[bg]], ['g_t'])
                    tt('dve', tS1[:], ps[bw][0:16, :], pw(R_W0), ALU.add, [psk[bw], 'pr1'], ['tS1'])
                    act(tS1[:], tS1[:], AF.Tanh, ['tS1'], ['tS1'], scale=0.5)
                    ts('dve', tS1[:], tS1[:], 0.5, 0.5, ALU.mult, ALU.add, ['tS1'], ['tS1'])
                    act(V6[:, 1, :], tS1[:], AF.Exp, ['tS1'], [('V6', 1)], scale=-CC)
                    tt('dve', tS2[:], ps[ba][0:16, :], pw(R_A0), ALU.add, [psk[ba], 'pr1'], ['tS2'])
                    act(tS2[:], tS2[:], AF.Tanh, ['tS2'], ['tS2'], scale=0.5)
                    ts('dve', a_t[:], tS2[:], 0.5, 0.5, ALU.mult, ALU.add, ['tS2'], ['a_t'])
                    tt('dve', kr[:], k_t[:], pw(R_RKK), ALU.mult, ['k_t', 'pr1'], ['kr'])
                    tt('dve', tS1[:], kr[:], kr[:], ALU.mult, ['kr'], ['tS1'])
                    red(ssk[:], v8(tS1[:]), ['tS1'], ['ssk'])
                    ts('dve', rs_in[0:16, 0:8], ssk[:], 1e-24, None, ALU.add, None, ['ssk', 'rs_out'], ['rs_in'])
                    rsqrt(8, [])
                    tt('dve', v8(V6[:, 0, :]), v8(kr[:]), bc_in(rs_out[0:16, 0:8], 64), ALU.mult, ['kr', 'rs_out'], [('V6', 0)])
                    tt('dve', V6[:, 2, :], a_t[:], V6[:, 0, :], ALU.mult, ['a_t', ('V6', 0)], [('V6', 2)])
                    tt('dve', tS1[:], a_t[:], pw(R_KA), ALU.mult, ['a_t', 'pr1'], ['tS1'])
                    ts('dve', tS2[:], pw(R_KA), -1.0, 1.0, ALU.mult, ALU.add, ['pr1'], ['tS2'])
                    tt('dve', tS1[:], tS1[:], tS2[:], ALU.add, ['tS1', 'tS2'], ['tS1'])
                    tt('dve', V6[:, 3, :], k_t[:], tS1[:], ALU.mult, ['k_t', 'tS1'], [('V6', 3)])
                    V6K = [('V6', i) for i in range(6)]
                    S.dma('sp', scr1.rearrange("q (b f) -> b q f", b=16), V6[:], reads=V6K, writes=['scr1'])
                    S.dma('sp', pv[:], scr1.rearrange("q (p c) -> p q c", c=64), reads=['scr1'], writes=['pv'])
                    S3 = Sb.rearrange("p (v k) -> p v k", v=64)
                    T3 = T1.rearrange("p (v k) -> p v k", v=64)
                    tt('dve', T3, S3, bc_mid(pv[:, 0, :], 64), ALU.mult, ['Sb', 'pv'], ['T1'])
                    red(skk[:], T3, ['T1'], ['skk'])
                    tt('dve', S3, S3, bc_mid(pv[:, 1, :], 64), ALU.mult, ['Sb', 'pv'], ['Sb'])
                    tt('dve', T3, bc_in(skk[:], 64), bc_mid(pv[:, 2, :], 64), ALU.mult, ['skk', 'pv', 'T1'], ['T1'])
                    tt('dve', S3, S3, T3, ALU.subtract, ['Sb', 'T1'], ['Sb'])
                    tt('dve', T3, bc_in(pv[:, 5, :], 64), bc_mid(pv[:, 3, :], 64), ALU.mult, ['pv', 'T1'], ['T1'])
                    tt('dve', S3, S3, T3, ALU.add, ['Sb', 'T1'], ['Sb'])
                    S.dma('sp', s_wkv_d[:, :], Sb, reads=['Sb'], writes=['o_swkv'])
                    tt('dve', T3, S3, bc_mid(pv[:, 4, :], 64), ALU.mult, ['Sb', 'pv', 'T1'], ['T1'])
                    red(yp[:], T3, ['T1'], ['yp'])
                    S.dma('sp', scr2[:, :], yp[:], reads=['yp'], writes=['scr2'])
                    S.dma('sp', y_t[:], scr2.rearrange("(b h) c -> b (h c)", b=16), reads=['scr2'], writes=['y_t'])
                    red(st1_s[:], v8(y_t[:]), ['y_t'], ['st1_s'])
                    tt('dve', tS1[:], y_t[:], y_t[:], ALU.mult, ['y_t'], ['tS1'])
                    red(st2_s[:], v8(tS1[:]), ['tS1'], ['st2_s'])
                    ts('dve', st1_s[:], st1_s[:], 1.0 / 64, None, ALU.mult, None, ['st1_s'], ['st1_s'])
                    tt('dve', rs_in[0:16, 0:8], st1_s[:], st1_s[:], ALU.mult, ['st1_s', 'rs_out'], ['rs_in'])
                    stt('dve', rs_in[0:16, 0:8], st2_s[:], 1.0 / 64, rs_in[0:16, 0:8], ALU.mult, ALU.subtract, ['st2_s', 'rs_in'], ['rs_in'])
                    ts('dve', rs_in[0:16, 0:8], rs_in[0:16, 0:8], 64e-5, None, ALU.add, None, ['rs_in'], ['rs_in'])
                    rsqrt(8, [])
                    tt('dve', v8(tS1[:]), v8(y_t[:]), bc_in(st1_s[:], 64), ALU.subtract, ['y_t', 'st1_s', 'tS1'], ['tS1'])
                    tt('dve', v8(tS1[:]), v8(tS1[:]), bc_in(rs_out[0:16, 0:8], 64), ALU.mult, ['tS1', 'rs_out'], ['tS1'])
                    tt('dve', tS1[:], tS1[:], pw(R_GNW), ALU.mult, ['tS1', 'pr1'], ['tS1'])
                    tt('dve', tS1[:], tS1[:], pw(R_GNB), ALU.add, ['tS1', 'pr1'], ['tS1'])
                    tt('dve', tS2[:], r_t, pw(R_RRK), ALU.mult, [('V6', 4), 'pr1'], ['tS2'])
                    tt('dve', tS2[:], tS2[:], V6[:, 3, :], ALU.mult, ['tS2', ('V6', 3)], ['tS2'])
                    red(bon[:], v8(tS2[:]), ['tS2'], ['bon'])
                    tt('dve', v8(tS2[:]), v8(v_t), bc_in(bon[:], 64), ALU.mult, [('V6', 5), 'bon', 'tS2'], ['tS2'])
                    tt('dve', tS1[:], tS1[:], tS2[:], ALU.add, ['tS1', 'tS2'], ['tS1'])
                    tt('dve', yr_bf[:], tS1[:], g_t[:], ALU.mult, ['tS1', 'g_t'], ['yr_bf'])
                    tr16(yr_bf, ['yr_bf'], yrT_s, 'yrT_s', 4)
                    S.barrier()

                with contextlib.ExitStack() as s2:
                    pr2 = SB(s2, "pr2", [16, 3584])
                    big1 = SB(s2, "big1", [128, 8192])
                    S.dma('sp', pr2[:], prow2_d[0:1, :].partition_broadcast(16), writes=['pr2'])
                    R_CB, R_MGN, R_MSK = 2048, 2560, 3072
                    cv = SB(s2, "cv", [16, 3, 512])
                    S.dma('sp', cv[:], conv0_d[:, :, :], writes=['cv'])
                    m0t = SB(s2, "m0t", [16, 4])
                    S.dma('sp', m0t[:], m0_d[:, :], writes=['m0t'])
                    C3 = big0[:].rearrange("p (d e) -> p d e", d=128)
                    TC3 = big1[:].rearrange("p (d e) -> p d e", d=128)
                    for eh in range(2):
                        S.dma('sp', big0[eh * 64:(eh + 1) * 64, :].rearrange("p (d e) -> p d e", d=128), C0_d[:, :, eh * 64:(eh + 1) * 64], reads=['Sb'], writes=[('C', eh)])
                    CK = [('C', 0), ('C', 1)]
                    n0p = SB(s2, "n0p", [128, 128])
                    for eh in range(2):
                        S.dma('sp', n0p[eh * 64:(eh + 1) * 64, :], n0_d[:, :], writes=[('n0p', eh)])
                    xc = SB(s2, "xc", [16, 512])
                    tM1 = SB(s2, "tM1", [16, 512])
                    tM2 = SB(s2, "tM2", [16, 512])
                    q_t = SB(s2, "q_t", [16, 512])
                    k_tm = SB(s2, "k_tm", [16, 512])
                    h_t = SB(s2, "h_t", [16, 512])
                    xc_bf = SB(s2, "xc_bf", [16, 512], BF16)
                    ym_bf = SB(s2, "ym_bf", [16, 512], BF16)
                    xcT_s = SB(s2, "xcT_s", [128, 4, 16], BF16)
                    gs = SB(s2, "gs", [16, 12, 4])
                    sc4 = SB(s2, "sc4", [16, 4, 4])
                    qp = SB(s2, "qp", [128, 128])
                    kp = SB(s2, "kp", [128, 128])
                    vp = SB(s2, "vp", [128, 64])
                    sc = SB(s2, "sc", [128, 4])
                    tq = SB(s2, "tq", [128, 128])
                    nn = SB(s2, "nn", [128, 128])
                    qC = SB(s2, "qC", [128, 64])
                    hp = SB(s2, "hp", [128, 64])
                    sm = SB(s2, "sm", [128, 4])
                    ms1 = SB(s2, "ms1s", [16, 4])
                    ms2 = SB(s2, "ms2s", [16, 4])
                    xm = P_s[:, 1792:2304]
                    mvv = P_s[:, 2304:2816]
                    oo = P_s[:, 2816:3328]
                    tt('dve', xc[:], cv[:, 0, :], pr2[:, 0:512], ALU.mult, ['cv', 'pr2'], ['xc'])
                    tt('dve', xc[:], xc[:], pr2[:, R_CB:R_CB + 512], ALU.add, ['xc', 'pr2'], ['xc'])
                    for tap in (1, 2):
                        tt('dve', tM1[:], cv[:, tap, :], pr2[:, tap * 512:(tap + 1) * 512], ALU.mult, ['cv', 'pr2'], ['tM1'])
                        tt('dve', xc[:], xc[:], tM1[:], ALU.add, ['xc', 'tM1'], ['xc'])
                    tt('dve', tM1[:], xm, pr2[:, 3 * 512:4 * 512], ALU.mult, PK + ['pr2'], ['tM1'])
                    tt('dve', xc[:], xc[:], tM1[:], ALU.add, ['xc', 'tM1'], ['xc'])
                    act(tM1[:], xc[:], AF.Tanh, ['xc'], ['tM1'], scale=0.5)
                    ts('dve', tM1[:], tM1[:], 0.5, 0.5, ALU.mult, ALU.add, ['tM1'], ['tM1'])
                    tt('dve', xc[:], xc[:], tM1[:], ALU.mult, ['xc', 'tM1'], ['xc'])
                    S.dma('sp', s_conv_d[:, 0:2, :], cv[:, 1:3, :], reads=['cv'], writes=['o_sconv0'])
                    S.dma('sp', s_conv_d[:, 2, :], xm, reads=PK, writes=['o_sconv1'])
                    cp('act', xc_bf[:], xc[:], ['xc'], ['xc_bf'])
                    tr16(xc_bf, ['xc_bf'], xcT_s, 'xcT_s', 4)
                    bq, bk = nb(), nb()
                    for hh in range(4):
                        mm(ps[bq][0:16, hh * 128:(hh + 1) * 128], xcT_s[:, hh, :], wq_bf[:, hh, :], True, True, ['xcT_s', 'wq_bf'], [psk[bq]])
                    for hh in range(4):
                        mm(ps[bk][0:16, hh * 128:(hh + 1) * 128], xcT_s[:, hh, :], wk_bf[:, hh, :], True, True, ['xcT_s', 'wk_bf'], [psk[bk]])
                    act(q_t[:], ps[bq][0:16, :], AF.Copy, [psk[bq]], ['q_t'], scale=128 ** -0.5)
                    cp('dve', k_tm[:], ps[bk][0:16, :], [psk[bk]], ['k_tm'])
                    G = lambda i: gs[:, i, :]
                    tt('dve', G(0), P_s[:, 3328:3332], gbt[0:16, 0:4], ALU.add, PK + ['gbt'], [('gs', 0)])
                    tt('dve', G(1), P_s[:, 3332:3336], gbt[0:16, 4:8], ALU.add, PK + ['gbt'], [('gs', 1)])
                    act(G(1), G(1), AF.Tanh, [('gs', 1)], [('gs', 1)], scale=0.5)
                    ts('dve', G(1), G(1), 0.5, 0.5, ALU.mult, ALU.add, [('gs', 1)], [('gs', 1)])
                    act(G(2), G(1), AF.Ln, [('gs', 1)], [('gs', 2)])
                    tt('dve', G(3), G(2), m0t[:], ALU.add, [('gs', 2), 'm0t'], [('gs', 3)])
                    tt('dve', G(4), G(3), G(0), ALU.max, [('gs', 3), ('gs', 0)], [('gs', 4)])
                    S.dma('sp', s_m_d[:, :], G(4), reads=[('gs', 4)], writes=['o_sm'])
                    tt('dve', G(5), G(0), G(4), ALU.subtract, [('gs', 0), ('gs', 4)], [('gs', 5)])
                    act(sc4[:, :, 0], G(5), AF.Exp, [('gs', 5)], [('sc4', 0)])
                    tt('dve', G(5), G(3), G(4), ALU.subtract, [('gs', 3), ('gs', 4), ('sc4', 0)], [('gs', 5)])
                    act(sc4[:, :, 1], G(5), AF.Exp, [('gs', 5)], [('sc4', 1)])
                    act(sc4[:, :, 2], G(4), AF.Exp, [('gs', 4)], [('sc4', 2)], scale=-1.0)
                    tt('dve', tM1[:], q_t[:], k_tm[:], ALU.mult, ['q_t', 'k_tm'], ['tM1'])
                    red(G(6), v4(tM1[:]), ['tM1'], [('gs', 6)])
                    tt('dve', sc4[:, :, 3], G(6), sc4[:, :, 0], ALU.mult, [('gs', 6), ('sc4', 0)], [('sc4', 3)])
                    SCK = [('sc4', i) for i in range(4)]
                    S.dma('sp', scrq.rearrange("(b h) d -> b (h d)", b=16), q_t[:], reads=['q_t'], writes=['scrq'])
                    S.dma('sp', scrk.rearrange("(b h) d -> b (h d)", b=16), k_tm[:], reads=['k_tm'], writes=['scrk'])
                    S.dma('sp', scrv.rearrange("(b h) d -> b (h d)", b=16), mvv, reads=PK, writes=['scrv'])
                    S.dma('sp', scrs.rearrange("(b h) s -> b (h s)", b=16), sc4[:].rearrange("p h s -> p (h s)"), reads=SCK, writes=['scrs'])
                    for eh in range(2):
                        psl = slice(eh * 64, (eh + 1) * 64)
                        S.dma('sp', qp[psl, :], scrq[:, :], reads=['scrq'], writes=[('qp', eh)])
                        S.dma('sp', kp[psl, :], scrk[:, :], reads=['scrk'], writes=[('kp', eh)])
                        S.dma('sp', vp[psl, :], scrv[:, eh * 64:(eh + 1) * 64], reads=['scrv'], writes=[('vp', eh)])
                        S.dma('sp', sc[psl, :], scrs[:, :], reads=['scrs'], writes=[('sc', eh)])
                    QP = [('qp', 0), ('qp', 1)]
                    KP = [('kp', 0), ('kp', 1)]
                    VP = [('vp', 0), ('vp', 1)]
                    SC = [('sc', 0), ('sc', 1)]
                    NP_ = [('n0p', 0), ('n0p', 1)]
                    tt('dve', TC3, C3, bc_in(qp[:], 64), ALU.mult, CK + QP + ['T1'], ['TC'])
                    red(qC[:], TC3.rearrange("p d e -> p e d"), ['TC'], ['qC'])
                    tt('dve', tq[:], qp[:], n0p[:], ALU.mult, QP + NP_, ['tq'])
                    red(sm[:, 0:1], tq[:], ['tq'], [('sm', 0)])
                    ts('dve', hp[:], qC[:], sc[:, 1:2], None, ALU.mult, None, ['qC'] + SC, ['hp'])
                    stt('dve', hp[:], vp[:], sc[:, 3:4], hp[:], ALU.mult, ALU.add, VP + SC + ['hp'], ['hp'])
                    stt('dve', sm[:, 1:2], sm[:, 0:1], sc[:, 1:2], sc[:, 3:4], ALU.mult, ALU.add, [('sm', 0)] + SC, [('sm', 1)])
                    stt('dve', sm[:, 2:3], sm[:, 1:2], -1.0, sm[:, 1:2], ALU.mult, ALU.max, [('sm', 1)], [('sm', 2)])
                    tt('dve', sm[:, 2:3], sm[:, 2:3], sc[:, 2:3], ALU.max, [('sm', 2)] + SC, [('sm', 2)])
                    S.op('dve', lambda en: en.reciprocal(out=sm[:, 3:4], in_=sm[:, 2:3]), reads=[('sm', 2)], writes=[('sm', 3)])
                    ts('dve', hp[:], hp[:], sm[:, 3:4], None, ALU.mult, None, ['hp', ('sm', 3)], ['hp'])
                    tt('dve', TC3, bc_in(kp[:], 64), bc_mid(vp[:], 128), ALU.mult, KP + VP + ['TC'], ['TC'])
                    ts('dve', big0[:], big0[:], sc[:, 1:2], None, ALU.mult, None, CK + SC, CK)
                    stt('dve', big0[:], big1[:], sc[:, 0:1], big0[:], ALU.mult, ALU.add, ['TC'] + CK + SC, CK)
                    ts('dve', nn[:], n0p[:], sc[:, 1:2], None, ALU.mult, None, NP_ + SC, ['nn'])
                    stt('dve', nn[:], kp[:], sc[:, 0:1], nn[:], ALU.mult, ALU.add, KP + SC + ['nn'], ['nn'])
                    for eh in range(2):
                        S.dma('sp', s_C_d[:, :, eh * 64:(eh + 1) * 64], big0[eh * 64:(eh + 1) * 64, :].rearrange("p (d e) -> p d e", d=128), reads=CK, writes=[('o_sC', eh)])
                        S.dma('sp', scrh[:, eh * 64:(eh + 1) * 64], hp[eh * 64:(eh + 1) * 64, :], reads=['hp'], writes=[('scrh', eh)])
                    S.dma('sp', s_n_d[:, :], nn[0:64, :], reads=['nn'], writes=['o_sn'])
                    S.dma('sp', h_t[:], scrh.rearrange("(b h) e -> b (h e)", b=16), reads=[('scrh', 0), ('scrh', 1)], writes=['h_t'])
                    red(ms1[:], v4(h_t[:]), ['h_t'], ['ms1s'])
                    tt('dve', tM1[:], h_t[:], h_t[:], ALU.mult, ['h_t'], ['tM1'])
                    red(ms2[:], v4(tM1[:]), ['tM1'], ['ms2s'])
                    ts('dve', ms1[:], ms1[:], 1.0 / 128, None, ALU.mult, None, ['ms1s'], ['ms1s'])
                    tt('dve', rs_in[0:16, 0:4], ms1[:], ms1[:], ALU.mult, ['ms1s', 'rs_out'], ['rs_in'])
                    stt('dve', rs_in[0:16, 0:4], ms2[:], 1.0 / 128, rs_in[0:16, 0:4], ALU.mult, ALU.subtract, ['ms2s', 'rs_in'], ['rs_in'])
                    ts('dve', rs_in[0:16, 0:4], rs_in[0:16, 0:4], 1e-5, None, ALU.add, None, ['rs_in'], ['rs_in'])
                    rsqrt(4, [])
                    tt('dve', v4(tM1[:]), v4(h_t[:]), bc_in(ms1[:], 128), ALU.subtract, ['h_t', 'ms1s', 'tM1'], ['tM1'])
                    tt('dve', v4(tM1[:]), v4(tM1[:]), bc_in(rs_out[0:16, 0:4], 128), ALU.mult, ['tM1', 'rs_out'], ['tM1'])
                    tt('dve', tM1[:], tM1[:], pr2[:, R_MGN:R_MGN + 512], ALU.mult, ['tM1', 'pr2'], ['tM1'])
                    tt('dve', tM2[:], xc[:], pr2[:, R_MSK:R_MSK + 512], ALU.mult, ['xc', 'pr2'], ['tM2'])
                    tt('dve', tM1[:], tM1[:], tM2[:], ALU.add, ['tM1', 'tM2'], ['tM1'])
                    act(tM2[:], oo, AF.Tanh, PK + ['tM2'], ['tM2'], scale=0.5)
                    ts('dve', tM2[:], tM2[:], 0.5, 0.5, ALU.mult, ALU.add, ['tM2'], ['tM2'])
                    tt('dve', ym_bf[:], tM1[:], tM2[:], ALU.mult, ['tM1', 'tM2'], ['ym_bf'])
                    tr16(ym_bf, ['ym_bf'], ymT_s, 'ymT_s', 4)
                    S.barrier()

                with contextlib.ExitStack() as s3:
                    pr3 = SB(s3, "pr3", [16, 4096])
                    S.dma('sp', pr3[:], prow3_d[0:1, :].partition_broadcast(16), writes=['pr3'])
                    mrg = SB(s3, "mrg", [16, D])
                    tg = SB(s3, "tg", [16, 512])
                    mrg_bf = SB(s3, "mrg_bf", [16, D], BF16)
                    mrgT_s = SB(s3, "mrgT_s", [128, 8, 16], BF16)
                    hn_s = SB(s3, "hn_s", [16, D])
                    hn_bf = SB(s3, "hn_bf", [16, D], BF16)
                    rl_s = SB(s3, "rl_s", [128, 512])
                    h1T_s = SB(s3, "h1T_s", [128, 512], BF16)
                    y_s = SB(s3, "y_s", [16, D])
                    for br, (pref, yT, yk) in enumerate((('s_ru', yrT_s, 'yrT_s'), ('s_mu', ymT_s, 'ymT_s'))):
                        for hh in range(2):
                            wu, wuk = w_get('%s%d' % (pref, hh))
                            bi = nb()
                            for kc in range(4):
                                mm(ps[bi][0:16, :], yT[:, kc, :], wu[:, kc, :], kc == 0, kc == 3, [wuk, yk], [psk[bi]])
                            w_done()
                            gcol = 3336 + br * 1024 + hh * 512
                            tt('dve', tg[:], P_s[:, gcol:gcol + 512], pr3[:, br * 1024 + hh * 512:br * 1024 + (hh + 1) * 512], ALU.add, PK + ['pr3', 'tg'], ['tg'])
                            act(tg[:], tg[:], AF.Tanh, ['tg'], ['tg'], scale=0.5)
                            ts('dve', tg[:], tg[:], 0.5, 0.5, ALU.mult, ALU.add, ['tg'], ['tg'])
                            if br == 0:
                                tt('dve', mrg[:, hh * 512:(hh + 1) * 512], tg[:], ps[bi][0:16, :], ALU.mult, ['tg', psk[bi]], [('mrg', hh)])
                            else:
                                tt('dve', tg[:], tg[:], ps[bi][0:16, :], ALU.mult, ['tg', psk[bi]], ['tg'])
                                tt('dve', mrg[:, hh * 512:(hh + 1) * 512], mrg[:, hh * 512:(hh + 1) * 512], tg[:], ALU.add, [('mrg', hh), 'tg'], [('mrg', hh)])
                    cp('act', mrg_bf[:], mrg[:], [('mrg', 0), ('mrg', 1)], ['mrg_bf'])
                    tr16(mrg_bf, ['mrg_bf'], mrgT_s, 'mrgT_s', 8)
                    for hh in range(2):
                        wo, wok = w_get('s_wo%d' % hh)
                        bi = nb()
                        for kc in range(8):
                            mm(ps[bi][0:16, :], mrgT_s[:, kc, :], wo[:, kc, :], kc == 0, kc == 7, [wok, 'mrgT_s'], [psk[bi]])
                        w_done()
                        tt('dve', x1_s[:, hh * 512:(hh + 1) * 512], xs_t[:, hh * 512:(hh + 1) * 512], ps[bi][0:16, :], ALU.add, ['xs_t', psk[bi]], [('x1_s', hh), 'x1_s'])
                    X1 = [('x1_s', 0), ('x1_s', 1)]
                    rms16(x1_s[:], X1, pr3[:, 3072:4096], ['pr3'], hn_s[:], ['hn_s'])
                    cp('act', hn_bf[:], hn_s[:], ['hn_s'], ['hn_bf'])
                    tr16(hn_bf, ['hn_bf'], hnT_sP, 'hnT_sP', 8)
                    S.barrier()
            S.barrier()

        x1_sP = SB(st, "x1_sP", [16, D])
        hnT_sP = SB(st, "hnT_sP", [128, 8, 16], BF16)
        if with_sample:
            sample_phase()
        xnT = SB(st, "xnT", [128, 8, TB], BF16)
        yrT = SB(st, "yrT", [128, 4, TB], BF16)
        ymT = SB(st, "ymT", [128, 4, TB], BF16)
        nofinal = False
        if stop is not None and stop.endswith('!'):
            nofinal = True
            stop = stop[:-1]
        try:
          chk('C')
          for blk in range(nblk):
              t0 = blk * TB
              last_blk = (blk == NBLK - 1)
              sr = contextlib.ExitStack()
              QKT = SB(sr, "QKT_%d" % blk, [128, 4, TB], BF16)
              RtT = SB(sr, "RtT_%d" % blk, [128, 4, TB], BF16)
              KtT = SB(sr, "KtT_%d" % blk, [128, 4, TB], BF16)
              nBtT = SB(sr, "nBtT_%d" % blk, [128, 4, TB], BF16)
              vT_bf = SB(sr, "vTbf_%d" % blk, [128, 4, TB], BF16)
              rkT_bf = SB(sr, "rkTbf_%d" % blk, [128, 4, TB], BF16)
              eL = SB(sr, "eL_%d" % blk, [128, 4, 4])
              eLp = SB(sr, "eLp_%d" % blk, [128, 4, 4])
              emid = SB(sr, "emid_%d" % blk, [128, 4, 4])
              Kt_tok = SB(sr, "Kttok_%d" % blk, [128, 4, 512], BF16)
              nBt_tok = SB(sr, "nBttok_%d" % blk, [128, 4, 512], BF16)
              V_tok = SB(sr, "Vtok_%d" % blk, [128, 4, 512], BF16)
              sgx_bf = SB(sr, "sgxbf_%d" % blk, [128, TB], BF16)
              sa = contextlib.ExitStack()
              if True:
                  xld = SB(sa, "xld_%d" % blk, [128, 4, D])
                  xs_bf = [SB(sa, "xsbf%d_%d" % (i, blk), [128, D], BF16) for i in range(2)]
                  junk = SB(sa, "junkA_%d" % blk, [128, D], BF16)
                  ssA = SB(sa, "ssA_%d" % blk, [128, 4])
                  S.op('pool', lambda en: en.memset(ssA[:], 0.0), writes=[('ssA', i) for i in range(4)])
                  for tt_ in range(4):
                      S.dma('sp', xld[:, tt_, :], x_d[t0 + tt_ * 128:t0 + (tt_ + 1) * 128, :], writes=[('xld', tt_)])
                      act(junk[:], xld[:, tt_, :], AF.Square, [('xld', tt_)], ['junkA', ('ssA', tt_)], accum=ssA[:, tt_:tt_ + 1])
                  ts('dve', rs_in[:, 0:4], ssA[:], 1.0 / D, 1e-6, ALU.mult, ALU.add, [('ssA', i) for i in range(4)], ['rs_in'])
                  rsqrt(4, [])
                  for tt_ in range(4):
                      xb_ = xs_bf[tt_ % 2]
                      xk = 'xsbf%d' % (tt_ % 2)
                      act(xb_[:], xld[:, tt_, :], AF.Copy, [('xld', tt_), 'rs_out'], [xk], scale=rs_out[:, tt_:tt_ + 1])
                      bi = nb()
                      psb = ps[bi][:].bitcast(BF16)
                      for kc in range(8):
                          tr(psb[:, kc * 128:(kc + 1) * 128], xb_[:, kc * 128:(kc + 1) * 128], ident_bf[:], [xk, 'ident_bf'], [psk[bi]])
                      tt('dve', xnT[:, :, tt_ * 128:(tt_ + 1) * 128], psb.rearrange("p (k t) -> p k t", k=8),
                         bc_in(pp[:, P_GMIX:P_GMIX + 8], 128), ALU.mult, [psk[bi], 'pp'], [('xnT', tt_)])
                  if last_blk:
                      xr = SB(sa, "xr", [1, D])
                      gr_ = SB(sa, "gr_", [1, D])
                      jr = SB(sa, "jr", [1, D])
                      ssr = SB(sa, "ssr", [1, 1])
                      S.op('pool', lambda en: en.memset(ssr[:], 0.0), writes=['ssr'])
                      S.dma('sp', xr[:], x_d[T - 1:T, :], writes=['xr'])
                      S.dma('sp', gr_[:], gmixrow_d[0:1, :], writes=['gr_'])
                      act(jr[:], xr[:], AF.Square, ['xr'], ['jr', 'ssr'], accum=ssr[:, 0:1])
                      ts('dve', rs_in[0:1, 0:1], ssr[:], 1.0 / D, 1e-6, ALU.mult, ALU.add, ['ssr', 'rs_out'], ['rs_in'])
                      rsqrt(1, [])
                      stt('dve', jr[:], xr[:], rs_out[0:1, 0:1], gr_[:], ALU.mult, ALU.mult, ['xr', 'rs_out', 'gr_'], ['jr'])
                      S.dma('sp', shift_d[0:1, :], jr[:], reads=['jr'], writes=['o_shift'])
              XN = [('xnT', i) for i in range(4)]
              chk('A%d' % blk)
              if blk == 0:
                  dbg("xnT", xnT[:, :, :], [128, 8, TB], XN)

              if True:
                  sp_ = contextlib.ExitStack()
                  TMPN = ('rT', 'kT', 'vT', 'sig', 'cs', 'EN', 'EP', 'EPX', 'aT', 'd2', 'tA', 'tB')
                  TS = []
                  for par in range(2):
                      d_ = {nm: SB(sp_, "%s%d_%d" % (nm, par, blk), [128, TB]) for nm in TMPN}
                      d_['sq_bf'] = SB(sp_, "sqbf%d_%d" % (par, blk), [128, TB], BF16)
                      d_['bia'] = SB(sp_, "bia%d_%d" % (par, blk), [128, 8])
                      d_['sbuf_s'] = [SB(sp_, "sbufs%d_%d_%d" % (i, par, blk), [128, TB + 1]) for i in range(2)]
                      TS.append(d_)
                  lat0 = SB(sp_, "lat0_%d" % blk, [128, TB])
                  lat1 = SB(sp_, "lat1_%d" % blk, [128, TB])
                  lat0_bf = SB(sp_, "lat0bf_%d" % blk, [128, TB], BF16)

                  def mix_evac(bi, piece, out_ap, okey, sb_, sk):
                      cp('act', sb_[:, 0:1], carry[:, piece:piece + 1], [('carry', piece)], [(sk, 0)])
                      act(sb_[:, 1:TB + 1], ps[bi][:], AF.Copy, [psk[bi], 'pp'], [(sk, 1)], scale=pp[:, P_MU + piece:P_MU + piece + 1])
                      stt('dve', out_ap, ps[bi][:], pq[:, Q_OMU + piece:Q_OMU + piece + 1], sb_[:, 0:TB], ALU.mult, ALU.add,
                          [psk[bi], 'pq', (sk, 0), (sk, 1)], [okey])
                      cp('act', carry[:, piece:piece + 1], sb_[:, TB:TB + 1], [(sk, 1)], [('carry', piece)])

                  wl, wk_ = w_get('lat')
                  bl = [nb(), nb()]
                  for pc in range(2):
                      for kc in range(8):
                          mm(ps[bl[pc]][:], wl[:, kc, pc * 128:(pc + 1) * 128], xnT[:, kc, :], kc == 0, kc == 7, [wk_] + XN, [psk[bl[pc]]])
                  w_done()
                  mix_evac(bl[0], 0, lat0[:], 'lat0', TS[0]['sbuf_s'][0], ('sbufs', 0, 0))
                  mix_evac(bl[1], 1, lat1[:], 'lat1', TS[0]['sbuf_s'][1], ('sbufs', 0, 1))
                  act(lat0_bf[0:64, :], lat0[0:64, :], AF.Tanh, ['lat0'], [('lat0bf', 0)])
                  cp('dve', lat0_bf[64:128, :], lat0[64:128, :], ['lat0'], [('lat0bf', 1)])
                  act(TS[0]['tA'][:], lat1[:], AF.Tanh, ['lat1'], [('tA', 0)], scale=0.5)
                  fix05(sgx_bf[:], TS[0]['tA'][:], [('tA', 0)], ['sgxbf'])
                  chk('Rlat%d' % blk)

                  def prep_j(j, par):
                      B_ = TS[par]
                      rT, kT, vT, sig, cs, EN, EP, EPX, aT, d2, tA, tB_ = (B_[n_] for n_ in TMPN)
                      sq_bf, bia, sbs = B_['sq_bf'], B_['bia'], B_['sbuf_s']
                      K = lambda nm: (nm, par)
                      wj, wkj = w_get('r%d' % j)
                      br = [nb(), nb(), nb()]
                      for pc in range(3):
                          for kc in range(8):
                              mm(ps[br[pc]][:], wj[:, kc, pc * 128:(pc + 1) * 128], xnT[:, kc, :], kc == 0, kc == 7, [wkj] + XN, [psk[br[pc]]])
                      w_done()
                      mix_evac(br[0], 2 + 3 * j, rT[:], K('rT'), sbs[0], ('sbufs', par, 0))
                      mix_evac(br[1], 3 + 3 * j, kT[:], K('kT'), sbs[1], ('sbufs', par, 1))
                      mix_evac(br[2], 4 + 3 * j, vT[:], K('vT'), sbs[0], ('sbufs', par, 0))
                      yield
                      bw, ba = nb(), nb()
                      mm(ps[bw][:], w2a2_bf[0:64, j * 128:(j + 1) * 128], lat0_bf[0:64, :], True, True, ['w2a2_bf', ('lat0bf', 0)], [psk[bw]])
                      mm(ps[ba][:], w2a2_bf[64:128, j * 128:(j + 1) * 128], lat0_bf[64:128, :], True, True, ['w2a2_bf', ('lat0bf', 1)], [psk[ba]])
                      act(tA[:], ps[bw][:], AF.Tanh, [psk[bw], 'pq'], [K('tA')], bias=pq[:, Q_HW0 + j:Q_HW0 + j + 1], scale=0.5)
                      act(tB_[:], ps[ba][:], AF.Tanh, [psk[ba], 'pq'], [K('tB')], bias=pq[:, Q_HA0 + j:Q_HA0 + j + 1], scale=0.5)
                      act(sq_bf[:], kT[:], AF.Square, [K('kT'), 'pp'], [K('sqbf')], scale=pp[:, P_RKK + j:P_RKK + j + 1])
                      bs_ = nb()
                      mm(ps[bs_][:], bones[:], sq_bf[:], True, True, ['bones', K('sqbf')], [psk[bs_]])
                      fix05(sig[:], tA[:], [K('tA')], [K('sig')])
                      fix05(aT[:], tB_[:], [K('tB')], [K('aT')])
                      act(d2[:], ps[bs_][:], AF.Identity, [psk[bs_], 'eps_c'], [K('d2')], bias=eps_c[:, 0:1])
                      yield
                      for c in range(4):
                          S.op('dve', lambda en, c=c: en.tensor_tensor_scan(out=cs[:, c * 128:(c + 1) * 128], data0=ones_f[:, :], data1=sig[:, c * 128:(c + 1) * 128],
                                                                       initial=0.0, op0=ALU.mult, op1=ALU.add), reads=[K('sig'), 'ones_f'], writes=[('cs', par, c)])
                      CSK = [('cs', par, c) for c in range(4)]
                      mids = cs[:].rearrange("p (c t) -> p c t", c=4)[:, :, 63]
                      lasts = cs[:].rearrange("p (c t) -> p c t", c=4)[:, :, 127]
                      ts('dve', bia[:, 0:4], mids, CC, None, ALU.mult, None, CSK, [('bia', par, 0)])
                      ts('dve', bia[:, 4:8], mids, -CC, None, ALU.mult, None, CSK, [('bia', par, 1)])
                      S.op('dve', lambda en: en.reciprocal(out=d2[:], in_=d2[:]), reads=[K('d2')], writes=[K('d2')])
                      yield
                      for c in range(4):
                          sl_ = slice(c * 128, (c + 1) * 128)
                          act(EN[:, sl_], cs[:, sl_], AF.Exp, CSK + [('bia', par, 1)], [('EN', par, c)], bias=bia[:, 4 + c:5 + c], scale=CC)
                          act(EPX[:, c * 128 + 1:(c + 1) * 128], cs[:, c * 128:(c + 1) * 128 - 1], AF.Exp, CSK + [('bia', par, 0)], [('EPX', par, c)], bias=bia[:, c:c + 1], scale=-CC)
                          act(EP[:, sl_], cs[:, sl_], AF.Exp, CSK + [('bia', par, 0)], [('EP', par, c)], bias=bia[:, c:c + 1], scale=-CC)
                          if c == 1:
                              yield
                      ENK = [('EN', par, c) for c in range(4)]
                      EPK = [('EP', par, c) for c in range(4)]
                      EPXK = [('EPX', par, c) for c in range(4)]
                      act(eL[:, j, :], lasts, AF.Exp, CSK, [('eL', j)], scale=-CC)
                      act(emid[:, j, :], mids, AF.Exp, CSK, [('emid', j)], scale=-CC)
                      act(EPX[:].rearrange("p (c t) -> p c t", c=4)[:, :, 0], mids, AF.Exp, CSK + [('EPX', par, c_) for c_ in range(4)], [('EPX', par, c_) for c_ in range(4)], scale=CC)
                      cp('act', eLp[:, j, :], EP[:].rearrange("p (c t) -> p c t", c=4)[:, :, 127], EPK, [('eLp', j)])
                      yield
                      stt('dve', tA[:], kT[:], pp[:, P_RKK + j:P_RKK + j + 1], EPX[:], ALU.mult, ALU.mult, [K('kT'), 'pp'] + EPXK, [K('tA')])
                      tt('dve', QKT[:, j, :], tA[:], d2[:], ALU.mult, [K('tA'), K('d2')], [('QKT', j)])
                      stt('dve', tB_[:], kT[:], pp[:, P_RKK + j:P_RKK + j + 1], aT[:], ALU.mult, ALU.mult, [K('kT'), 'pp', K('aT')], [K('tB')])
                      stt('dve', nBtT[:, j, :], tB_[:], -1.0, EN[:], ALU.mult, ALU.mult, [K('tB')] + ENK, [('nBtT', j)])
                      yield
                      act(tA[:], aT[:], AF.Identity, [K('aT'), 'pp', 'pq'], [K('tA')], bias=pq[:, Q_OMKA + j:Q_OMKA + j + 1], scale=pp[:, P_KA + j:P_KA + j + 1])
                      tt('dve', tA[:], tA[:], kT[:], ALU.mult, [K('tA'), K('kT')], [K('tA')])
                      tt('dve', KtT[:, j, :], tA[:], EN[:], ALU.mult, [K('tA')] + ENK, [('KtT', j)])
                      tt('dve', RtT[:, j, :], rT[:], EP[:], ALU.mult, [K('rT')] + EPK, [('RtT', j)])
                      stt('dve', rkT_bf[:, j, :], rT[:], pp[:, P_RRK + j:P_RRK + j + 1], tA[:], ALU.mult, ALU.mult, [K('rT'), 'pp', K('tA')], [('rkT', j)])
                      cp('act', vT_bf[:, j, :], vT[:], [K('vT')], [('vTbf', j)])

                  gens = [prep_j(j, j % 2) for j in range(4)]
                  SKEW = 3
                  active, prog, nxt_g = [], {}, 0
                  while nxt_g < 4 or active:
                      if len(active) < 2 and nxt_g < 4 and (not active or prog[active[0]] >= SKEW):
                          active.append(nxt_g)
                          prog[nxt_g] = 0
                          nxt_g += 1
                      for gi in list(active):
                          try:
                              next(gens[gi])
                              prog[gi] += 1
                          except StopIteration:
                              active.remove(gi)
                  S.barrier()
                  sp_.close()
                  sa.close()

                  chk('Rprep%d' % blk)
                  QK_K = [('QKT', j) for j in range(4)]
                  RT_K = [('RtT', j) for j in range(4)]
                  KT_K = [('KtT', j) for j in range(4)]
                  BT_K = [('nBtT', j) for j in range(4)]
                  for c in range(4):
                      for (src, dst, sk, dk) in ((KtT, Kt_tok, KT_K, 'Kttok'), (nBtT, nBt_tok, BT_K, 'nBttok'), (vT_bf, V_tok, [('vTbf', j) for j in range(4)], 'Vtok')):
                          bi = nb()
                          psb = ps[bi][:].bitcast(BF16)
                          for j in range(4):
                              tr(psb[:, j * 128:(j + 1) * 128], src[:, j, c * 128:(c + 1) * 128], ident_bf[:], sk + ['ident_bf'], [psk[bi]])
                          cp('act', dst[:, c, :], psb[:, 0:512], [psk[bi]], [(dk, c)])

                  chk('Rtr%d' % blk)
                  smm = contextlib.ExitStack()
                  xmT = SB(smm, "xmT_%d" % blk, [128, 4, TB + 3])
                  xcT = SB(smm, "xcT_%d" % blk, [128, 4, TB])
                  xcT_bf = SB(smm, "xcTbf_%d" % blk, [128, 4, TB], BF16)
                  soT = SB(smm, "soT_%d" % blk, [128, 4, TB], BF16)
                  qT_bf = SB(smm, "qTbf_%d" % blk, [128, 4, TB], BF16)
                  kT_bf = SB(smm, "kTbf_%d" % blk, [128, 4, TB], BF16)
                  k_tok = SB(smm, "ktok_%d" % blk, [128, 4, 512], BF16)
                  Vp = SB(smm, "Vp_%d" % blk, [128, 4, 4, 129], BF16)
                  mA = SB(smm, "mA_%d" % blk, [128, TB])
                  mB = SB(smm, "mB_%d" % blk, [128, TB])
                  gtok = SB(smm, "gtok_%d" % blk, [128, 4, 8])
                  gT_i = SB(smm, "gTi_%d" % blk, [4, TB])
                  gT_s = SB(smm, "gTs_%d" % blk, [4, TB])
                  gT_cp = SB(smm, "gTcp_%d" % blk, [4, TB])
                  gT_eb = SB(smm, "gTeb_%d" % blk, [4, TB])
                  gT_M = SB(smm, "gTM_%d" % blk, [4, TB])
                  gtk = SB(smm, "gtk_%d" % blk, [128, 4, 2, 4])
                  cpLb = SB(smm, "cpLb_%d" % blk, [128, 4, 4])
                  dg4 = SB(smm, "dg4_%d" % blk, [4, 4, 4])
                  ST_bf = SB(smm, "STbf_%d" % blk, [128, 512], BF16)
                  numS = SB(smm, "numS_%d" % blk, [128, 4, 129])
                  hS = SB(smm, "hS_%d" % blk, [128, 512])
                  hsq = SB(smm, "hsq_%d" % blk, [128, 512])
                  hn_bf = SB(smm, "hnbf_%d" % blk, [128, 512], BF16)
                  den4 = SB(smm, "den4_%d" % blk, [128, 4])
                  ms1 = SB(smm, "ms1_%d" % blk, [128, 4])
                  ms2 = SB(smm, "ms2_%d" % blk, [128, 4])
                  rrM = [0]

                  def nbM():
                      rrM[0] = (rrM[0] + 1) % 3
                      return 5 + rrM[0]

                  def m_gen():

                      cp('act', xmT[:, :, 0:3], xmc[:, :, :], ['xmc'], [('xmT', 'c')])
                      wx, wxk = w_get('xm')
                      for jj in range(4):
                          bi = nbM()
                          for kc in range(8):
                              mm(ps[bi][:], wx[:, kc, jj * 128:(jj + 1) * 128], xnT[:, kc, :], kc == 0, kc == 7, [wxk] + XN, [psk[bi]])
                          cp('act', xmT[:, jj, 3:TB + 3], ps[bi][:], [psk[bi]], [('xmT', jj)])
                          yield
                      w_done()
                      yield
                      XMK = [('xmT', 'c')] + [('xmT', jj) for jj in range(4)]
                      if last_blk:
                          for jj in range(4):
                              S.dma('sp', conv_d[:, jj * 128:(jj + 1) * 128].rearrange("t p -> p t"), xmT[:, jj, TB:TB + 3], reads=XMK, writes=[('o_conv', jj)])
                      cp('act', xmc[:, :, :], xmT[:, :, TB:TB + 3], XMK, ['xmc'])
                      for jj in range(4):
                          cw = lambda tap: pp[:, P_CW + tap * 4 + jj:P_CW + tap * 4 + jj + 1]
                          ts('dve', mA[:], xmT[:, jj, 0:TB], cw(0), pp[:, P_CB + jj:P_CB + jj + 1], ALU.mult, ALU.add, XMK + ['pp'], ['mA'])
                          for tap in (1, 2, 3):
                              stt('dve', mA[:], xmT[:, jj, tap:tap + TB], cw(tap), mA[:], ALU.mult, ALU.add, XMK + ['pp', 'mA'], ['mA'])
                          act(mB[:], mA[:], AF.Tanh, ['mA'], ['mB'], scale=0.5)
                          fix05(mB[:], mB[:], ['mB'], ['mB'])
                          tt('dve', xcT[:, jj, :], mA[:], mB[:], ALU.mult, ['mA', 'mB'], [('xcT', jj)])
                          cp('act', xcT_bf[:, jj, :], xcT[:, jj, :], [('xcT', jj)], [('xcTbf', jj)])
                          yield
                      XCB = [('xcTbf', jj) for jj in range(4)]
                      for hh in range(4):
                          bi = nbM()
                          mm(ps[bi][:], wq_bf[:, hh, :], xcT_bf[:, hh, :], True, True, ['wq_bf'] + XCB, [psk[bi]])
                          act(qT_bf[:, hh, :], ps[bi][:], AF.Copy, [psk[bi]], [('qTbf', hh)], scale=128 ** -0.5)
                          bi = nbM()
                          mm(ps[bi][:], wk_bf[:, hh, :], xcT_bf[:, hh, :], True, True, ['wk_bf'] + XCB, [psk[bi]])
                          cp('dve', kT_bf[:, hh, :], ps[bi][:], [psk[bi]], [('kTbf', hh)])
                          yield
                      for c in range(4):
                          bi = nbM()
                          for hh in range(4):
                              mm(ps[bi][:, hh * 128:(hh + 1) * 128], xcT_bf[:, hh, c * 128:(c + 1) * 128], wk_bf[:, hh, :], True, True, ['wk_bf'] + XCB, [psk[bi]])
                          cp('act', k_tok[:, c, :], ps[bi][:], [psk[bi]], [('ktok', c)])
                          yield
                      wg, wgk = w_get('gt')
                      b_i, b_f = nbM(), nbM()
                      for kc in range(8):
                          mm(ps[b_i][0:4, :], wg[:, kc, 0:4], xnT[:, kc, :], kc == 0, kc == 7, [wgk] + XN, [psk[b_i]])
                      for kc in range(8):
                          mm(ps[b_f][0:4, :], wg[:, kc, 4:8], xnT[:, kc, :], kc == 0, kc == 7, [wgk] + XN, [psk[b_f]])
                      w_done()
                      yield
                      act(gT_i[:], ps[b_i][0:4, :], AF.Exp, [psk[b_i], 'gbf'], ['gTi'], bias=gbf[:, 0:1])
                      act(gT_s[:], ps[b_f][0:4, :], AF.Tanh, [psk[b_f], 'hfb'], ['gTs'], bias=hfb[:, 0:1], scale=0.5)
                      ts('dve', gT_s[:], gT_s[:], 0.5, 0.5, ALU.mult, ALU.add, ['gTs'], ['gTs'])
                      for c in range(4):
                          sl_ = slice(c * 128, (c + 1) * 128)
                          S.op('dve', lambda en, sl_=sl_: en.tensor_tensor_scan(out=gT_cp[:, sl_], data0=gT_s[:, sl_], data1=zeros_f[0:4, :], initial=1.0,
                                                                       op0=ALU.mult, op1=ALU.add), reads=['gTs', 'zeros_f'], writes=[('gTcp', c)])
                      CPK = [('gTcp', c) for c in range(4)]
                      S.op('dve', lambda en: en.tensor_tensor_scan(out=gT_M[:], data0=gT_s[:], data1=gT_i[:], initial=Mst[:, 0:1],
                                                                   op0=ALU.mult, op1=ALU.max), reads=['gTs', 'gTi', 'Mst'], writes=['gTM'])
                      cp('dve', Mst[:, 0:1], gT_M[:, TB - 1:TB], ['gTM'], ['Mst'])
                      S.op('dve', lambda en: en.reciprocal(out=gT_eb[:], in_=gT_cp[:]), reads=CPK, writes=['gTeb'])
                      tt('dve', gT_eb[:], gT_eb[:], gT_i[:], ALU.mult, ['gTeb', 'gTi'], ['gTeb'])
                      b_g1 = nbM()
                      for c in range(4):
                          sl_ = slice(c * 128, (c + 1) * 128)
                          tr(ps[b_g1][:, c * 8:c * 8 + 4], gT_cp[:, sl_], ident_f[0:4, 0:4], CPK + ['ident_f'], [psk[b_g1]])
                          tr(ps[b_g1][:, c * 8 + 4:c * 8 + 8], gT_eb[:, sl_], ident_f[0:4, 0:4], ['gTeb', 'ident_f'], [psk[b_g1]])
                      cp('dve', gtk[:].rearrange("p c a h -> p (c a h)"), ps[b_g1][:, 0:32], [psk[b_g1]], ['gtk'])
                      for c in range(4):
                          ts('dve', dg4[:, c, :], ident_f[0:4, 0:4], gT_cp[:, c * 128 + 127:c * 128 + 128], None, ALU.mult, None, CPK + ['ident_f'], [('dg4', c)])
                      b_g2 = nbM()
                      mm(ps[b_g2][:, 0:16], ones_f[0:4, :], dg4[:].rearrange("p c h -> p (c h)"), True, True, ['ones_f'] + [('dg4', c) for c in range(4)], [psk[b_g2]])
                      cp('dve', cpLb[:].rearrange("p c h -> p (c h)"), ps[b_g2][:, 0:16], [psk[b_g2]], ['cpLb'])
                      wv, wvk = w_get('mv')
                      for c in range(4):
                          bi = nbM()
                          for kc in range(8):
                              mm(ps[bi][:], xnT[:, kc, c * 128:(c + 1) * 128], wv[:, kc, :], kc == 0, kc == 7, [wvk] + XN, [psk[bi]])
                          tt('dve', Vp[:, c, :, 0:128], ps[bi][:].rearrange("p (h e) -> p h e", h=4), bc_in(gtk[:, c, 1, :], 128), ALU.mult, [psk[bi], 'gtk'], [('Vp', c)])
                          cp('act', Vp[:, c, :, 128], gtk[:, c, 1, :], ['gtk', ('Vp', c)], [('Vp', c)])
                          yield
                      w_done()
                      yield
                      wo_, wok = w_get('o')
                      for jj in range(4):
                          bi = nbM()
                          for kc in range(8):
                              mm(ps[bi][:], wo_[:, kc, jj * 128:(jj + 1) * 128], xnT[:, kc, :], kc == 0, kc == 7, [wok] + XN, [psk[bi]])
                          act(soT[:, jj, :], ps[bi][:], AF.Tanh, [psk[bi]], [('soT', jj)], scale=0.5)
                          fix05(soT[:, jj, :], soT[:, jj, :], [('soT', jj)], [('soT', jj)])
                          yield
                      w_done()
                      yield
                      QTK = [('qTbf', hh) for hh in range(4)]
                      KTK = [('kTbf', hh) for hh in range(4)]
                      for c in range(4):
                          csl = slice(c * 128, (c + 1) * 128)
                          b_s = nbM()
                          for hh in range(4):
                              mm(ps[b_s][:, hh * 128:(hh + 1) * 128], kT_bf[:, hh, csl], qT_bf[:, hh, csl], True, True, QTK + KTK, [psk[b_s]])
                          tt('dve', ST_bf[:].rearrange("p (a b) -> p a b", a=4), ps[b_s][:].rearrange("p (a b) -> p a b", a=4), bc_mid(ui_f[:], 4), ALU.mult, [psk[b_s], 'ui_f'], ['STbf'])
                          yield
                          b_n = [nbM(), nbM()]
                          for hh in range(4):
                              o_ = ps[b_n[hh // 2]][:, (hh % 2) * 129:(hh % 2) * 129 + 129]
                              mm(o_, ST_bf[:, hh * 128:(hh + 1) * 128], Vp[:, c, hh, :], True, False, ['STbf', ('Vp', c)], [psk[b_n[hh // 2]]])
                              mm(o_, qT_bf[:, hh, csl], Caug_bf[:, hh, :], False, True, QTK + ['Caug_bf'], [psk[b_n[hh // 2]]])
                          for g in range(2):
                              tt('dve', numS[:, 2 * g:2 * g + 2, :], ps[b_n[g]][:, 0:258].rearrange("p (h e) -> p h e", h=2), bc_in(gtk[:, c, 0, 2 * g:2 * g + 2], 129), ALU.mult,
                                 [psk[b_n[g]], 'gtk'], [('numS', g)])
                          NK = [('numS', 0), ('numS', 1)]
                          stt('dve', den4[:], numS[:, :, 128], -1.0, numS[:, :, 128], ALU.mult, ALU.max, NK, ['den4'])
                          ts('dve', den4[:], den4[:], 1.0, None, ALU.max, None, ['den4'], ['den4'])
                          S.op('dve', lambda en: en.reciprocal(out=den4[:], in_=den4[:]), reads=['den4'], writes=['den4'])
                          tt('dve', hS[:].rearrange("p (h e) -> p h e", h=4), numS[:, :, 0:128], bc_in(den4[:], 128), ALU.mult, NK + ['den4'], ['hS'])
                          yield
                          if blk == 0 and c == 0:
                              dbg("hS_c0", hS[:], [128, 512], ['hS'])
                          b_c = [nbM(), nbM()]
                          for hh in range(4):
                              o_ = ps[b_c[hh // 2]][:, (hh % 2) * 129:(hh % 2) * 129 + 129]
                              mm(o_, k_tok[:, c, hh * 128:(hh + 1) * 128], Vp[:, c, hh, :], True, True, [('ktok', c), ('Vp', c)], [psk[b_c[hh // 2]]])
                          for g in range(2):
                              tt('dve', Caug[:, 2 * g:2 * g + 2, :], Caug[:, 2 * g:2 * g + 2, :], ps[b_c[g]][:, 0:258].rearrange("p (h e) -> p h e", h=2), ALU.add,
                                 ['Caug', psk[b_c[g]]], ['Caug'])
                          tt('dve', Caug[:], Caug[:], bc_in(cpLb[:, c, :], 129), ALU.mult, ['Caug', 'cpLb'], ['Caug'])
                          cp('act', Caug_bf[:], Caug[:], ['Caug'], ['Caug_bf'])
                          yield
                          S.op('dve', lambda en: en.tensor_reduce(out=ms1[:], in_=hS[:].rearrange("p (h e) -> p h e", h=4), axis=AX.X, op=ALU.add), reads=['hS'], writes=['ms1'])
                          act(hsq[:], hS[:], AF.Square, ['hS'], ['hsq'])
                          S.op('dve', lambda en: en.tensor_reduce(out=ms2[:], in_=hsq[:].rearrange("p (h e) -> p h e", h=4), axis=AX.X, op=ALU.add), reads=['hsq'], writes=['ms2'])
                          ts('dve', ms1[:], ms1[:], 1.0 / 128, None, ALU.mult, None, ['ms1'], ['ms1'])
                          tt('dve', rsB_in[:, 0:4], ms1[:], ms1[:], ALU.mult, ['ms1', 'rsB_out'], ['rsB_in'])
                          stt('dve', rsB_in[:, 0:4], ms2[:], 1.0 / 128, rsB_in[:, 0:4], ALU.mult, ALU.subtract, ['ms2', 'rsB_in'], ['rsB_in'])
                          ts('dve', rsB_in[:, 0:4], rsB_in[:, 0:4], 1e-5, None, ALU.add, None, ['rsB_in'], ['rsB_in'])
                          rsqrt(4, [], bgset=True)
                          tt('dve', hsq[:].rearrange("p (h e) -> p h e", h=4), hS[:].rearrange("p (h e) -> p h e", h=4), bc_in(ms1[:], 128), ALU.subtract, ['hS', 'ms1', 'hsq'], ['hsq'])
                          tt('dve', hn_bf[:].rearrange("p (h e) -> p h e", h=4), hsq[:].rearrange("p (h e) -> p h e", h=4), bc_in(rsB_out[:, 0:4], 128), ALU.mult, ['hsq', 'rsB_out'], ['hnbf'])
                          yield
                          b_t = nbM()
                          psb = ps[b_t][:].bitcast(BF16)
                          for hh in range(4):
                              tr(psb[:, hh * 128:(hh + 1) * 128], hn_bf[:, hh * 128:(hh + 1) * 128], ident_bf[:], ['hnbf', 'ident_bf'], [psk[b_t]])
                          v4 = lambda ap: ap.rearrange("p (j t) -> p j t", j=4)
                          tt('dve', v4(mA[:]), v4(psb[:, 0:512]), bc_in(pp[:, P_MGN:P_MGN + 4], 128), ALU.mult, [psk[b_t], 'pp'], ['mA'])
                          tt('dve', v4(mB[:]), xcT[:, :, csl], bc_in(pp[:, P_MSK:P_MSK + 4], 128), ALU.mult, [('xcT', jj) for jj in range(4)] + ['pp'], ['mB'])
                          tt('dve', mA[:], mA[:], mB[:], ALU.add, ['mA', 'mB'], ['mA'])
                          tt('dve', ymT[:, :, csl], v4(mA[:]), soT[:, :, csl], ALU.mult, ['mA'] + [('soT', jj) for jj in range(4)], [('ymT', c)])
                          yield
                      yield

                  gM = [m_gen()]

                  with contextlib.ExitStack() as sm:
                      MTs, MTp = {}, {}
                      for hf in range(2):
                          for nm in ('PA0', 'PA1', 'PB0', 'PB1', 'T0', 'T1'):
                              MTs[(nm, hf)] = SB(sm, "%s%d_%d" % (nm, hf, blk), [128, 512], BF16)
                          for par in range(2):
                              for nm in ('AkvT', 'ArkT', 'nArbT', 'Tf'):
                                  MTp[(nm, hf, par)] = SB(sm, "%s%d%d_%d" % (nm, hf, par, blk), [128, 512], BF16)
                      RHS_sb = SB(sm, "RHSsb_%d" % blk, [128, 512], BF16)
                      U_sb = SB(sm, "Usb_%d" % blk, [128, 512], BF16)
                      Htmp = SB(sm, "Htmp_%d" % blk, [128, 256])
                      y_sb = SB(sm, "ysb_%d" % blk, [128, 512])
                      yn_bf = SB(sm, "ynbf_%d" % blk, [128, 512], BF16)
                      st1 = SB(sm, "st1_%d" % blk, [128, 8])
                      st2 = SB(sm, "st2_%d" % blk, [128, 8])
                      y2 = SB(sm, "y2_%d" % blk, [128, 512])
                      y3 = SB(sm, "y3_%d" % blk, [128, 512])
                      rrA, rrB = [0], [0]

                      def nbA():
                          rrA[0] = (rrA[0] + 1) % 2
                          return rrA[0]

                      def nbB():
                          rrB[0] = (rrB[0] + 1) % 3
                          return 2 + rrB[0]

                      def stepM():
                          if gM[0] is not None:
                              try:
                                  next(gM[0])
                              except StopIteration:
                                  gM[0] = None

                      def gen_mat(c):
                          par = c % 2
                          csl = slice(c * 128, (c + 1) * 128)

                          def hv(tile_, h):
                              jj, base = h // 2, (h % 2) * 64
                              return tile_[base:base + 64, jj, csl]
                          for hf in range(2):
                              heads = [2 * i + hf for i in range(4)]
                              specs = (('PB0', nBtT, QKT, su_f, 'su_f', BT_K + QK_K),
                                       ('PA0', QKT, nBtT, sl_f, 'sl_f', BT_K + QK_K),
                                       ('AkvT', KtT, QKT, su_f, 'su_f', KT_K + QK_K),
                                       ('ArkT', KtT, RtT, ui_f, 'ui_f', KT_K + RT_K),
                                       ('nArbT', nBtT, RtT, ui_f, 'ui_f', BT_K + RT_K))
                              for (nm, lt, rt, mask, mkk, rk_) in specs:
                                  if nm in ('PA0', 'PB0'):
                                      dst, dk = MTs[(nm, hf)], ('M', nm, hf)
                                  else:
                                      dst, dk = MTp[(nm, hf, par)], ('M', nm, hf, par)
                                  bi = nbA()
                                  for i, h in enumerate(heads):
                                      mm(ps[bi][:, i * 128:(i + 1) * 128], hv(lt, h), hv(rt, h), True, True, rk_, [psk[bi]])
                                  tt('dve', dst[:].rearrange("p (a b) -> p a b", a=4), ps[bi][:].rearrange("p (a b) -> p a b", a=4),
                                     bc_mid(mask[:], 4), ALU.mult, [psk[bi], mkk], [dk])
                                  yield
                              tt('dve', MTs[('T0', hf)][:].rearrange("p (a b) -> p a b", a=4), MTs[('PB0', hf)][:].rearrange("p (a b) -> p a b", a=4),
                                 bc_mid(ident_f[:], 4), ALU.add, [('M', 'PB0', hf), 'ident_f'], [('M', 'T0', hf)])
                          cur = 0
                          for lvl in range(6):
                              nxt = 1 - cur
                              lastl = (lvl == 5)
                              for hf in range(2):
                                  mk = lambda nm: ('M', nm, hf)
                                  PAc, PBc = MTs[('PA%d' % cur, hf)], MTs[('PB%d' % cur, hf)]
                                  ba_ = nbA()
                                  for i in range(4):
                                      sl_ = slice(i * 128, (i + 1) * 128)
                                      mm(ps[ba_][:, sl_], PBc[:, sl_], PAc[:, sl_], True, True, [mk('PA%d' % cur), mk('PB%d' % cur)], [psk[ba_]])
                                  cp('act', MTs[('PA%d' % nxt, hf)][:], ps[ba_][:], [psk[ba_]], [mk('PA%d' % nxt)])
                                  if not lastl:
                                      bb_ = nbA()
                                      for i in range(4):
                                          sl_ = slice(i * 128, (i + 1) * 128)
                                          mm(ps[bb_][:, sl_], PAc[:, sl_], PBc[:, sl_], True, True, [mk('PA%d' % cur), mk('PB%d' % cur)], [psk[bb_]])
                                      cp('act', MTs[('PB%d' % nxt, hf)][:], ps[bb_][:], [psk[bb_]], [mk('PB%d' % nxt)])
                                  yield
                              for hf in range(2):
                                  mk = lambda nm: ('M', nm, hf)
                                  PAn = MTs[('PA%d' % nxt, hf)]
                                  Tc = MTs[('T%d' % cur, hf)]
                                  if lastl:
                                      Tn, tnk = MTp[('Tf', hf, par)], ('M', 'Tf', hf, par)
                                  else:
                                      Tn, tnk = MTs[('T%d' % nxt, hf)], mk('T%d' % nxt)
                                  bt_ = nbA()
                                  for i in range(4):
                                      sl_ = slice(i * 128, (i + 1) * 128)
                                      mm(ps[bt_][:, sl_], PAn[:, sl_], Tc[:, sl_], True, True, [mk('PA%d' % nxt), mk('T%d' % cur)], [psk[bt_]])
                                  tt('dve', Tn[:], ps[bt_][:], Tc[:], ALU.add, [psk[bt_], mk('T%d' % cur)], [tnk])
                                  yield
                              cur = nxt

                      def gen_seq(c):
                          par = c % 2
                          cg = blk * 4 + c
                          csl = slice(c * 128, (c + 1) * 128)
                          MP = lambda nm, hf: MTp[(nm, hf, par)]
                          MK = lambda nm, hf: ('M', nm, hf, par)
                          hb_cur = Hbf[cg % 2]
                          hk_cur = 'Hbf%d' % (cg % 2)
                          tt('dve', hb_cur[:].rearrange("p (j v) -> p j v", j=4), Hst[:].rearrange("p (j v) -> p j v", j=4),
                             bc_in(emid[:, :, c], 64), ALU.mult, ['Hst'] + [('emid', j) for j in range(4)], [hk_cur])
                          b_rhs = nbB()
                          for h in range(8):
                              jj, base, hf, i = h // 2, (h % 2) * 64, h % 2, h // 2
                              osl = slice(h * 64, (h + 1) * 64)
                              mm(ps[b_rhs][:, osl], MP('AkvT', hf)[:, i * 128:(i + 1) * 128], V_tok[:, c, osl], True, False, [MK('AkvT', hf), ('Vtok', c)], [psk[b_rhs]])
                              mm(ps[b_rhs][:, osl], QKT[base:base + 64, jj, csl], hb_cur[base:base + 64, jj * 64:(jj + 1) * 64], False, True, QK_K + [hk_cur], [psk[b_rhs]])
                          cp('act', RHS_sb[:], ps[b_rhs][:], [psk[b_rhs]], ['RHSsb'])
                          yield
                          b_u = nbB()
                          for h in range(8):
                              hf, i = h % 2, h // 2
                              osl = slice(h * 64, (h + 1) * 64)
                              mm(ps[b_u][:, osl], MP('Tf', hf)[:, i * 128:(i + 1) * 128], RHS_sb[:, osl], True, True, [MK('Tf', hf), 'RHSsb'], [psk[b_u]])
                          cp('dve', U_sb[:], ps[b_u][:], [psk[b_u]], ['Usb'])
                          yield
                          b_y = nbB()
                          for h in range(8):
                              jj, base, hf, i = h // 2, (h % 2) * 64, h % 2, h // 2
                              osl = slice(h * 64, (h + 1) * 64)
                              mm(ps[b_y][:, osl], RtT[base:base + 64, jj, csl], hb_cur[base:base + 64, jj * 64:(jj + 1) * 64], True, False, RT_K + [hk_cur], [psk[b_y]])
                              mm(ps[b_y][:, osl], MP('ArkT', hf)[:, i * 128:(i + 1) * 128], V_tok[:, c, osl], False, False, [MK('ArkT', hf), ('Vtok', c)], [psk[b_y]])
                              mm(ps[b_y][:, osl], MP('nArbT', hf)[:, i * 128:(i + 1) * 128], U_sb[:, osl], False, True, [MK('nArbT', hf), 'Usb'], [psk[b_y]])
                          b_h = nbB()
                          for h in range(8):
                              jj, base = h // 2, (h % 2) * 64
                              osl = slice(h * 64, (h + 1) * 64)
                              mm(ps[b_h][base:base + 64, jj * 64:(jj + 1) * 64], Kt_tok[:, c, osl], V_tok[:, c, osl], True, False, [('Kttok', c), ('Vtok', c)], [psk[b_h]])
                              mm(ps[b_h][base:base + 64, jj * 64:(jj + 1) * 64], nBt_tok[:, c, osl], U_sb[:, osl], False, True, [('nBttok', c), 'Usb'], [psk[b_h]])
                          tt('dve', Htmp[:].rearrange("p (j v) -> p j v", j=4), ps[b_h][:, 0:256].rearrange("p (j v) -> p j v", j=4),
                             bc_in(eLp[:, :, c], 64), ALU.mult, [psk[b_h]] + [('eLp', j) for j in range(4)], ['Htmp'])
                          tt('dve', Hst[:].rearrange("p (j v) -> p j v", j=4), Hst[:].rearrange("p (j v) -> p j v", j=4),
                             bc_in(eL[:, :, c], 64), ALU.mult, ['Hst'] + [('eL', j) for j in range(4)], ['Hst'])
                          tt('dve', Hst[:], Hst[:], Htmp[:], ALU.add, ['Hst', 'Htmp'], ['Hst'])
                          cp('act', y_sb[:], ps[b_y][:], [psk[b_y]], ['ysb'])
                          yield
                          S.op('dve', lambda en: en.tensor_reduce(out=st1[:], in_=y_sb[:].rearrange("p (h v) -> p h v", h=8), axis=AX.X, op=ALU.add), reads=['ysb'], writes=['st1'])
                          act(y3[:], y_sb[:], AF.Square, ['ysb'], ['y3'])
                          S.op('dve', lambda en: en.tensor_reduce(out=st2[:], in_=y3[:].rearrange("p (h v) -> p h v", h=8), axis=AX.X, op=ALU.add), reads=['y3'], writes=['st2'])
                          ts('dve', st1[:], st1[:], 1.0 / 64, None, ALU.mult, None, ['st1'], ['st1'])
                          tt('dve', rs_in[:, 0:8], st1[:], st1[:], ALU.mult, ['st1', 'rs_out'], ['rs_in'])
                          stt('dve', rs_in[:, 0:8], st2[:], 1.0 / 64, rs_in[:, 0:8], ALU.mult, ALU.subtract, ['st2', 'rs_in'], ['rs_in'])
                          ts('dve', rs_in[:, 0:8], rs_in[:, 0:8], 64e-5, None, ALU.add, None, ['rs_in'], ['rs_in'])
                          rsqrt(8, [])
                          tt('dve', y2[:].rearrange("p (h v) -> p h v", h=8), y_sb[:].rearrange("p (h v) -> p h v", h=8), bc_in(st1[:], 64), ALU.subtract, ['ysb', 'st1'], [('y2', j) for j in range(4)])
                          tt('dve', yn_bf[:].rearrange("p (h v) -> p h v", h=8), y2[:].rearrange("p (h v) -> p h v", h=8), bc_in(rs_out[:, 0:8], 64), ALU.mult, [('y2', j) for j in range(4)] + ['rs_out'], ['ynbf'])
                          yield
                          b_t = nbB()
                          psb = ps[b_t][:].bitcast(BF16)
                          for j in range(4):
                              tr(psb[:, j * 128:(j + 1) * 128], yn_bf[:, j * 128:(j + 1) * 128], ident_bf[:], ['ynbf', 'ident_bf'], [psk[b_t]])
                          b_g = nbB()
                          for j in range(4):
                              mm(ps[b_g][:, j * 128:(j + 1) * 128], g2_bf[:, j * 128:(j + 1) * 128], sgx_bf[:, csl], True, True, ['g2_bf', 'sgxbf'], [psk[b_g]])
                          b_b = nbB()
                          for j in range(4):
                              mm(ps[b_b][:, j * 128:(j + 1) * 128], bones[:], rkT_bf[:, j, csl], True, True, ['bones', ('rkT', j)], [psk[b_b]])
                          v4 = lambda ap: ap.rearrange("p (j t) -> p j t", j=4)
                          for j in range(4):
                              act(y2[:, j * 128:(j + 1) * 128], psb[:, j * 128:(j + 1) * 128], AF.Identity, [psk[b_t], 'pp'], [('y2', j)],
                                  bias=pp[:, P_GNB + j:P_GNB + j + 1], scale=pp[:, P_GNW + j:P_GNW + j + 1])
                          tt('dve', v4(y3[:]), v4(ps[b_b][:]), vT_bf[:, :, csl], ALU.mult, [psk[b_b]] + [('vTbf', j) for j in range(4)], ['y3'])
                          yield
                          tt('dve', y3[:], y3[:], y2[:], ALU.add, ['y3'] + [('y2', j) for j in range(4)], ['y3'])
                          tt('dve', yrT[:, :, csl], v4(y3[:]), v4(ps[b_g][:]), ALU.mult, ['y3', psk[b_g]], [('yrT', c)])

                      for c in range(5):
                          gl = []
                          if c < 4:
                              gl.append(gen_mat(c))
                          if c >= 1:
                              gl.append(gen_seq(c - 1))
                          while gl:
                              for g_ in list(gl):
                                  try:
                                      next(g_)
                                  except StopIteration:
                                      gl.remove(g_)
                              stepM()
                      while gM[0] is not None:
                          stepM()
                      S.barrier()
                  S.barrier()
                  smm.close()
              sr.close()
              YR = [('yrT', c) for c in range(4)]
              chk('R%d' % blk)
              if blk == 0:
                  dbg("yrT", yrT[:, :, :], [128, 4, TB], YR)
                  dbg("Hst", Hst[:], [128, 256], ['Hst'])

              YM = [('ymT', c) for c in range(4)]
              chk('M%d' % blk)
              if blk == 0:
                  dbg("ymT", ymT[:, :, :], [128, 4, TB], YM)

              with contextlib.ExitStack() as sf:
                  mrgT = SB(sf, "mrgT_%d" % blk, [128, 8, TB], BF16)
                  x1 = SB(sf, "x1_%d" % blk, [128, 4, D])
                  xre = [SB(sf, "xre%d_%d" % (i, blk), [128, D]) for i in range(2)]
                  thr2 = [SB(sf, "thr%d_%d" % (i, blk), [128, TB]) for i in range(2)]
                  thm2 = [SB(sf, "thm%d_%d" % (i, blk), [128, TB]) for i in range(2)]
                  uu2 = [SB(sf, "uu%d_%d" % (i, blk), [128, TB]) for i in range(2)]
                  ww2 = [SB(sf, "ww%d_%d" % (i, blk), [128, TB]) for i in range(2)]
                  h1T = SB(sf, "h1T_%d" % blk, [128, 32, TB], BF16)
                  rl = [SB(sf, "rl%d_%d" % (i, blk), [128, TB]) for i in range(2)]
                  xs_bf = [SB(sf, "xsbfF%d_%d" % (i, blk), [128, D], BF16) for i in range(2)]
                  junk = SB(sf, "junkF_%d" % blk, [128, D], BF16)
                  ssF = SB(sf, "ssF_%d" % blk, [128, 4])
                  yo = [SB(sf, "yo%d_%d" % (i, blk), [128, D]) for i in range(2)]
                  for hh in range(2):
                      wgr, kgr = w_get('gr%d' % hh)
                      wgm, kgm = w_get('gm%d' % hh)
                      wru, kru = w_get('ru%d' % hh)
                      wmu, kmu = w_get('mu%d' % hh)
                      for p4 in range(4):
                          pc = hh * 4 + p4
                          csl_ = slice(p4 * 128, (p4 + 1) * 128)
                          b1, b2, b3, b4 = nb(), nb(), nb(), nb()
                          for kc in range(8):
                              mm(ps[b1][:], wgr[:, kc, csl_], xnT[:, kc, :], kc == 0, kc == 7, [kgr] + XN, [psk[b1]])
                          for kc in range(8):
                              mm(ps[b2][:], wgm[:, kc, csl_], xnT[:, kc, :], kc == 0, kc == 7, [kgm] + XN, [psk[b2]])
                          for kc in range(4):
                              mm(ps[b3][:], wru[:, kc, csl_], yrT[:, kc, :], kc == 0, kc == 3, [kru] + YR, [psk[b3]])
                          for kc in range(4):
                              mm(ps[b4][:], wmu[:, kc, csl_], ymT[:, kc, :], kc == 0, kc == 3, [kmu] + YM, [psk[b4]])
                          q2 = pc % 2
                          thr, thm, uu, ww = thr2[q2], thm2[q2], uu2[q2], ww2[q2]
                          act(thr[:], ps[b1][:], AF.Tanh, [psk[b1], 'pq'], [('thr', q2)], bias=pq[:, Q_HGTB + pc:Q_HGTB + pc + 1], scale=0.5)
                          act(thm[:], ps[b2][:], AF.Tanh, [psk[b2], 'pq'], [('thm', q2)], bias=pq[:, Q_HGTB + 8 + pc:Q_HGTB + 8 + pc + 1], scale=0.5)
                          stt('dve', uu[:], thr[:], 1.0, ps[b3][:], ALU.add, ALU.mult, [('thr', q2), psk[b3]], [('uu', q2)])
                          stt('dve', ww[:], thm[:], 1.0, ps[b4][:], ALU.add, ALU.mult, [('thm', q2), psk[b4]], [('ww', q2)])
                          tt('dve', mrgT[:, pc, :], uu[:], ww[:], ALU.add, [('uu', q2), ('ww', q2)], [('mrgT', pc)])
                      w_done(4)
                  MG = [('mrgT', pc) for pc in range(8)]
                  wo0, ko0 = w_get('wo0')
                  wo1, ko1 = w_get('wo1')
                  for tt_ in range(4):
                      xr_ = xre[tt_ % 2]
                      xrk = 'xre%d' % (tt_ % 2)
                      S.dma('sp', xr_[:], x_d[t0 + tt_ * 128:t0 + (tt_ + 1) * 128, :], writes=[xrk])
                      for hh, (wo_, ko_) in enumerate(((wo0, ko0), (wo1, ko1))):
                          bi = nb()
                          for kc in range(8):
                              mm(ps[bi][:], mrgT[:, kc, tt_ * 128:(tt_ + 1) * 128], wo_[:, kc, :], kc == 0, kc == 7, [ko_] + MG, [psk[bi]])
                          stt('dve', x1[:, tt_, hh * 512:(hh + 1) * 512], ps[bi][:], 0.5, xr_[:, hh * 512:(hh + 1) * 512], ALU.mult, ALU.add, [psk[bi], xrk], [('x1', tt_, hh)])
                  w_done(2)
                  S.op('pool', lambda en: en.memset(ssF[:], 0.0), writes=[('ssF', i) for i in range(4)])
                  for tt_ in range(4):
                      act(junk[:], x1[:, tt_, :], AF.Square, [('x1', tt_, 0), ('x1', tt_, 1)], ['junkF', ('ssF', tt_)], accum=ssF[:, tt_:tt_ + 1])
                  ts('dve', rs_in[:, 0:4], ssF[:], 1.0 / D, 1e-6, ALU.mult, ALU.add, [('ssF', i) for i in range(4)] + ['rs_out'], ['rs_in'])
                  rsqrt(4, [])
                  for tt_ in range(4):
                      xb_ = xs_bf[tt_ % 2]
                      xk = 'xsbfF%d' % (tt_ % 2)
                      act(xb_[:], x1[:, tt_, :], AF.Copy, [('x1', tt_, 0), ('x1', tt_, 1), 'rs_out'], [xk], scale=rs_out[:, tt_:tt_ + 1])
                      bi = nb()
                      psb = ps[bi][:].bitcast(BF16)
                      for kc in range(8):
                          tr(psb[:, kc * 128:(kc + 1) * 128], xb_[:, kc * 128:(kc + 1) * 128], ident_bf[:], [xk, 'ident_bf'], [psk[bi]])
                      tt('dve', xnT[:, :, tt_ * 128:(tt_ + 1) * 128], psb.rearrange("p (k t) -> p k t", k=8),
                         bc_in(pp[:, P_GFFN:P_GFFN + 8], 128), ALU.mult, [psk[bi], 'pp'], [('xnT', tt_)])
                  pig = (blk == 0 and with_sample)
                  if pig:
                      BUP, BDN = 4, [5, 6]
                      reserved.update([4, 5, 6])
                      rl_e = SB(sf, "rl_e", [128, 512])
                      h1T_e = SB(sf, "h1T_e", [128, 512], BF16)
                      junk_e = SB(sf, "junk_e", [16, D], BF16)
                      ss_e = SB(sf, "ss_e", [16, 1])
                      y_e = SB(sf, "y_e", [16, D])
                  for g in range(8):
                      w1g, k1g = w_get('w1_%d' % g)
                      if pig:
                          for p4 in range(4):
                              pc = g * 4 + p4
                              for kc in range(8):
                                  mm(ps[BUP][:, pc * 16:(pc + 1) * 16], w1g[:, kc, p4 * 128:(p4 + 1) * 128], hnT_sP[:, kc, :], kc == 0, kc == 7, [k1g, 'hnT_sP'], [psk[BUP]])
                      for p4 in range(4):
                          pc = g * 4 + p4
                          bi = nb()
                          for kc in range(8):
                              mm(ps[bi][:], w1g[:, kc, p4 * 128:(p4 + 1) * 128], xnT[:, kc, :], kc == 0, kc == 7, [k1g] + XN, [psk[bi]])
                          r_ = rl[pc % 2]
                          rk_ = 'rl%d' % (pc % 2)
                          act(r_[:], ps[bi][:], AF.Relu, [psk[bi]], [rk_])
                          act(h1T[:, pc, :], r_[:], AF.Square, [rk_], [('h1T', pc)])
                      w_done()
                  if pig:
                      act(rl_e[:], ps[BUP][:], AF.Relu, [psk[BUP]], ['rl_e'])
                      act(h1T_e[:], rl_e[:], AF.Square, ['rl_e'], ['h1T_e'])
                  for hh in range(2):
                      banks = [0, 1, 2, 3]
                      for kg in range(4):
                          w2g, k2g = w_get('w2_%d_%d' % (hh, kg))
                          if pig:
                              for kc in range(8):
                                  kk_ = kg * 8 + kc
                                  mm(ps[BDN[hh]][0:16, :], h1T_e[:, kk_ * 16:(kk_ + 1) * 16], w2g[:, kc, :], kk_ == 0, kk_ == 31, [k2g, 'h1T_e'], [psk[BDN[hh]]])
                          for tt_ in range(4):
                              for kc in range(8):
                                  kk_ = kg * 8 + kc
                                  mm(ps[banks[tt_]][:], h1T[:, kk_, tt_ * 128:(tt_ + 1) * 128], w2g[:, kc, :], kk_ == 0, kk_ == 31, [k2g, ('h1T', kk_)], [psk[banks[tt_]]])
                          w_done()
                      for tt_ in range(4):
                          tt('dve', x1[:, tt_, hh * 512:(hh + 1) * 512], x1[:, tt_, hh * 512:(hh + 1) * 512], ps[banks[tt_]][:], ALU.add,
                             [('x1', tt_, hh), psk[banks[tt_]]], [('x1', tt_, hh)])
                  bank_rr[0] = 4
                  if pig:
                      for hh in range(2):
                          tt('dve', x1_sP[:, hh * 512:(hh + 1) * 512], x1_sP[:, hh * 512:(hh + 1) * 512], ps[BDN[hh]][0:16, :], ALU.add, [('x1_s', hh), psk[BDN[hh]]], [('x1_s', hh)])
                      reserved.difference_update([4, 5, 6])
                      S.op('pool', lambda en: en.memset(ss_e[:], 0.0), writes=['ss_e'])
                      act(junk_e[:], x1_sP[:], AF.Square, [('x1_s', 0), ('x1_s', 1)], ['junk_e', 'ss_e'], accum=ss_e[:, 0:1])
                      ts('dve', rs_in[0:16, 0:1], ss_e[:], 1.0 / D, 1e-6, ALU.mult, ALU.add, ['ss_e', 'rs_out'], ['rs_in'])
                      rsqrt(1, [])
                      stt('dve', y_e[:], x1_sP[:], rs_out[0:16, 0:1], gfin_bc[0:16, :], ALU.mult, ALU.mult, [('x1_s', 0), ('x1_s', 1), 'rs_out', 'gfin_bc'], ['y_e'])
                      S.dma('sp', ys_d[:, :], y_e[:], reads=['y_e'], writes=['o_ys'])
                  S.op('pool', lambda en: en.memset(ssF[:], 0.0), reads=['rs_in'], writes=[('ssF', i) for i in range(4)])
                  for tt_ in range(4):
                      act(junk[:], x1[:, tt_, :], AF.Square, [('x1', tt_, 0), ('x1', tt_, 1)], ['junkF', ('ssF', tt_)], accum=ssF[:, tt_:tt_ + 1])
                  ts('dve', rs_in[:, 0:4], ssF[:], 1.0 / D, 1e-6, ALU.mult, ALU.add, [('ssF', i) for i in range(4)] + ['rs_out'], ['rs_in'])
                  rsqrt(4, [])
                  for tt_ in range(4):
                      yo_ = yo[tt_ % 2]
                      yk = 'yo%d' % (tt_ % 2)
                      stt('dve', yo_[:], x1[:, tt_, :], rs_out[:, tt_:tt_ + 1], gfin_bc[:], ALU.mult, ALU.mult, [('x1', tt_, 0), ('x1', tt_, 1), 'rs_out', 'gfin_bc'], [yk])
                      S.dma('sp', y_d[t0 + tt_ * 128:t0 + (tt_ + 1) * 128, :], yo_[:], reads=[yk], writes=[('o_y', blk, tt_)])
                  S.barrier()

        except _Stop:
            pass
        if nofinal:
            S.finish('sp')
        with contextlib.ExitStack() as se:
            if nofinal:
                se.close()
                return nc, dbg_outs
            So = SB(se, "So", [64, 8, 64])
            Co = SB(se, "Co", [128, 4, 129])
            rM = SB(se, "rM", [4, 1])
            dgM = SB(se, "dgM", [4, 4])
            rMb = SB(se, "rMb", [128, 4])
            mo = SB(se, "mo", [4, 1])
            bis = [nb(), nb()]
            for par in range(2):
                for jj in range(4):
                    h, base = 2 * jj + par, par * 64
                    tr(ps[bis[par]][0:64, jj * 64:(jj + 1) * 64], Hst[base:base + 64, jj * 64:(jj + 1) * 64], ident_f[base:base + 64, base:base + 64], ['Hst', 'ident_f'], [psk[bis[par]]])
            for par in range(2):
                cp('dve', So[:].rearrange("p (j q) k -> p j q k", q=2)[:, :, par, :], ps[bis[par]][0:64, 0:256].rearrange("p (j k) -> p j k", j=4), [psk[bis[par]]], [('So', par)])
            S.dma('sp', wkv_d.rearrange("h v k -> v h k"), So[:], reads=[('So', 0), ('So', 1)], writes=['o_wkv'])
            S.op('dve', lambda en: en.reciprocal(out=rM[:], in_=Mst[:]), reads=['Mst'], writes=['rM'])
            ts('dve', dgM[:], ident_f[0:4, 0:4], rM[:, 0:1], None, ALU.mult, None, ['rM', 'ident_f'], ['dgM'])
            b2 = nb()
            mm(ps[b2][:, 0:4], ones_f[0:4, :], dgM[:], True, True, ['ones_f', 'dgM'], [psk[b2]])
            cp('dve', rMb[:], ps[b2][:, 0:4], [psk[b2]], ['rMb'])
            tt('dve', Co[:], Caug[:], bc_in(rMb[:], 129), ALU.mult, ['Caug', 'rMb'], ['Co'])
            S.dma('sp', C_d.rearrange("h d e -> d h e"), Co[:, :, 0:128], reads=['Co'], writes=['o_C'])
            S.dma('sp', n_d.rearrange("h d -> d h"), Co[:, :, 128], reads=['Co'], writes=['o_n'])
            act(mo[:], Mst[:], AF.Ln, ['Mst'], ['mo'])
            S.dma('sp', m_d[:, :], mo[:], reads=['mo'], writes=['o_m'])
            S.finish('sp')
        print("sched: ninst", S.ninst, "nwaits", S.nwaits, {e: S.cnt[e] for e in S.cnt})
    return nc, dbg_outs


def _prep_shared(inp):
    f = lambda a: np.ascontiguousarray(np.asarray(a, dtype=np.float32))
    w_in = f(inp['w_in'][0])
    sh = {}
    sh['w_in_l'] = f(w_in[:, PERM].reshape(8, 128, 5384).transpose(1, 0, 2))
    sh['r_up_l'] = f(inp['r_up'][0].reshape(4, 128, 1024).transpose(1, 0, 2))
    sh['m_up_l'] = f(inp['m_up'][0].reshape(4, 128, 1024).transpose(1, 0, 2))
    sh['w_out_l'] = f(inp['w_out'][0].reshape(8, 128, 1024).transpose(1, 0, 2))
    sh['w1_l'] = f(inp['ffn_w1'][0].reshape(8, 128, 4096).transpose(1, 0, 2))
    sh['w2_l'] = f(inp['ffn_w2'][0].reshape(32, 128, 1024).transpose(1, 0, 2))
    sh['w2a2'] = f(np.concatenate([inp['r_w2'][0], inp['r_a2'][0]], 0))
    sh['g2'] = f(inp['r_g2'][0])
    sh['wq_l'] = f(inp['m_wq'][0].transpose(1, 0, 2))
    sh['wk_l'] = f(inp['m_wk'][0].transpose(1, 0, 2))
    pp = np.zeros((128, NPC), np.float32)
    mu = np.asarray(inp['r_mu'][0], np.float32)
    for pi, s in enumerate(PIECE_ORIG):
        pp[:, P_MU + pi] = mu[s:s + 128]
    col = lambda v, n: np.asarray(v, np.float32).reshape(n, 128).T
    pp[:, P_W0:P_W0 + 4] = col(inp['r_w0'][0], 4)
    pp[:, P_A0:P_A0 + 4] = col(inp['r_a0'][0], 4)
    pp[:, P_RKK:P_RKK + 4] = col(inp['r_kk'][0], 4)
    pp[:, P_KA:P_KA + 4] = col(inp['r_ka'][0], 4)
    pp[:, P_RRK:P_RRK + 4] = col(inp['r_rk'][0].reshape(512), 4)
    pp[:, P_GNW:P_GNW + 4] = col(inp['r_gn_w'][0], 4)
    pp[:, P_GNB:P_GNB + 4] = col(inp['r_gn_b'][0], 4)
    for tap in range(4):
        pp[:, P_CW + tap * 4:P_CW + tap * 4 + 4] = col(inp['m_conv_w'][0][tap], 4)
    pp[:, P_CB:P_CB + 4] = col(inp['m_conv_b'][0], 4)
    pp[:, P_MGN:P_MGN + 4] = col(inp['m_gn_w'][0], 4)
    pp[:, P_MSK:P_MSK + 4] = col(inp['m_skip'][0], 4)
    pp[:, P_GTB:P_GTB + 16] = col(inp['gate_b'][0], 16)
    pp[:, P_GMIX:P_GMIX + 8] = col(inp['norm_mix_g'][0], 8)
    pp[:, P_GFFN:P_GFFN + 8] = col(inp['norm_ffn_g'][0], 8)
    sh['ppack'] = pp
    sh['gbias_t'] = f(np.concatenate([inp['m_i_b'][0], inp['m_f_b'][0]])[None, :])
    sh['gbias_f'] = f(np.stack([inp['m_i_b'][0], inp['m_f_b'][0]], 1))
    sh['g_final'] = f(np.asarray(inp['norm_final_g'])[None, :])
    sh['g_mix_row'] = f(inp['norm_mix_g'][0][None, :])
    v = lambda a: np.asarray(a, np.float32).reshape(-1)
    mu_perm = np.concatenate([mu[s_:s_ + 128] for s_ in PIECE_ORIG])
    sh['prow1'] = f(np.concatenate([mu_perm, v(inp['r_w0'][0]), v(inp['r_a0'][0]), v(inp['r_kk'][0]), v(inp['r_ka'][0]),
                                    v(inp['r_rk'][0]), v(inp['r_gn_w'][0]), v(inp['r_gn_b'][0])])[None, :])
    sh['prow2'] = f(np.concatenate([v(inp['m_conv_w'][0]), v(inp['m_conv_b'][0]), v(inp['m_gn_w'][0]), v(inp['m_skip'][0])])[None, :])
    sh['prow3'] = f(np.concatenate([v(inp['gate_b'][0]), v(inp['norm_mix_g'][0]), v(inp['norm_ffn_g'][0])])[None, :])
    return sh


_CACHE = {}


def kernel(**inputs):
    debug = bool(inputs.pop('_debug', False))
    stop = inputs.pop('_stop', None)
    key = ('nc', debug, stop)
    if key not in _CACHE:
        _CACHE[key] = build_program(debug=debug, stop=stop)
    nc, dbg_outs = _CACHE[key]
    sh = _prep_shared(inputs)
    xp = np.asarray(inputs['x_prompt'], np.float32)
    f32 = lambda a: np.ascontiguousarray(np.asarray(a, np.float32))
    xs_all = f32(inputs['x_sample'])[:, 0, :]
    sh0_all = f32(inputs['state_rwkv_shift'])[0]
    wkv_all = f32(inputs['state_rwkv_wkv'])[0]
    C_all = f32(inputs['state_mlstm_C'])[0]
    n_all = f32(inputs['state_mlstm_n'])[0]
    m_all = f32(inputs['state_mlstm_m'])[0]
    cv_all = f32(inputs['state_mlstm_conv'])[0]
    in_maps = []
    for c in range(NCORES):
        m = dict(sh)
        m['x'] = np.ascontiguousarray(xp[c])
        sl = slice(16 * c, 16 * (c + 1))
        m['xs'] = np.ascontiguousarray(xs_all[sl])
        m['sh0'] = np.ascontiguousarray(sh0_all[sl])
        m['s_wkv0'] = np.ascontiguousarray(wkv_all[sl].reshape(128, 4096))
        m['s_C0'] = np.ascontiguousarray(C_all[sl].reshape(64, 128, 128))
        m['s_n0'] = np.ascontiguousarray(n_all[sl].reshape(64, 128))
        m['s_m0'] = np.ascontiguousarray(m_all[sl])
        m['s_conv0'] = np.ascontiguousarray(cv_all[sl])
        in_maps.append(m)
    res = run_bass_kernel_spmd(nc, in_maps, core_ids=list(range(NCORES)))
    R = res.results
    B = NCORES
    y_prompt = np.stack([R[c]['y'] for c in range(B)], 0)
    p_shift = np.stack([R[c]['p_shift'].reshape(D) for c in range(B)], 0)[None]
    p_wkv = np.stack([R[c]['p_wkv'] for c in range(B)], 0)[None]
    p_C = np.stack([R[c]['p_C'] for c in range(B)], 0)[None]
    p_n = np.stack([R[c]['p_n'] for c in range(B)], 0)[None]
    p_m = np.stack([R[c]['p_m'].reshape(4) for c in range(B)], 0)[None]
    p_conv = np.stack([R[c]['p_conv'] for c in range(B)], 0)[None]
    cat = lambda k: np.concatenate([R[c][k] for c in range(B)], 0)
    y_sample = cat('ys').reshape(128, 1, D)
    s_shift = cat('s_shift').reshape(1, 128, D)
    s_wkv = cat('s_wkv').reshape(1, 128, 8, 64, 64)
    s_C = cat('s_C').reshape(1, 128, 4, 128, 128)
    s_n = cat('s_n').reshape(1, 128, 4, 128)
    s_m = cat('s_m').reshape(1, 128, 4)
    s_conv = cat('s_conv').reshape(1, 128, 3, 512)
    out = (y_prompt, y_sample, p_shift, p_wkv, p_C, p_n, p_m, p_conv, s_shift, s_wkv, s_C, s_n, s_m, s_conv)
    if debug:
        return out, {k: [R[c]["dbg_" + k] for c in range(B)] for k in dbg_outs}
    return out
```

```python
import contextlib
import math
import numpy as np
import concourse.bass as bass
import concourse.mybir as mybir
from concourse.bass_utils import run_bass_kernel_spmd

F32 = mybir.dt.float32
BF16 = mybir.dt.bfloat16
I32 = mybir.dt.int32
AF = mybir.ActivationFunctionType
ALU = mybir.AluOpType
AX = mybir.AxisListType

NCORES = 8
T = 2048
TB = 512
NBLK = T // TB
D = 1024
L = 128
CC = math.exp(-0.5)
NDMA = 24
NSLOT = 4

PIECE_ORIG = [1536, 1664] + [base + j * 128 for j in range(4) for base in (0, 512, 1024)]
PERM = np.concatenate([np.arange(s, s + 128) for s in PIECE_ORIG] + [np.arange(1792, 5384)])
C_LAT = 0
C_R = 256
C_XM = 1792
C_MV = 2304
C_O = 2816
C_GT = 3328
C_GR = 3336
C_GM = 4360

P_MU, P_W0, P_A0, P_RKK, P_KA, P_RRK, P_GNW, P_GNB, P_CW, P_CB, P_MGN, P_MSK, P_GTB, P_GMIX, P_GFFN = \
    0, 14, 18, 22, 26, 30, 34, 38, 42, 58, 62, 66, 70, 86, 94
NPC = 102
Q_OMU, Q_HW0, Q_HA0, Q_OMKA, Q_HGTB = 0, 14, 18, 22, 26
NQC = 42


class Sched:
    def __init__(self, nc, stack):
        self.nc = nc
        self.eng = {'pe': nc.tensor, 'act': nc.scalar, 'dve': nc.vector, 'pool': nc.gpsimd, 'sp': nc.sync}
        self.sem = {e: stack.enter_context(nc.semaphore("s_" + e)) for e in self.eng}
        self.cnt = {e: 0 for e in self.eng}
        self.seen = {e: {} for e in self.eng}
        self.dsem = [stack.enter_context(nc.semaphore("s_dma%d" % i)) for i in range(NDMA)]
        self.dval = [0] * NDMA
        self.drr = {'sp': 0, 'pool': 0, 'act': 0}
        self.dpool = {'sp': list(range(0, 16)), 'act': list(range(0, 16)), 'pool': list(range(16, NDMA))}
        self.last_w = {}
        self.readers = {}
        self.nwaits = 0
        self.ninst = 0
        self.defer = True
        self.pend = []

    def _need(self, e, tok):
        semk, val, sem = tok
        if self.seen[e].get(semk, 0) >= val:
            return
        self.eng[e].wait_ge(sem, val)
        self.nwaits += 1
        self.seen[e][semk] = val

    def _deps(self, e, reads, writes):
        best = {}

        def add(t):
            if t is not None and (t[0] not in best or best[t[0]][1] < t[1]):
                best[t[0]] = t
        for k in reads:
            add(self.last_w.get(k))
        for k in writes:
            add(self.last_w.get(k))
            for t in self.readers.get(k, ()):
                add(t)
        for t in best.values():
            if e == 'pe' and t[0] == 'pe':
                continue
            self._need(e, t)

    def _record(self, tok, reads, writes):
        for k in reads:
            lst = self.readers.setdefault(k, [])
            for i, t in enumerate(lst):
                if t[0] == tok[0]:
                    lst[i] = tok
                    break
            else:
                lst.append(tok)
        for k in writes:
            self.last_w[k] = tok
            self.readers[k] = []

    def op(self, e, fn, reads=(), writes=(), cost=0.3):
        if self.defer:
            self.pend.append(('op', e, fn, tuple(reads), tuple(writes), cost))
            return None
        return self._op_now(e, fn, reads, writes)

    def dma(self, q, out, in_, reads=(), writes=(), cost=None):
        if cost is None:
            n_ = 1
            for d_ in out.shape:
                n_ *= d_
            cost = 2.0 + n_ * 4 / 350e3
        if self.defer:
            self.pend.append(('dma', q, (out, in_), tuple(reads), tuple(writes), cost))
            return None
        return self._dma_now(q, out, in_, reads, writes)

    def flush(self):
        import heapq
        ops = self.pend
        self.pend = []
        n = len(ops)
        if n == 0:
            return
        LAT = 0.2
        last_w, readers = {}, {}
        deps = [set() for _ in range(n)]
        for i, o in enumerate(ops):
            rd, wr = o[3], o[4]
            for k in rd:
                j = last_w.get(k)
                if j is not None:
                    deps[i].add(j)
            for k in wr:
                j = last_w.get(k)
                if j is not None:
                    deps[i].add(j)
                for r in readers.get(k, ()):
                    deps[i].add(r)
            for k in rd:
                readers.setdefault(k, []).append(i)
            for k in wr:
                last_w[k] = i
                readers[k] = []
            deps[i].discard(i)
        succ = [[] for _ in range(n)]
        indeg = [0] * n
        for i in range(n):
            indeg[i] = len(deps[i])
            for j in deps[i]:
                succ[j].append(i)
        blev = [0.0] * n
        for i in range(n - 1, -1, -1):
            b_ = 0.0
            for sc in succ[i]:
                lat_ = 0.0 if ops[sc][1] == ops[i][1] else LAT
                if blev[sc] + lat_ > b_:
                    b_ = blev[sc] + lat_
            blev[i] = ops[i][5] + b_
        engs = ('pe', 'act', 'dve', 'pool', 'sp')
        free_at = {e: 0.0 for e in engs}
        ready_t = [0.0] * n
        fin = [0.0] * n
        waiting = {e: [] for e in engs}
        avail = {e: [] for e in engs}
        for i in range(n):
            if indeg[i] == 0:
                heapq.heappush(waiting[ops[i][1]], (0.0, i))
        order = []
        done = 0
        while done < n:
            best = None
            for e in engs:
                w, a = waiting[e], avail[e]
                while w and w[0][0] <= free_at[e]:
                    rt, i = heapq.heappop(w)
                    heapq.heappush(a, (-blev[i], i))
                if a:
                    cand = (free_at[e], a[0][1], e, True)
                elif w:
                    cand = (w[0][0], w[0][1], e, False)
                else:
                    continue
                if best is None or cand[:2] < best[:2]:
                    best = cand
            st_, i, e, from_a = best
            if from_a:
                heapq.heappop(avail[e])
            else:
                heapq.heappop(waiting[e])
            start = max(free_at[e], ready_t[i])
            fin[i] = start + ops[i][5]
            free_at[e] = fin[i] if e != 'sp' and ops[i][0] != 'dma' else start + 0.1
            if ops[i][0] == 'dma':
                free_at[e] = start + (1.0 if e == 'pool' else 0.1)
            order.append(i)
            done += 1
            for sc in succ[i]:
                lat = 0.0 if (ops[sc][1] == e and ops[i][0] != 'dma') else LAT
                t = fin[i] + lat
                if t > ready_t[sc]:
                    ready_t[sc] = t
                indeg[sc] -= 1
                if indeg[sc] == 0:
                    heapq.heappush(waiting[ops[sc][1]], (ready_t[sc], sc))
        for i in order:
            o = ops[i]
            if o[0] == 'op':
                self._op_now(o[1], o[2], o[3], o[4])
            else:
                self._dma_now(o[1], o[2][0], o[2][1], o[3], o[4])

    def _op_now(self, e, fn, reads=(), writes=()):
        self._deps(e, reads, writes)
        inst = fn(self.eng[e])
        self.cnt[e] += 1
        inst.then_inc(self.sem[e], 1)
        tok = (e, self.cnt[e], self.sem[e])
        self._record(tok, reads, writes)
        self.ninst += 1
        return tok

    def _dma_now(self, q, out, in_, reads=(), writes=()):
        self._deps(q, reads, writes)
        pl = self.dpool[q]
        i = pl[self.drr[q] % len(pl)]
        self.drr[q] += 1
        semk = "d%d" % i
        if self.dval[i] > 0:
            self._need(q, (semk, self.dval[i], self.dsem[i]))
        inst = self.eng[q].dma_start(out=out, in_=in_)
        self.dval[i] += 16
        inst.then_inc(self.dsem[i], 16)
        tok = (semk, self.dval[i], self.dsem[i])
        self._record(tok, reads, writes)
        self.ninst += 1
        return tok

    def barrier(self):
        self.flush()
        for e in self.eng:
            for o in self.eng:
                if o != e and self.cnt[o] > 0:
                    self._need(e, (o, self.cnt[o], self.sem[o]))
            for i in range(NDMA):
                if self.dval[i] > 0:
                    self._need(e, ("d%d" % i, self.dval[i], self.dsem[i]))

    def finish(self, e='sp'):
        self.flush()
        for o in self.eng:
            if o != e and self.cnt[o] > 0:
                self._need(e, (o, self.cnt[o], self.sem[o]))
        for i in range(NDMA):
            if self.dval[i] > 0:
                self._need(e, ("d%d" % i, self.dval[i], self.dsem[i]))


class _Stop(Exception):
    pass


def build_program(debug=False, nblk=NBLK, stop=None, with_sample=True):
    nc = bass.Bass("TRN2", target_bir_lowering=False)
    dram_in = lambda name, shape: nc.dram_tensor(name, list(shape), F32, kind="ExternalInput").ap()
    dram_out = lambda name, shape: nc.dram_tensor(name, list(shape), F32, kind="ExternalOutput").ap()
    x_d = dram_in("x", [T, D])
    win_d = dram_in("w_in_l", [128, 8, 5384])
    rup_d = dram_in("r_up_l", [128, 4, 1024])
    mup_d = dram_in("m_up_l", [128, 4, 1024])
    wout_d = dram_in("w_out_l", [128, 8, 1024])
    w1_d = dram_in("w1_l", [128, 8, 4096])
    w2_d = dram_in("w2_l", [128, 32, 1024])
    w2a2_d = dram_in("w2a2", [128, 512])
    g2_d = dram_in("g2", [128, 512])
    wq_d = dram_in("wq_l", [128, 4, 128])
    wk_d = dram_in("wk_l", [128, 4, 128])
    pp_d = dram_in("ppack", [128, NPC])
    gbt_d = dram_in("gbias_t", [1, 8])
    gbf_d = dram_in("gbias_f", [4, 2])
    gfin_d = dram_in("g_final", [1, D])
    gmixrow_d = dram_in("g_mix_row", [1, D])

    y_d = dram_out("y", [T, D])
    shift_d = dram_out("p_shift", [1, D])
    wkv_d = dram_out("p_wkv", [8, 64, 64])
    C_d = dram_out("p_C", [4, 128, 128])
    n_d = dram_out("p_n", [4, 128])
    m_d = dram_out("p_m", [4, 1])
    conv_d = dram_out("p_conv", [3, 512])
    xs_d = dram_in("xs", [16, D])
    sh0_d = dram_in("sh0", [16, D])
    wkv0_d = dram_in("s_wkv0", [128, 4096])
    C0_d = dram_in("s_C0", [64, 128, 128])
    n0_d = dram_in("s_n0", [64, 128])
    m0_d = dram_in("s_m0", [16, 4])
    conv0_d = dram_in("s_conv0", [16, 3, 512])
    prow1_d = dram_in("prow1", [1, 5376])
    prow2_d = dram_in("prow2", [1, 3584])
    prow3_d = dram_in("prow3", [1, 4096])
    ys_d = dram_out("ys", [16, D])
    s_shift_d = dram_out("s_shift", [16, D])
    s_wkv_d = dram_out("s_wkv", [128, 4096])
    s_C_d = dram_out("s_C", [64, 128, 128])
    s_n_d = dram_out("s_n", [64, 128])
    s_m_d = dram_out("s_m", [16, 4])
    s_conv_d = dram_out("s_conv", [16, 3, 512])
    scr1 = nc.dram_tensor("scr1", [6, 16 * 512], F32).ap()
    scr2 = nc.dram_tensor("scr2", [128, 64], F32).ap()
    scrq = nc.dram_tensor("scrq", [64, 128], F32).ap()
    scrk = nc.dram_tensor("scrk", [64, 128], F32).ap()
    scrv = nc.dram_tensor("scrv", [64, 128], F32).ap()
    scrs = nc.dram_tensor("scrs", [64, 4], F32).ap()
    scrh = nc.dram_tensor("scrh", [64, 128], F32).ap()
    dbg_outs = {}

    with contextlib.ExitStack() as st:
        st.enter_context(nc.allow_non_contiguous_dma(reason="small strided state outputs"))
        S = Sched(nc, st)

        def SB(stack, name, shape, dt=F32):
            return stack.enter_context(nc.sbuf_tensor(name, list(shape), dt))

        def fsz(ap):
            n_ = 1
            for d_ in ap.shape[1:]:
                n_ *= d_
            return n_

        def tt(e, out, a, b, op, r, w):
            return S.op(e, lambda en: en.tensor_tensor(out=out, in0=a, in1=b, op=op), reads=r, writes=w, cost=0.12 + 0.00105 * fsz(out))

        def ts(e, out, a, s1, s2, op0, op1, r, w):
            if s2 is None:
                return S.op(e, lambda en: en.tensor_scalar(out=out, in0=a, scalar1=s1, scalar2=None, op0=op0), reads=r, writes=w, cost=0.12 + 0.00105 * fsz(out))
            return S.op(e, lambda en: en.tensor_scalar(out=out, in0=a, scalar1=s1, scalar2=s2, op0=op0, op1=op1), reads=r, writes=w, cost=0.12 + 0.00105 * fsz(out))

        def stt(e, out, a, s, b, op0, op1, r, w):
            return S.op(e, lambda en: en.scalar_tensor_tensor(out=out, in0=a, scalar=s, in1=b, op0=op0, op1=op1), reads=r, writes=w, cost=0.12 + 0.0022 * fsz(out))

        def act(out, in_, func, r, w, bias=None, scale=1.0, accum=None):
            def f(en):
                kw = {}
                if bias is not None:
                    kw['bias'] = bias
                if accum is not None:
                    kw['accum_out'] = accum
                return en.activation(out=out, in_=in_, func=func, scale=scale, **kw)
            return S.op('act', f, reads=r, writes=w, cost=0.22 + 0.00072 * fsz(out))

        def cp(e, out, in_, r, w):
            if e == 'act':
                return act(out, in_, AF.Copy, r, w)
            return S.op(e, lambda en: en.tensor_copy(out=out, in_=in_), reads=r, writes=w, cost=0.12 + 0.00105 * fsz(out))

        def fix05(out, in_, r, w, np_=128):
            return act(out, in_, AF.Identity, r + ['half_c'], w, bias=half_c[0:np_, 0:1], scale=0.5)

        def mm(out, lhsT, rhs, start, stop, r, w):
            return S.op('pe', lambda en: en.matmul(out, lhsT, rhs, start=start, stop=stop), reads=r, writes=w, cost=0.035 + 0.00042 * fsz(out))

        def tr(out, in_, ident, r, w):
            return S.op('pe', lambda en: en.transpose(out, in_, ident), reads=r, writes=w, cost=0.035 + 0.00042 * fsz(out))

        def dbg(name, ap, shape, keys):
            if not debug:
                return
            d = dram_out("dbg_" + name, shape)
            dbg_outs[name] = d
            S.dma('pool', d, ap, reads=keys)

        def bc_mid(ap2d, n):
            P_, X_ = ap2d.shape
            return ap2d.unsqueeze(1).to_broadcast([P_, n, X_])

        def bc_in(ap2d, n):
            P_, A_ = ap2d.shape
            return ap2d.unsqueeze(2).to_broadcast([P_, A_, n])

        ident_bf = SB(st, "ident_bf", [128, 128], BF16)
        ident_f = SB(st, "ident_f", [128, 128])
        bones = SB(st, "bones", [128, 128], BF16)
        su_f = SB(st, "su_f", [128, 128])
        ui_f = SB(st, "ui_f", [128, 128])
        sl_f = SB(st, "sl_f", [128, 128])
        ones_f = SB(st, "ones_f", [128, 128])
        zeros_f = SB(st, "zeros_f", [128, 128])
        pp = SB(st, "pp", [128, NPC])
        pq = SB(st, "pq", [128, NQC])
        w2a2_bf = SB(st, "w2a2_bf", [128, 512], BF16)
        g2_bf = SB(st, "g2_bf", [128, 512], BF16)
        wq_bf = SB(st, "wq_bf", [128, 4, 128], BF16)
        wk_bf = SB(st, "wk_bf", [128, 4, 128], BF16)
        gfin_bc = SB(st, "gfin_bc", [128, D])
        gbt = SB(st, "gbt", [128, 8])
        gbf = SB(st, "gbf", [4, 2])
        hfb = SB(st, "hfb", [4, 1])
        half_c = SB(st, "half_c", [128, 1])
        eps_c = SB(st, "eps_c", [128, 1])
        carry = SB(st, "carry", [128, 14])
        Hst = SB(st, "Hst", [128, 256])
        Hbf = [SB(st, "Hbf%d" % i, [128, 256], BF16) for i in range(2)]
        Caug = SB(st, "Caug", [128, 4, 129])
        Caug_bf = SB(st, "Caug_bf", [128, 4, 129], BF16)
        Mst = SB(st, "Mst", [4, 1])
        xmc = SB(st, "xmc", [128, 4, 3])
        wbuf = [SB(st, "wbuf%d" % i, [128, 4096], BF16) for i in range(NSLOT)]
        rs_in = SB(st, "rs_in", [128, 8])
        rs_out = SB(st, "rs_out", [128, 8])
        rs_t = SB(st, "rs_t", [128, 8])
        ps = [st.enter_context(nc.psum_tensor("ps%d" % i, [128, 512], F32)) for i in range(8)]
        psk = ["ps%d" % i for i in range(8)]
        bank_rr = [0]

        reserved = set()

        def nb():
            for _ in range(8):
                i = bank_rr[0]
                bank_rr[0] = (i + 1) % 8
                if i not in reserved:
                    return i
            raise RuntimeError("no psum bank")

        rsB_in = SB(st, "rsB_in", [128, 8])
        rsB_out = SB(st, "rsB_out", [128, 8])
        rsB_t = SB(st, "rsB_t", [128, 8])

        def rsqrt(n, rk, bgset=False):
            if bgset:
                r_in, r_out, r_t, k_in, k_out, k_t = rsB_in, rsB_out, rsB_t, 'rsB_in', 'rsB_out', 'rsB_t'
            else:
                r_in, r_out, r_t, k_in, k_out, k_t = rs_in, rs_out, rs_t, 'rs_in', 'rs_out', 'rs_t'
            xi = r_in[:, 0:n].bitcast(I32)
            ti = r_t[:, 0:n].bitcast(I32)
            oi = r_out[:, 0:n].bitcast(I32)
            S.op('dve', lambda en: en.tensor_single_scalar(out=ti, in_=xi, scalar=1, op=ALU.arith_shift_right), reads=[k_in] + rk, writes=[k_t])
            S.op('dve', lambda en: en.tensor_scalar(out=oi, in0=ti, scalar1=-1, scalar2=0x5f3759df, op0=ALU.mult, op1=ALU.add), reads=[k_t], writes=[k_out])
            for _ in range(2):
                tt('dve', r_t[:, 0:n], r_out[:, 0:n], r_out[:, 0:n], ALU.mult, [k_out], [k_t])
                tt('dve', r_t[:, 0:n], r_t[:, 0:n], r_in[:, 0:n], ALU.mult, [k_t, k_in], [k_t])
                ts('dve', r_t[:, 0:n], r_t[:, 0:n], -0.5, 1.5, ALU.mult, ALU.add, [k_t], [k_t])
                tt('dve', r_out[:, 0:n], r_out[:, 0:n], r_t[:, 0:n], ALU.mult, [k_out, k_t], [k_out])

        S.op('pool', lambda en: en.memset(ident_f[:], 0.0), writes=['ident_f'])
        S.op('pool', lambda en: en.affine_select(out=ident_f[:], in_=ident_f[:], pattern=[[-1, 128]], compare_op=ALU.not_equal, fill=1.0, base=0, channel_multiplier=1), reads=['ident_f'], writes=['ident_f'])
        cp('pool', ident_bf[:], ident_f[:], ['ident_f'], ['ident_bf'])
        for (tl, op, nm, sg) in ((su_f, ALU.is_gt, 'su_f', -1), (ui_f, ALU.is_ge, 'ui_f', -1), (sl_f, ALU.is_gt, 'sl_f', 1)):
            S.op('pool', lambda en, tl=tl: en.memset(tl[:], 1.0), writes=[nm])
            S.op('pool', lambda en, tl=tl, sg=sg, op=op: en.affine_select(out=tl[:], in_=tl[:], pattern=[[-sg, 128]], compare_op=op, fill=0.0, base=0, channel_multiplier=sg), reads=[nm], writes=[nm])
        S.op('pool', lambda en: en.memset(ones_f[:], 1.0), writes=['ones_f'])
        S.op('pool', lambda en: en.memset(half_c[:], 0.5), writes=['half_c'])
        S.op('pool', lambda en: en.memset(eps_c[:], 1e-24), writes=['eps_c'])
        S.op('pool', lambda en: en.memset(rs_in[:], 1.0), writes=['rs_in'])
        S.op('pool', lambda en: en.memset(rs_out[:], 1.0), writes=['rs_out'])
        S.op('pool', lambda en: en.memset(rs_t[:], 1.0), writes=['rs_t'])
        S.op('pool', lambda en: en.memset(rsB_in[:], 1.0), writes=['rsB_in'])
        S.op('pool', lambda en: en.memset(rsB_out[:], 1.0), writes=['rsB_out'])
        S.op('pool', lambda en: en.memset(rsB_t[:], 1.0), writes=['rsB_t'])
        S.op('pool', lambda en: en.memset(zeros_f[:], 0.0), writes=['zeros_f'])
        S.op('pool', lambda en: en.memset(bones[:], 0.0), writes=['bones'])
        S.op('pool', lambda en: en.memset(bones[0:64, 0:64], 1.0), reads=['bones'], writes=['bones'])
        S.op('pool', lambda en: en.memset(bones[64:128, 64:128], 1.0), reads=['bones'], writes=['bones'])
        S.op('pool', lambda en: en.memset(carry[:], 0.0), writes=['carry'])
        S.op('pool', lambda en: en.memset(Hst[:], 0.0), writes=['Hst'])
        S.op('pool', lambda en: en.memset(Hbf[0][:], 0.0), writes=['Hbf0'])
        S.op('pool', lambda en: en.memset(Caug[:], 0.0), writes=['Caug'])
        S.op('pool', lambda en: en.memset(Caug_bf[:], 0.0), writes=['Caug_bf'])
        S.op('pool', lambda en: en.memset(Mst[:], 1.0), writes=['Mst'])
        S.op('pool', lambda en: en.memset(xmc[:], 0.0), writes=['xmc'])
        S.dma('sp', pp[:], pp_d[:, :], writes=['pp'])
        S.dma('sp', gfin_bc[:], gfin_d[0:1, :].partition_broadcast(128), writes=['gfin_bc'])
        S.dma('sp', gbt[:], gbt_d[0:1, :].partition_broadcast(128), writes=['gbt'])
        S.dma('sp', gbf[:], gbf_d[:, :], writes=['gbf'])
        S.dma('pool', w2a2_bf[:], w2a2_d[:, :], writes=['w2a2_bf'])
        S.dma('pool', g2_bf[:], g2_d[:, :], writes=['g2_bf'])
        S.dma('pool', wq_bf[:], wq_d[:, :, :], writes=['wq_bf'])
        S.dma('pool', wk_bf[:], wk_d[:, :, :], writes=['wk_bf'])
        ts('dve', pq[:, Q_OMU:Q_OMU + 14], pp[:, P_MU:P_MU + 14], -1.0, 1.0, ALU.mult, ALU.add, ['pp'], ['pq'])
        ts('dve', pq[:, Q_HW0:Q_HW0 + 4], pp[:, P_W0:P_W0 + 4], 0.5, None, ALU.mult, None, ['pp'], ['pq'])
        ts('dve', pq[:, Q_HA0:Q_HA0 + 4], pp[:, P_A0:P_A0 + 4], 0.5, None, ALU.mult, None, ['pp'], ['pq'])
        ts('dve', pq[:, Q_OMKA:Q_OMKA + 4], pp[:, P_KA:P_KA + 4], -1.0, 1.0, ALU.mult, ALU.add, ['pp'], ['pq'])
        ts('dve', pq[:, Q_HGTB:Q_HGTB + 16], pp[:, P_GTB:P_GTB + 16], 0.5, None, ALU.mult, None, ['pp'], ['pq'])
        ts('dve', hfb[:], gbf[:, 1:2], 0.5, None, ALU.mult, None, ['gbf'], ['hfb'])
        PPK = ['pp', 'pq']

        wseq = []
        if with_sample:
            for g in range(11):
                c0, c1 = g * 512, min((g + 1) * 512, 5384)
                wseq.append(('s_in%d' % g, win_d[:, :, c0:c1], 8, c1 - c0))
            for hh in range(2):
                wseq.append(('s_ru%d' % hh, rup_d[:, :, hh * 512:(hh + 1) * 512], 4, 512))
            for hh in range(2):
                wseq.append(('s_mu%d' % hh, mup_d[:, :, hh * 512:(hh + 1) * 512], 4, 512))
            for hh in range(2):
                wseq.append(('s_wo%d' % hh, wout_d[:, :, hh * 512:(hh + 1) * 512], 8, 512))
        for b in range(nblk):
            wseq.append(('lat', win_d[:, :, C_LAT:C_LAT + 256], 8, 256))
            for j in range(4):
                wseq.append(('r%d' % j, win_d[:, :, C_R + j * 384:C_R + (j + 1) * 384], 8, 384))
            wseq.append(('xm', win_d[:, :, C_XM:C_XM + 512], 8, 512))
            wseq.append(('gt', win_d[:, :, C_GT:C_GT + 8], 8, 8))
            wseq.append(('mv', win_d[:, :, C_MV:C_MV + 512], 8, 512))
            wseq.append(('o', win_d[:, :, C_O:C_O + 512], 8, 512))
            for hh in range(2):
                wseq.append(('gr%d' % hh, win_d[:, :, C_GR + hh * 512:C_GR + (hh + 1) * 512], 8, 512))
                wseq.append(('gm%d' % hh, win_d[:, :, C_GM + hh * 512:C_GM + (hh + 1) * 512], 8, 512))
                wseq.append(('ru%d' % hh, rup_d[:, :, hh * 512:(hh + 1) * 512], 4, 512))
                wseq.append(('mu%d' % hh, mup_d[:, :, hh * 512:(hh + 1) * 512], 4, 512))
            for hh in range(2):
                wseq.append(('wo%d' % hh, wout_d[:, :, hh * 512:(hh + 1) * 512], 8, 512))
            for g in range(8):
                wseq.append(('w1_%d' % g, w1_d[:, :, g * 512:(g + 1) * 512], 8, 512))
            for hh in range(2):
                for kg in range(4):
                    wseq.append(('w2_%d_%d' % (hh, kg), w2_d[:, kg * 8:(kg + 1) * 8, hh * 512:(hh + 1) * 512], 8, 512))
        wstate = {'issued': 0, 'used': 0, 'done': 0}

        def w_pump():
            while wstate['issued'] < min(wstate['done'] + NSLOT, len(wseq)):
                i = wstate['issued']
                name, src, kc, ncol = wseq[i]
                slot = i % NSLOT
                dst = wbuf[slot][:, 0:kc * ncol].rearrange("p (k c) -> p k c", k=kc)
                S.dma('pool', dst, src, writes=[('wbuf', slot)])
                wstate['issued'] = i + 1

        def w_get(name):
            i = wstate['used']
            nm, src, kc, ncol = wseq[i]
            assert nm == name, (nm, name)
            w_pump()
            assert wstate['issued'] > i
            wstate['used'] = i + 1
            slot = i % NSLOT
            return wbuf[slot][:, 0:kc * ncol].rearrange("p (k c) -> p k c", k=kc), ('wbuf', slot)

        def w_done(n=1):
            wstate['done'] += n
            assert wstate['done'] <= wstate['used']
            w_pump()

        w_pump()

        def chk(name):
            if stop == name:
                raise _Stop()

        def sample_phase():
            with contextlib.ExitStack() as ss_:
                P_s = SB(ss_, "P_s", [16, 5384])
                big0 = SB(ss_, "big0", [128, 8192])
                xs_t = SB(ss_, "xs_t", [16, D])
                x1_s = x1_sP
                gmix_bc = SB(ss_, "gmix_bc", [16, D])
                junk_s = SB(ss_, "junk_s", [16, D], BF16)
                xnb = SB(ss_, "xnb", [16, D], BF16)
                shb = SB(ss_, "shb", [16, D], BF16)
                xnT_s = SB(ss_, "xnT_s", [128, 8, 16], BF16)
                shT_s = SB(ss_, "shT_s", [128, 8, 16], BF16)
                ss_s = SB(ss_, "ss_s", [16, 1])
                yrT_s = SB(ss_, "yrT_s", [128, 4, 16], BF16)
                ymT_s = SB(ss_, "ymT_s", [128, 4, 16], BF16)
                PK = [('P_s', g) for g in range(11)] + [('P_s', 3, 'b')]

                def rms16(src, srck, gain, gaink, outp, outk):
                    S.op('pool', lambda en: en.memset(ss_s[:], 0.0), reads=['rs_in'], writes=['ss_s'])
                    act(junk_s[:], src, AF.Square, srck, ['junk_s', 'ss_s'], accum=ss_s[:, 0:1])
                    ts('dve', rs_in[0:16, 0:1], ss_s[:], 1.0 / D, 1e-6, ALU.mult, ALU.add, ['ss_s', 'rs_out'], ['rs_in'])
                    rsqrt(1, [])
                    stt('dve', outp, src, rs_out[0:16, 0:1], gain, ALU.mult, ALU.mult, srck + ['rs_out'] + gaink, outk)

                def tr16(src_bf, srck, dst, dstk, nch):
                    bi = nb()
                    psb = ps[bi][:].bitcast(BF16)
                    for kc in range(nch):
                        tr(psb[:, kc * 16:(kc + 1) * 16], src_bf[0:16, kc * 128:(kc + 1) * 128], ident_bf[0:16, 0:16], srck + ['ident_bf'], [psk[bi]])
                    cp('dve', dst[:, 0:nch, :], psb[:, 0:nch * 16].rearrange("p (k t) -> p k t", k=nch), [psk[bi]], [dstk])

                def red(out, in3, r, w):
                    return S.op('dve', lambda en: en.tensor_reduce(out=out, in_=in3, axis=AX.X, op=ALU.add), reads=r, writes=w)

                Sb = big0[:, 0:4096]
                S.dma('sp', Sb, wkv0_d[:, :], writes=['Sb'])
                S.dma('sp', xs_t[:], xs_d[:, :], writes=['xs_t'])
                S.dma('pool', shb[:], sh0_d[:, :], writes=['shb'])
                S.dma('sp', gmix_bc[:], gmixrow_d[0:1, :].partition_broadcast(16), writes=['gmix_bc'])
                rms16(xs_t[:], ['xs_t'], gmix_bc[:], ['gmix_bc'], x1_s[:], ['x1_s'])
                S.dma('sp', s_shift_d[:, :], x1_s[:], reads=['x1_s'], writes=['o_sshift'])
                cp('act', xnb[:], x1_s[:], ['x1_s'], ['xnb'])
                tr16(xnb, ['xnb'], xnT_s, 'xnT_s', 8)
                tr16(shb, ['shb'], shT_s, 'shT_s', 8)

                with contextlib.ExitStack() as s1:
                    pr1 = SB(s1, "pr1", [16, 5376])
                    T1t = SB(s1, "T1t", [128, 4096])
                    T1 = T1t[:]
                    S.dma('sp', pr1[:], prow1_d[0:1, :].partition_broadcast(16), writes=['pr1'])
                    pf_sb = SB(s1, "pf_sb", [16, 512])
                    dtmp = SB(s1, "dtmp", [16, 512])
                    for g in range(11):
                        wg, wgk = w_get('s_in%d' % g)
                        c0, c1 = g * 512, min((g + 1) * 512, 5384)
                        ncl = c1 - c0
                        bi = nb()
                        for kc in range(8):
                            mm(ps[bi][0:16, 0:ncl], xnT_s[:, kc, :], wg[:, kc, :], kc == 0, kc == 7, [wgk, 'xnT_s'], [psk[bi]])
                        if c0 < 1792:
                            ncm = min(c1, 1792) - c0
                            b2 = nb()
                            for kc in range(8):
                                mm(ps[b2][0:16, 0:ncm], shT_s[:, kc, :], wg[:, kc, 0:ncm], kc == 0, kc == 7, [wgk, 'shT_s'], [psk[b2]])
                            cp('act', pf_sb[:, 0:ncm], ps[b2][0:16, 0:ncm], [psk[b2]], ['pf_sb'])
                            tt('dve', dtmp[:, 0:ncm], pf_sb[:, 0:ncm], ps[bi][0:16, 0:ncm], ALU.subtract, ['pf_sb', psk[bi]], ['dtmp'])
                            tt('dve', dtmp[:, 0:ncm], dtmp[:, 0:ncm], pr1[:, c0:c0 + ncm], ALU.mult, ['dtmp', 'pr1'], ['dtmp'])
                            tt('dve', P_s[:, c0:c0 + ncm], dtmp[:, 0:ncm], ps[bi][0:16, 0:ncm], ALU.add, ['dtmp', psk[bi]], [('P_s', g)])
                            if ncm < ncl:
                                cp('dve', P_s[:, c0 + ncm:c1], ps[bi][0:16, ncm:ncl], [psk[bi]], [('P_s', g, 'b')])
                        else:
                            cp('act' if g % 2 else 'dve', P_s[:, c0:c1], ps[bi][0:16, 0:ncl], [psk[bi]], [('P_s', g)])
                        w_done()
                    R_W0, R_A0, R_RKK, R_KA, R_RRK, R_GNW, R_GNB = 1792, 2304, 2816, 3328, 3840, 4352, 4864
                    pw = lambda off: pr1[:, off:off + 512]
                    rkv = P_s[:, 256:1792].rearrange("p (j w c) -> p j w c", j=4, w=3)
                    V6 = SB(s1, "V6", [16, 6, 512])
                    k_t = SB(s1, "k_t", [16, 512])
                    a_t = SB(s1, "a_t", [16, 512])
                    kr = SB(s1, "kr", [16, 512])
                    g_t = SB(s1, "g_t", [16, 512])
                    y_t = SB(s1, "y_t", [16, 512])
                    tS1 = SB(s1, "tS1", [16, 512])
                    tS2 = SB(s1, "tS2", [16, 512])
                    yr_bf = SB(s1, "yr_bf", [16, 512], BF16)
                    latb = SB(s1, "latb", [16, 256], BF16)
                    latT_s = SB(s1, "latT_s", [128, 2, 16], BF16)
                    ssk = SB(s1, "ssk", [16, 8])
                    st1_s = SB(s1, "st1_s", [16, 8])
                    st2_s = SB(s1, "st2_s", [16, 8])
                    bon = SB(s1, "bon", [16, 8])
                    pv = SB(s1, "pv", [128, 6, 64])
                    skk = SB(s1, "skk", [128, 64])
                    yp = SB(s1, "yp", [128, 64])
                    v4 = lambda ap: ap.rearrange("p (j c) -> p j c", j=4)
                    v8 = lambda ap: ap.rearrange("p (h c) -> p h c", h=8)
                    r_t, v_t = V6[:, 4, :], V6[:, 5, :]
                    cp('dve', v4(r_t), rkv[:, :, 0, :], PK, [('V6', 4)])
                    cp('dve', v4(k_t[:]), rkv[:, :, 1, :], PK, ['k_t'])
                    cp('dve', v4(v_t), rkv[:, :, 2, :], PK, [('V6', 5)])
                    act(latb[:, 0:64], P_s[:, 0:64], AF.Tanh, PK, [('latb', 0)])
                    cp('dve', latb[:, 64:128], P_s[:, 64:128], PK, [('latb', 1)])
                    act(tS1[:, 0:128], P_s[:, 128:256], AF.Tanh, PK, ['tS1'], scale=0.5)
                    ts('dve', latb[:, 128:256], tS1[:, 0:128], 0.5, 0.5, ALU.mult, ALU.add, ['tS1'], [('latb', 2)])
                    tr16(latb, [('latb', i) for i in range(3)], latT_s, 'latT_s', 2)
                    bw, ba, bg = nb(), nb(), nb()
                    mm(ps[bw][0:16, :], latT_s[0:64, 0, :], w2a2_bf[0:64, :], True, True, ['latT_s', 'w2a2_bf'], [psk[bw]])
                    mm(ps[ba][0:16, :], latT_s[64:128, 0, :], w2a2_bf[64:128, :], True, True, ['latT_s', 'w2a2_bf'], [psk[ba]])
                    mm(ps[bg][0:16, :], latT_s[:, 1, :], g2_bf[:, :], True, True, ['latT_s', 'g2_bf'], [psk[bg]])
                    cp('act', g_t[:], ps[bg][0:16, :], [psk[bg]], ['g_t'])
                    tt('dve', tS1[:], ps[bw][0:16, :], pw(R_W0), ALU.add, [psk[bw], 'pr1'], ['tS1'])
                    act(tS1[:], tS1[:], AF.Tanh, ['tS1'], ['tS1'], scale=0.5)
                    ts('dve', tS1[:], tS1[:], 0.5, 0.5, ALU.mult, ALU.add, ['tS1'], ['tS1'])
                    act(V6[:, 1, :], tS1[:], AF.Exp, ['tS1'], [('V6', 1)], scale=-CC)
                    tt('dve', tS2[:], ps[ba][0:16, :], pw(R_A0), ALU.add, [psk[ba], 'pr1'], ['tS2'])
                    act(tS2[:], tS2[:], AF.Tanh, ['tS2'], ['tS2'], scale=0.5)
                    ts('dve', a_t[:], tS2[:], 0.5, 0.5, ALU.mult, ALU.add, ['tS2'], ['a_t'])
                    tt('dve', kr[:], k_t[:], pw(R_RKK), ALU.mult, ['k_t', 'pr1'], ['kr'])
                    tt('dve', tS1[:], kr[:], kr[:], ALU.mult, ['kr'], ['tS1'])
                    red(ssk[:], v8(tS1[:]), ['tS1'], ['ssk'])
                    ts('dve', rs_in[0:16, 0:8], ssk[:], 1e-24, None, ALU.add, None, ['ssk', 'rs_out'], ['rs_in'])
                    rsqrt(8, [])
                    tt('dve', v8(V6[:, 0, :]), v8(kr[:]), bc_in(rs_out[0:16, 0:8], 64), ALU.mult, ['kr', 'rs_out'], [('V6', 0)])
                    tt('dve', V6[:, 2, :], a_t[:], V6[:, 0, :], ALU.mult, ['a_t', ('V6', 0)], [('V6', 2)])
                    tt('dve', tS1[:], a_t[:], pw(R_KA), ALU.mult, ['a_t', 'pr1'], ['tS1'])
                    ts('dve', tS2[:], pw(R_KA), -1.0, 1.0, ALU.mult, ALU.add, ['pr1'], ['tS2'])
                    tt('dve', tS1[:], tS1[:], tS2[:], ALU.add, ['tS1', 'tS2'], ['tS1'])
                    tt('dve', V6[:, 3, :], k_t[:], tS1[:], ALU.mult, ['k_t', 'tS1'], [('V6', 3)])
                    V6K = [('V6', i) for i in range(6)]
                    S.dma('sp', scr1.rearrange("q (b f) -> b q f", b=16), V6[:], reads=V6K, writes=['scr1'])
                    S.dma('sp', pv[:], scr1.rearrange("q (p c) -> p q c", c=64), reads=['scr1'], writes=['pv'])
                    S3 = Sb.rearrange("p (v k) -> p v k", v=64)
                    T3 = T1.rearrange("p (v k) -> p v k", v=64)
                    tt('dve', T3, S3, bc_mid(pv[:, 0, :], 64), ALU.mult, ['Sb', 'pv'], ['T1'])
                    red(skk[:], T3, ['T1'], ['skk'])
                    tt('dve', S3, S3, bc_mid(pv[:, 1, :], 64), ALU.mult, ['Sb', 'pv'], ['Sb'])
                    tt('dve', T3, bc_in(skk[:], 64), bc_mid(pv[:, 2, :], 64), ALU.mult, ['skk', 'pv', 'T1'], ['T1'])
                    tt('dve', S3, S3, T3, ALU.subtract, ['Sb', 'T1'], ['Sb'])
                    tt('dve', T3, bc_in(pv[:, 5, :], 64), bc_mid(pv[:, 3, :], 64), ALU.mult, ['pv', 'T1'], ['T1'])
                    tt('dve', S3, S3, T3, ALU.add, ['Sb', 'T1'], ['Sb'])
                    S.dma('sp', s_wkv_d[:, :], Sb, reads=['Sb'], writes=['o_swkv'])
                    tt('dve', T3, S3, bc_mid(pv[:, 4, :], 64), ALU.mult, ['Sb', 'pv', 'T1'], ['T1'])
                    red(yp[:], T3, ['T1'], ['yp'])
                    S.dma('sp', scr2[:, :], yp[:], reads=['yp'], writes=['scr2'])
                    S.dma('sp', y_t[:], scr2.rearrange("(b h) c -> b (h c)", b=16), reads=['scr2'], writes=['y_t'])
                    red(st1_s[:], v8(y_t[:]), ['y_t'], ['st1_s'])
                    tt('dve', tS1[:], y_t[:], y_t[:], ALU.mult, ['y_t'], ['tS1'])
                    red(st2_s[:], v8(tS1[:]), ['tS1'], ['st2_s'])
                    ts('dve', st1_s[:], st1_s[:], 1.0 / 64, None, ALU.mult, None, ['st1_s'], ['st1_s'])
                    tt('dve', rs_in[0:16, 0:8], st1_s[:], st1_s[:], ALU.mult, ['st1_s', 'rs_out'], ['rs_in'])
                    stt('dve', rs_in[0:16, 0:8], st2_s[:], 1.0 / 64, rs_in[0:16, 0:8], ALU.mult, ALU.subtract, ['st2_s', 'rs_in'], ['rs_in'])
                    ts('dve', rs_in[0:16, 0:8], rs_in[0:16, 0:8], 64e-5, None, ALU.add, None, ['rs_in'], ['rs_in'])
                    rsqrt(8, [])
                    tt('dve', v8(tS1[:]), v8(y_t[:]), bc_in(st1_s[:], 64), ALU.subtract, ['y_t', 'st1_s', 'tS1'], ['tS1'])
                    tt('dve', v8(tS1[:]), v8(tS1[:]), bc_in(rs_out[0:16, 0:8], 64), ALU.mult, ['tS1', 'rs_out'], ['tS1'])
                    tt('dve', tS1[:], tS1[:], pw(R_GNW), ALU.mult, ['tS1', 'pr1'], ['tS1'])
                    tt('dve', tS1[:], tS1[:], pw(R_GNB), ALU.add, ['tS1', 'pr1'], ['tS1'])
                    tt('dve', tS2[:], r_t, pw(R_RRK), ALU.mult, [('V6', 4), 'pr1'], ['tS2'])
                    tt('dve', tS2[:], tS2[:], V6[:, 3, :], ALU.mult, ['tS2', ('V6', 3)], ['tS2'])
                    red(bon[:], v8(tS2[:]), ['tS2'], ['bon'])
                    tt('dve', v8(tS2[:]), v8(v_t), bc_in(bon[:], 64), ALU.mult, [('V6', 5), 'bon', 'tS2'], ['tS2'])
                    tt('dve', tS1[:], tS1[:], tS2[:], ALU.add, ['tS1', 'tS2'], ['tS1'])
                    tt('dve', yr_bf[:], tS1[:], g_t[:], ALU.mult, ['tS1', 'g_t'], ['yr_bf'])
                    tr16(yr_bf, ['yr_bf'], yrT_s, 'yrT_s', 4)
                    S.barrier()

                with contextlib.ExitStack() as s2:
                    pr2 = SB(s2, "pr2", [16, 3584])
                    big1 = SB(s2, "big1", [128, 8192])
                    S.dma('sp', pr2[:], prow2_d[0:1, :].partition_broadcast(16), writes=['pr2'])
                    R_CB, R_MGN, R_MSK = 2048, 2560, 3072
                    cv = SB(s2, "cv", [16, 3, 512])
                    S.dma('sp', cv[:], conv0_d[:, :, :], writes=['cv'])
                    m0t = SB(s2, "m0t", [16, 4])
                    S.dma('sp', m0t[:], m0_d[:, :], writes=['m0t'])
                    C3 = big0[:].rearrange("p (d e) -> p d e", d=128)
                    TC3 = big1[:].rearrange("p (d e) -> p d e", d=128)
                    for eh in range(2):
                        S.dma('sp', big0[eh * 64:(eh + 1) * 64, :].rearrange("p (d e) -> p d e", d=128), C0_d[:, :, eh * 64:(eh + 1) * 64], reads=['Sb'], writes=[('C', eh)])
                    CK = [('C', 0), ('C', 1)]
                    n0p = SB(s2, "n0p", [128, 128])
                    for eh in range(2):
                        S.dma('sp', n0p[eh * 64:(eh + 1) * 64, :], n0_d[:, :], writes=[('n0p', eh)])
                    xc = SB(s2, "xc", [16, 512])
                    tM1 = SB(s2, "tM1", [16, 512])
                    tM2 = SB(s2, "tM2", [16, 512])
                    q_t = SB(s2, "q_t", [16, 512])
                    k_tm = SB(s2, "k_tm", [16, 512])
                    h_t = SB(s2, "h_t", [16, 512])
                    xc_bf = SB(s2, "xc_bf", [16, 512], BF16)
                    ym_bf = SB(s2, "ym_bf", [16, 512], BF16)
                    xcT_s = SB(s2, "xcT_s", [128, 4, 16], BF16)
                    gs = SB(s2, "gs", [16, 12, 4])
                    sc4 = SB(s2, "sc4", [16, 4, 4])
                    qp = SB(s2, "qp", [128, 128])
                    kp = SB(s2, "kp", [128, 128])
                    vp = SB(s2, "vp", [128, 64])
                    sc = SB(s2, "sc", [128, 4])
                    tq = SB(s2, "tq", [128, 128])
                    nn = SB(s2, "nn", [128, 128])
                    qC = SB(s2, "qC", [128, 64])
                    hp = SB(s2, "hp", [128, 64])
                    sm = SB(s2, "sm", [128, 4])
                    ms1 = SB(s2, "ms1s", [16, 4])
                    ms2 = SB(s2, "ms2s", [16, 4])
                    xm = P_s[:, 1792:2304]
                    mvv = P_s[:, 2304:2816]
                    oo = P_s[:, 2816:3328]
                    tt('dve', xc[:], cv[:, 0, :], pr2[:, 0:512], ALU.mult, ['cv', 'pr2'], ['xc'])
                    tt('dve', xc[:], xc[:], pr2[:, R_CB:R_CB + 512], ALU.add, ['xc', 'pr2'], ['xc'])
                    for tap in (1, 2):
                        tt('dve', tM1[:], cv[:, tap, :], pr2[:, tap * 512:(tap + 1) * 512], ALU.mult, ['cv', 'pr2'], ['tM1'])
                        tt('dve', xc[:], xc[:], tM1[:], ALU.add, ['xc', 'tM1'], ['xc'])
                    tt('dve', tM1[:], xm, pr2[:, 3 * 512:4 * 512], ALU.mult, PK + ['pr2'], ['tM1'])
                    tt('dve', xc[:], xc[:], tM1[:], ALU.add, ['xc', 'tM1'], ['xc'])
                    act(tM1[:], xc[:], AF.Tanh, ['xc'], ['tM1'], scale=0.5)
                    ts('dve', tM1[:], tM1[:], 0.5, 0.5, ALU.mult, ALU.add, ['tM1'], ['tM1'])
                    tt('dve', xc[:], xc[:], tM1[:], ALU.mult, ['xc', 'tM1'], ['xc'])
                    S.dma('sp', s_conv_d[:, 0:2, :], cv[:, 1:3, :], reads=['cv'], writes=['o_sconv0'])
                    S.dma('sp', s_conv_d[:, 2, :], xm, reads=PK, writes=['o_sconv1'])
                    cp('act', xc_bf[:], xc[:], ['xc'], ['xc_bf'])
                    tr16(xc_bf, ['xc_bf'], xcT_s, 'xcT_s', 4)
                    bq, bk = nb(), nb()
                    for hh in range(4):
                        mm(ps[bq][0:16, hh * 128:(hh + 1) * 128], xcT_s[:, hh, :], wq_bf[:, hh, :], True, True, ['xcT_s', 'wq_bf'], [psk[bq]])
                    for hh in range(4):
                        mm(ps[bk][0:16, hh * 128:(hh + 1) * 128], xcT_s[:, hh, :], wk_bf[:, hh, :], True, True, ['xcT_s', 'wk_bf'], [psk[bk]])
                    act(q_t[:], ps[bq][0:16, :], AF.Copy, [psk[bq]], ['q_t'], scale=128 ** -0.5)
                    cp('dve', k_tm[:], ps[bk][0:16, :], [psk[bk]], ['k_tm'])
                    G = lambda i: gs[:, i, :]
                    tt('dve', G(0), P_s[:, 3328:3332], gbt[0:16, 0:4], ALU.add, PK + ['gbt'], [('gs', 0)])
                    tt('dve', G(1), P_s[:, 3332:3336], gbt[0:16, 4:8], ALU.add, PK + ['gbt'], [('gs', 1)])
                    act(G(1), G(1), AF.Tanh, [('gs', 1)], [('gs', 1)], scale=0.5)
                    ts('dve', G(1), G(1), 0.5, 0.5, ALU.mult, ALU.add, [('gs', 1)], [('gs', 1)])
                    act(G(2), G(1), AF.Ln, [('gs', 1)], [('gs', 2)])
                    tt('dve', G(3), G(2), m0t[:], ALU.add, [('gs', 2), 'm0t'], [('gs', 3)])
                    tt('dve', G(4), G(3), G(0), ALU.max, [('gs', 3), ('gs', 0)], [('gs', 4)])
                    S.dma('sp', s_m_d[:, :], G(4), reads=[('gs', 4)], writes=['o_sm'])
                    tt('dve', G(5), G(0), G(4), ALU.subtract, [('gs', 0), ('gs', 4)], [('gs', 5)])
                    act(sc4[:, :, 0], G(5), AF.Exp, [('gs', 5)], [('sc4', 0)])
                    tt('dve', G(5), G(3), G(4), ALU.subtract, [('gs', 3), ('gs', 4), ('sc4', 0)], [('gs', 5)])
                    act(sc4[:, :, 1], G(5), AF.Exp, [('gs', 5)], [('sc4', 1)])
                    act(sc4[:, :, 2], G(4), AF.Exp, [('gs', 4)], [('sc4', 2)], scale=-1.0)
                    tt('dve', tM1[:], q_t[:], k_tm[:], ALU.mult, ['q_t', 'k_tm'], ['tM1'])
                    red(G(6), v4(tM1[:]), ['tM1'], [('gs', 6)])
                    tt('dve', sc4[:, :, 3], G(6), sc4[:, :, 0], ALU.mult, [('gs', 6), ('sc4', 0)], [('sc4', 3)])
                    SCK = [('sc4', i) for i in range(4)]
                    S.dma('sp', scrq.rearrange("(b h) d -> b (h d)", b=16), q_t[:], reads=['q_t'], writes=['scrq'])
                    S.dma('sp', scrk.rearrange("(b h) d -> b (h d)", b=16), k_tm[:], reads=['k_tm'], writes=['scrk'])
                    S.dma('sp', scrv.rearrange("(b h) d -> b (h d)", b=16), mvv, reads=PK, writes=['scrv'])
                    S.dma('sp', scrs.rearrange("(b h) s -> b (h s)", b=16), sc4[:].rearrange("p h s -> p (h s)"), reads=SCK, writes=['scrs'])
                    for eh in range(2):
                        psl = slice(eh * 64, (eh + 1) * 64)
                        S.dma('sp', qp[psl, :], scrq[:, :], reads=['scrq'], writes=[('qp', eh)])
                        S.dma('sp', kp[psl, :], scrk[:, :], reads=['scrk'], writes=[('kp', eh)])
                        S.dma('sp', vp[psl, :], scrv[:, eh * 64:(eh + 1) * 64], reads=['scrv'], writes=[('vp', eh)])
                        S.dma('sp', sc[psl, :], scrs[:, :], reads=['scrs'], writes=[('sc', eh)])
                    QP = [('qp', 0), ('qp', 1)]
                    KP = [('kp', 0), ('kp', 1)]
                    VP = [('vp', 0), ('vp', 1)]
                    SC = [('sc', 0), ('sc', 1)]
                    NP_ = [('n0p', 0), ('n0p', 1)]
                    tt('dve', TC3, C3, bc_in(qp[:], 64), ALU.mult, CK + QP + ['T1'], ['TC'])
                    red(qC[:], TC3.rearrange("p d e -> p e d"), ['TC'], ['qC'])
                    tt('dve', tq[:], qp[:], n0p[:], ALU.mult, QP + NP_, ['tq'])
                    red(sm[:, 0:1], tq[:], ['tq'], [('sm', 0)])
                    ts('dve', hp[:], qC[:], sc[:, 1:2], None, ALU.mult, None, ['qC'] + SC, ['hp'])
                    stt('dve', hp[:], vp[:], sc[:, 3:4], hp[:], ALU.mult, ALU.add, VP + SC + ['hp'], ['hp'])
                    stt('dve', sm[:, 1:2], sm[:, 0:1], sc[:, 1:2], sc[:, 3:4], ALU.mult, ALU.add, [('sm', 0)] + SC, [('sm', 1)])
                    stt('dve', sm[:, 2:3], sm[:, 1:2], -1.0, sm[:, 1:2], ALU.mult, ALU.max, [('sm', 1)], [('sm', 2)])
                    tt('dve', sm[:, 2:3], sm[:, 2:3], sc[:, 2:3], ALU.max, [('sm', 2)] + SC, [('sm', 2)])
                    S.op('dve', lambda en: en.reciprocal(out=sm[:, 3:4], in_=sm[:, 2:3]), reads=[('sm', 2)], writes=[('sm', 3)])
                    ts('dve', hp[:], hp[:], sm[:, 3:4], None, ALU.mult, None, ['hp', ('sm', 3)], ['hp'])
                    tt('dve', TC3, bc_in(kp[:], 64), bc_mid(vp[:], 128), ALU.mult, KP + VP + ['TC'], ['TC'])
                    ts('dve', big0[:], big0[:], sc[:, 1:2], None, ALU.mult, None, CK + SC, CK)
                    stt('dve', big0[:], big1[:], sc[:, 0:1], big0[:], ALU.mult, ALU.add, ['TC'] + CK + SC, CK)
                    ts('dve', nn[:], n0p[:], sc[:, 1:2], None, ALU.mult, None, NP_ + SC, ['nn'])
                    stt('dve', nn[:], kp[:], sc[:, 0:1], nn[:], ALU.mult, ALU.add, KP + SC + ['nn'], ['nn'])
                    for eh in range(2):
                        S.dma('sp', s_C_d[:, :, eh * 64:(eh + 1) * 64], big0[eh * 64:(eh + 1) * 64, :].rearrange("p (d e) -> p d e", d=128), reads=CK, writes=[('o_sC', eh)])
                        S.dma('sp', scrh[:, eh * 64:(eh + 1) * 64], hp[eh * 64:(eh + 1) * 64, :], reads=['hp'], writes=[('scrh', eh)])
                    S.dma('sp', s_n_d[:, :], nn[0:64, :], reads=['nn'], writes=['o_sn'])
                    S.dma('sp', h_t[:], scrh.rearrange("(b h) e -> b (h e)", b=16), reads=[('scrh', 0), ('scrh', 1)], writes=['h_t'])
                    red(ms1[:], v4(h_t[:]), ['h_t'], ['ms1s'])
                    tt('dve', tM1[:], h_t[:], h_t[:], ALU.mult, ['h_t'], ['tM1'])
                    red(ms2[:], v4(tM1[:]), ['tM1'], ['ms2s'])
                    ts('dve', ms1[:], ms1[:], 1.0 / 128, None, ALU.mult, None, ['ms1s'], ['ms1s'])
                    tt('dve', rs_in[0:16, 0:4], ms1[:], ms1[:], ALU.mult, ['ms1s', 'rs_out'], ['rs_in'])
                    stt('dve', rs_in[0:16, 0:4], ms2[:], 1.0 / 128, rs_in[0:16, 0:4], ALU.mult, ALU.subtract, ['ms2s', 'rs_in'], ['rs_in'])
                    ts('dve', rs_in[0:16, 0:4], rs_in[0:16, 0:4], 1e-5, None, ALU.add, None, ['rs_in'], ['rs_in'])
                    rsqrt(4, [])
                    tt('dve', v4(tM1[:]), v4(h_t[:]), bc_in(ms1[:], 128), ALU.subtract, ['h_t', 'ms1s', 'tM1'], ['tM1'])
                    tt('dve', v4(tM1[:]), v4(tM1[:]), bc_in(rs_out[0:16, 0:4], 128), ALU.mult, ['tM1', 'rs_out'], ['tM1'])
                    tt('dve', tM1[:], tM1[:], pr2[:, R_MGN:R_MGN + 512], ALU.mult, ['tM1', 'pr2'], ['tM1'])
                    tt('dve', tM2[:], xc[:], pr2[:, R_MSK:R_MSK + 512], ALU.mult, ['xc', 'pr2'], ['tM2'])
                    tt('dve', tM1[:], tM1[:], tM2[:], ALU.add, ['tM1', 'tM2'], ['tM1'])
                    act(tM2[:], oo, AF.Tanh, PK + ['tM2'], ['tM2'], scale=0.5)
                    ts('dve', tM2[:], tM2[:], 0.5, 0.5, ALU.mult, ALU.add, ['tM2'], ['tM2'])
                    tt('dve', ym_bf[:], tM1[:], tM2[:], ALU.mult, ['tM1', 'tM2'], ['ym_bf'])
                    tr16(ym_bf, ['ym_bf'], ymT_s, 'ymT_s', 4)
                    S.barrier()

                with contextlib.ExitStack() as s3:
                    pr3 = SB(s3, "pr3", [16, 4096])
                    S.dma('sp', pr3[:], prow3_d[0:1, :].partition_broadcast(16), writes=['pr3'])
                    mrg = SB(s3, "mrg", [16, D])
                    tg = SB(s3, "tg", [16, 512])
                    mrg_bf = SB(s3, "mrg_bf", [16, D], BF16)
                    mrgT_s = SB(s3, "mrgT_s", [128, 8, 16], BF16)
                    hn_s = SB(s3, "hn_s", [16, D])
                    hn_bf = SB(s3, "hn_bf", [16, D], BF16)
                    rl_s = SB(s3, "rl_s", [128, 512])
                    h1T_s = SB(s3, "h1T_s", [128, 512], BF16)
                    y_s = SB(s3, "y_s", [16, D])
                    for br, (pref, yT, yk) in enumerate((('s_ru', yrT_s, 'yrT_s'), ('s_mu', ymT_s, 'ymT_s'))):
                        for hh in range(2):
                            wu, wuk = w_get('%s%d' % (pref, hh))
                            bi = nb()
                            for kc in range(4):
                                mm(ps[bi][0:16, :], yT[:, kc, :], wu[:, kc, :], kc == 0, kc == 3, [wuk, yk], [psk[bi]])
                            w_done()
                            gcol = 3336 + br * 1024 + hh * 512
                            tt('dve', tg[:], P_s[:, gcol:gcol + 512], pr3[:, br * 1024 + hh * 512:br * 1024 + (hh + 1) * 512], ALU.add, PK + ['pr3', 'tg'], ['tg'])
                            act(tg[:], tg[:], AF.Tanh, ['tg'], ['tg'], scale=0.5)
                            ts('dve', tg[:], tg[:], 0.5, 0.5, ALU.mult, ALU.add, ['tg'], ['tg'])
                            if br == 0:
                                tt('dve', mrg[:, hh * 512:(hh + 1) * 512], tg[:], ps[bi][0:16, :], ALU.mult, ['tg', psk[bi]], [('mrg', hh)])
                            else:
                                tt('dve', tg[:], tg[:], ps[bi][0:16, :], ALU.mult, ['tg', psk[bi]], ['tg'])
                                tt('dve', mrg[:, hh * 512:(hh + 1) * 512], mrg[:, hh * 512:(hh + 1) * 512], tg[:], ALU.add, [('mrg', hh), 'tg'], [('mrg', hh)])
                    cp('act', mrg_bf[:], mrg[:], [('mrg', 0), ('mrg', 1)], ['mrg_bf'])
                    tr16(mrg_bf, ['mrg_bf'], mrgT_s, 'mrgT_s', 8)
                    for hh in range(2):
                        wo, wok = w_get('s_wo%d' % hh)
                        bi = nb()
                        for kc in range(8):
                            mm(ps[bi][0:16, :], mrgT_s[:, kc, :], wo[:, kc, :], kc == 0, kc == 7, [wok, 'mrgT_s'], [psk[bi]])
                        w_done()
                        tt('dve', x1_s[:, hh * 512:(hh + 1) * 512], xs_t[:, hh * 512:(hh + 1) * 512], ps[bi][0:16, :], ALU.add, ['xs_t', psk[bi]], [('x1_s', hh), 'x1_s'])
                    X1 = [('x1_s', 0), ('x1_s', 1)]
                    rms16(x1_s[:], X1, pr3[:, 3072:4096], ['pr3'], hn_s[:], ['hn_s'])
                    cp('act', hn_bf[:], hn_s[:], ['hn_s'], ['hn_bf'])
                    tr16(hn_bf, ['hn_bf'], hnT_sP, 'hnT_sP', 8)
                    S.barrier()
            S.barrier()

        x1_sP = SB(st, "x1_sP", [16, D])
        hnT_sP = SB(st, "hnT_sP", [128, 8, 16], BF16)
        if with_sample:
            sample_phase()
        xnT = SB(st, "xnT", [128, 8, TB], BF16)
        yrT = SB(st, "yrT", [128, 4, TB], BF16)
        ymT = SB(st, "ymT", [128, 4, TB], BF16)
        nofinal = False
        if stop is not None and stop.endswith('!'):
            nofinal = True
            stop = stop[:-1]
        try:
          chk('C')
          for blk in range(nblk):
              t0 = blk * TB
              last_blk = (blk == NBLK - 1)
              sr = contextlib.ExitStack()
              QKT = SB(sr, "QKT_%d" % blk, [128, 4, TB], BF16)
              RtT = SB(sr, "RtT_%d" % blk, [128, 4, TB], BF16)
              KtT = SB(sr, "KtT_%d" % blk, [128, 4, TB], BF16)
              nBtT = SB(sr, "nBtT_%d" % blk, [128, 4, TB], BF16)
              vT_bf = SB(sr, "vTbf_%d" % blk, [128, 4, TB], BF16)
              rkT_bf = SB(sr, "rkTbf_%d" % blk, [128, 4, TB], BF16)
              eL = SB(sr, "eL_%d" % blk, [128, 4, 4])
              eLp = SB(sr, "eLp_%d" % blk, [128, 4, 4])
              emid = SB(sr, "emid_%d" % blk, [128, 4, 4])
              Kt_tok = SB(sr, "Kttok_%d" % blk, [128, 4, 512], BF16)
              nBt_tok = SB(sr, "nBttok_%d" % blk, [128, 4, 512], BF16)
              V_tok = SB(sr, "Vtok_%d" % blk, [128, 4, 512], BF16)
              sgx_bf = SB(sr, "sgxbf_%d" % blk, [128, TB], BF16)
              sa = contextlib.ExitStack()
              if True:
                  xld = SB(sa, "xld_%d" % blk, [128, 4, D])
                  xs_bf = [SB(sa, "xsbf%d_%d" % (i, blk), [128, D], BF16) for i in range(2)]
                  junk = SB(sa, "junkA_%d" % blk, [128, D], BF16)
                  ssA = SB(sa, "ssA_%d" % blk, [128, 4])
                  S.op('pool', lambda en: en.memset(ssA[:], 0.0), writes=[('ssA', i) for i in range(4)])
                  for tt_ in range(4):
                      S.dma('sp', xld[:, tt_, :], x_d[t0 + tt_ * 128:t0 + (tt_ + 1) * 128, :], writes=[('xld', tt_)])
                      act(junk[:], xld[:, tt_, :], AF.Square, [('xld', tt_)], ['junkA', ('ssA', tt_)], accum=ssA[:, tt_:tt_ + 1])
                  ts('dve', rs_in[:, 0:4], ssA[:], 1.0 / D, 1e-6, ALU.mult, ALU.add, [('ssA', i) for i in range(4)], ['rs_in'])
                  rsqrt(4, [])
                  for tt_ in range(4):
                      xb_ = xs_bf[tt_ % 2]
                      xk = 'xsbf%d' % (tt_ % 2)
                      act(xb_[:], xld[:, tt_, :], AF.Copy, [('xld', tt_), 'rs_out'], [xk], scale=rs_out[:, tt_:tt_ + 1])
                      bi = nb()
                      psb = ps[bi][:].bitcast(BF16)
                      for kc in range(8):
                          tr(psb[:, kc * 128:(kc + 1) * 128], xb_[:, kc * 128:(kc + 1) * 128], ident_bf[:], [xk, 'ident_bf'], [psk[bi]])
                      tt('dve', xnT[:, :, tt_ * 128:(tt_ + 1) * 128], psb.rearrange("p (k t) -> p k t", k=8),
                         bc_in(pp[:, P_GMIX:P_GMIX + 8], 128), ALU.mult, [psk[bi], 'pp'], [('xnT', tt_)])
                  if last_blk:
                      xr = SB(sa, "xr", [1, D])
                      gr_ = SB(sa, "gr_", [1, D])
                      jr = SB(sa, "jr", [1, D])
                      ssr = SB(sa, "ssr", [1, 1])
                      S.op('pool', lambda en: en.memset(ssr[:], 0.0), writes=['ssr'])
                      S.dma('sp', xr[:], x_d[T - 1:T, :], writes=['xr'])
                      S.dma('sp', gr_[:], gmixrow_d[0:1, :], writes=['gr_'])
                      act(jr[:], xr[:], AF.Square, ['xr'], ['jr', 'ssr'], accum=ssr[:, 0:1])
                      ts('dve', rs_in[0:1, 0:1], ssr[:], 1.0 / D, 1e-6, ALU.mult, ALU.add, ['ssr', 'rs_out'], ['rs_in'])
                      rsqrt(1, [])
                      stt('dve', jr[:], xr[:], rs_out[0:1, 0:1], gr_[:], ALU.mult, ALU.mult, ['xr', 'rs_out', 'gr_'], ['jr'])
                      S.dma('sp', shift_d[0:1, :], jr[:], reads=['jr'], writes=['o_shift'])
              XN = [('xnT', i) for i in range(4)]
              chk('A%d' % blk)
              if blk == 0:
                  dbg("xnT", xnT[:, :, :], [128, 8, TB], XN)

              if True:
                  sp_ = contextlib.ExitStack()
                  TMPN = ('rT', 'kT', 'vT', 'sig', 'cs', 'EN', 'EP', 'EPX', 'aT', 'd2', 'tA', 'tB')
                  TS = []
                  for par in range(2):
                      d_ = {nm: SB(sp_, "%s%d_%d" % (nm, par, blk), [128, TB]) for nm in TMPN}
                      d_['sq_bf'] = SB(sp_, "sqbf%d_%d" % (par, blk), [128, TB], BF16)
                      d_['bia'] = SB(sp_, "bia%d_%d" % (par, blk), [128, 8])
                      d_['sbuf_s'] = [SB(sp_, "sbufs%d_%d_%d" % (i, par, blk), [128, TB + 1]) for i in range(2)]
                      TS.append(d_)
                  lat0 = SB(sp_, "lat0_%d" % blk, [128, TB])
                  lat1 = SB(sp_, "lat1_%d" % blk, [128, TB])
                  lat0_bf = SB(sp_, "lat0bf_%d" % blk, [128, TB], BF16)

                  def mix_evac(bi, piece, out_ap, okey, sb_, sk):
                      cp('act', sb_[:, 0:1], carry[:, piece:piece + 1], [('carry', piece)], [(sk, 0)])
                      act(sb_[:, 1:TB + 1], ps[bi][:], AF.Copy, [psk[bi], 'pp'], [(sk, 1)], scale=pp[:, P_MU + piece:P_MU + piece + 1])
                      stt('dve', out_ap, ps[bi][:], pq[:, Q_OMU + piece:Q_OMU + piece + 1], sb_[:, 0:TB], ALU.mult, ALU.add,
                          [psk[bi], 'pq', (sk, 0), (sk, 1)], [okey])
                      cp('act', carry[:, piece:piece + 1], sb_[:, TB:TB + 1], [(sk, 1)], [('carry', piece)])

                  wl, wk_ = w_get('lat')
                  bl = [nb(), nb()]
                  for pc in range(2):
                      for kc in range(8):
                          mm(ps[bl[pc]][:], wl[:, kc, pc * 128:(pc + 1) * 128], xnT[:, kc, :], kc == 0, kc == 7, [wk_] + XN, [psk[bl[pc]]])
                  w_done()
                  mix_evac(bl[0], 0, lat0[:], 'lat0', TS[0]['sbuf_s'][0], ('sbufs', 0, 0))
                  mix_evac(bl[1], 1, lat1[:], 'lat1', TS[0]['sbuf_s'][1], ('sbufs', 0, 1))
                  act(lat0_bf[0:64, :], lat0[0:64, :], AF.Tanh, ['lat0'], [('lat0bf', 0)])
                  cp('dve', lat0_bf[64:128, :], lat0[64:128, :], ['lat0'], [('lat0bf', 1)])
                  act(TS[0]['tA'][:], lat1[:], AF.Tanh, ['lat1'], [('tA', 0)], scale=0.5)
                  fix05(sgx_bf[:], TS[0]['tA'][:], [('tA', 0)], ['sgxbf'])
                  chk('Rlat%d' % blk)

                  def prep_j(j, par):
                      B_ = TS[par]
                      rT, kT, vT, sig, cs, EN, EP, EPX, aT, d2, tA, tB_ = (B_[n_] for n_ in TMPN)
                      sq_bf, bia, sbs = B_['sq_bf'], B_['bia'], B_['sbuf_s']
                      K = lambda nm: (nm, par)
                      wj, wkj = w_get('r%d' % j)
                      br = [nb(), nb(), nb()]
                      for pc in range(3):
                          for kc in range(8):
                              mm(ps[br[pc]][:], wj[:, kc, pc * 128:(pc + 1) * 128], xnT[:, kc, :], kc == 0, kc == 7, [wkj] + XN, [psk[br[pc]]])
                      w_done()
                      mix_evac(br[0], 2 + 3 * j, rT[:], K('rT'), sbs[0], ('sbufs', par, 0))
                      mix_evac(br[1], 3 + 3 * j, kT[:], K('kT'), sbs[1], ('sbufs', par, 1))
                      mix_evac(br[2], 4 + 3 * j, vT[:], K('vT'), sbs[0], ('sbufs', par, 0))
                      yield
                      bw, ba = nb(), nb()
                      mm(ps[bw][:], w2a2_bf[0:64, j * 128:(j + 1) * 128], lat0_bf[0:64, :], True, True, ['w2a2_bf', ('lat0bf', 0)], [psk[bw]])
                      mm(ps[ba][:], w2a2_bf[64:128, j * 128:(j + 1) * 128], lat0_bf[64:128, :], True, True, ['w2a2_bf', ('lat0bf', 1)], [psk[ba]])
                      act(tA[:], ps[bw][:], AF.Tanh, [psk[bw], 'pq'], [K('tA')], bias=pq[:, Q_HW0 + j:Q_HW0 + j + 1], scale=0.5)
                      act(tB_[:], ps[ba][:], AF.Tanh, [psk[ba], 'pq'], [K('tB')], bias=pq[:, Q_HA0 + j:Q_HA0 + j + 1], scale=0.5)
                      act(sq_bf[:], kT[:], AF.Square, [K('kT'), 'pp'], [K('sqbf')], scale=pp[:, P_RKK + j:P_RKK + j + 1])
                      bs_ = nb()
                      mm(ps[bs_][:], bones[:], sq_bf[:], True, True, ['bones', K('sqbf')], [psk[bs_]])
                      fix05(sig[:], tA[:], [K('tA')], [K('sig')])
                      fix05(aT[:], tB_[:], [K('tB')], [K('aT')])
                      act(d2[:], ps[bs_][:], AF.Identity, [psk[bs_], 'eps_c'], [K('d2')], bias=eps_c[:, 0:1])
                      yield
                      for c in range(4):
                          S.op('dve', lambda en, c=c: en.tensor_tensor_scan(out=cs[:, c * 128:(c + 1) * 128], data0=ones_f[:, :], data1=sig[:, c * 128:(c + 1) * 128],
                                                                       initial=0.0, op0=ALU.mult, op1=ALU.add), reads=[K('sig'), 'ones_f'], writes=[('cs', par, c)])
                      CSK = [('cs', par, c) for c in range(4)]
                      mids = cs[:].rearrange("p (c t) -> p c t", c=4)[:, :, 63]
                      lasts = cs[:].rearrange("p (c t) -> p c t", c=4)[:, :, 127]
                      ts('dve', bia[:, 0:4], mids, CC, None, ALU.mult, None, CSK, [('bia', par, 0)])
                      ts('dve', bia[:, 4:8], mids, -CC, None, ALU.mult, None, CSK, [('bia', par, 1)])
                      S.op('dve', lambda en: en.reciprocal(out=d2[:], in_=d2[:]), reads=[K('d2')], writes=[K('d2')])
                      yield
                      for c in range(4):
                          sl_ = slice(c * 128, (c + 1) * 128)
                          act(EN[:, sl_], cs[:, sl_], AF.Exp, CSK + [('bia', par, 1)], [('EN', par, c)], bias=bia[:, 4 + c:5 + c], scale=CC)
                          act(EPX[:, c * 128 + 1:(c + 1) * 128], cs[:, c * 128:(c + 1) * 128 - 1], AF.Exp, CSK + [('bia', par, 0)], [('EPX', par, c)], bias=bia[:, c:c + 1], scale=-CC)
                          act(EP[:, sl_], cs[:, sl_], AF.Exp, CSK + [('bia', par, 0)], [('EP', par, c)], bias=bia[:, c:c + 1], scale=-CC)
                          if c == 1:
                              yield
                      ENK = [('EN', par, c) for c in range(4)]
                      EPK = [('EP', par, c) for c in range(4)]
                      EPXK = [('EPX', par, c) for c in range(4)]
                      act(eL[:, j, :], lasts, AF.Exp, CSK, [('eL', j)], scale=-CC)
                      act(emid[:, j, :], mids, AF.Exp, CSK, [('emid', j)], scale=-CC)
                      act(EPX[:].rearrange("p (c t) -> p c t", c=4)[:, :, 0], mids, AF.Exp, CSK + [('EPX', par, c_) for c_ in range(4)], [('EPX', par, c_) for c_ in range(4)], scale=CC)
                      cp('act', eLp[:, j, :], EP[:].rearrange("p (c t) -> p c t", c=4)[:, :, 127], EPK, [('eLp', j)])
                      yield
                      stt('dve', tA[:], kT[:], pp[:, P_RKK + j:P_RKK + j + 1], EPX[:], ALU.mult, ALU.mult, [K('kT'), 'pp'] + EPXK, [K('tA')])
                      tt('dve', QKT[:, j, :], tA[:], d2[:], ALU.mult, [K('tA'), K('d2')], [('QKT', j)])
                      stt('dve', tB_[:], kT[:], pp[:, P_RKK + j:P_RKK + j + 1], aT[:], ALU.mult, ALU.mult, [K('kT'), 'pp', K('aT')], [K('tB')])
                      stt('dve', nBtT[:, j, :], tB_[:], -1.0, EN[:], ALU.mult, ALU.mult, [K('tB')] + ENK, [('nBtT', j)])
                      yield
                      act(tA[:], aT[:], AF.Identity, [K('aT'), 'pp', 'pq'], [K('tA')], bias=pq[:, Q_OMKA + j:Q_OMKA + j + 1], scale=pp[:, P_KA + j:P_KA + j + 1])
                      tt('dve', tA[:], tA[:], kT[:], ALU.mult, [K('tA'), K('kT')], [K('tA')])
                      tt('dve', KtT[:, j, :], tA[:], EN[:], ALU.mult, [K('tA')] + ENK, [('KtT', j)])
                      tt('dve', RtT[:, j, :], rT[:], EP[:], ALU.mult, [K('rT')] + EPK, [('RtT', j)])
                      stt('dve', rkT_bf[:, j, :], rT[:], pp[:, P_RRK + j:P_RRK + j + 1], tA[:], ALU.mult, ALU.mult, [K('rT'), 'pp', K('tA')], [('rkT', j)])
                      cp('act', vT_bf[:, j, :], vT[:], [K('vT')], [('vTbf', j)])

                  gens = [prep_j(j, j % 2) for j in range(4)]
                  SKEW = 3
                  active, prog, nxt_g = [], {}, 0
                  while nxt_g < 4 or active:
                      if len(active) < 2 and nxt_g < 4 and (not active or prog[active[0]] >= SKEW):
                          active.append(nxt_g)
                          prog[nxt_g] = 0
                          nxt_g += 1
                      for gi in list(active):
                          try:
                              next(gens[gi])
                              prog[gi] += 1
                          except StopIteration:
                              active.remove(gi)
                  S.barrier()
                  sp_.close()
                  sa.close()

                  chk('Rprep%d' % blk)
                  QK_K = [('QKT', j) for j in range(4)]
                  RT_K = [('RtT', j) for j in range(4)]
                  KT_K = [('KtT', j) for j in range(4)]
                  BT_K = [('nBtT', j) for j in range(4)]
                  for c in range(4):
                      for (src, dst, sk, dk) in ((KtT, Kt_tok, KT_K, 'Kttok'), (nBtT, nBt_tok, BT_K, 'nBttok'), (vT_bf, V_tok, [('vTbf', j) for j in range(4)], 'Vtok')):
                          bi = nb()
                          psb = ps[bi][:].bitcast(BF16)
                          for j in range(4):
                              tr(psb[:, j * 128:(j + 1) * 128], src[:, j, c * 128:(c + 1) * 128], ident_bf[:], sk + ['ident_bf'], [psk[bi]])
                          cp('act', dst[:, c, :], psb[:, 0:512], [psk[bi]], [(dk, c)])

                  chk('Rtr%d' % blk)
                  smm = contextlib.ExitStack()
                  xmT = SB(smm, "xmT_%d" % blk, [128, 4, TB + 3])
                  xcT = SB(smm, "xcT_%d" % blk, [128, 4, TB])
                  xcT_bf = SB(smm, "xcTbf_%d" % blk, [128, 4, TB], BF16)
                  soT = SB(smm, "soT_%d" % blk, [128, 4, TB], BF16)
                  qT_bf = SB(smm, "qTbf_%d" % blk, [128, 4, TB], BF16)
                  kT_bf = SB(smm, "kTbf_%d" % blk, [128, 4, TB], BF16)
                  k_tok = SB(smm, "ktok_%d" % blk, [128, 4, 512], BF16)
                  Vp = SB(smm, "Vp_%d" % blk, [128, 4, 4, 129], BF16)
                  mA = SB(smm, "mA_%d" % blk, [128, TB])
                  mB = SB(smm, "mB_%d" % blk, [128, TB])
                  gtok = SB(smm, "gtok_%d" % blk, [128, 4, 8])
                  gT_i = SB(smm, "gTi_%d" % blk, [4, TB])
                  gT_s = SB(smm, "gTs_%d" % blk, [4, TB])
                  gT_cp = SB(smm, "gTcp_%d" % blk, [4, TB])
                  gT_eb = SB(smm, "gTeb_%d" % blk, [4, TB])
                  gT_M = SB(smm, "gTM_%d" % blk, [4, TB])
                  gtk = SB(smm, "gtk_%d" % blk, [128, 4, 2, 4])
                  cpLb = SB(smm, "cpLb_%d" % blk, [128, 4, 4])
                  dg4 = SB(smm, "dg4_%d" % blk, [4, 4, 4])
                  ST_bf = SB(smm, "STbf_%d" % blk, [128, 512], BF16)
                  numS = SB(smm, "numS_%d" % blk, [128, 4, 129])
                  hS = SB(smm, "hS_%d" % blk, [128, 512])
                  hsq = SB(smm, "hsq_%d" % blk, [128, 512])
                  hn_bf = SB(smm, "hnbf_%d" % blk, [128, 512], BF16)
                  den4 = SB(smm, "den4_%d" % blk, [128, 4])
                  ms1 = SB(smm, "ms1_%d" % blk, [128, 4])
                  ms2 = SB(smm, "ms2_%d" % blk, [128, 4])
                  rrM = [0]

                  def nbM():
                      rrM[0] = (rrM[0] + 1) % 3
                      return 5 + rrM[0]

                  def m_gen():

                      cp('act', xmT[:, :, 0:3], xmc[:, :, :], ['xmc'], [('xmT', 'c')])
                      wx, wxk = w_get('xm')
                      for jj in range(4):
                          bi = nbM()
                          for kc in range(8):
                              mm(ps[bi][:], wx[:, kc, jj * 128:(jj + 1) * 128], xnT[:, kc, :], kc == 0, kc == 7, [wxk] + XN, [psk[bi]])
                          cp('act', xmT[:, jj, 3:TB + 3], ps[bi][:], [psk[bi]], [('xmT', jj)])
                          yield
                      w_done()
                      yield
                      XMK = [('xmT', 'c')] + [('xmT', jj) for jj in range(4)]
                      if last_blk:
                          for jj in range(4):
                              S.dma('sp', conv_d[:, jj * 128:(jj + 1) * 128].rearrange("t p -> p t"), xmT[:, jj, TB:TB + 3], reads=XMK, writes=[('o_conv', jj)])
                      cp('act', xmc[:, :, :], xmT[:, :, TB:TB + 3], XMK, ['xmc'])
                      for jj in range(4):
                          cw = lambda tap: pp[:, P_CW + tap * 4 + jj:P_CW + tap * 4 + jj + 1]
                          ts('dve', mA[:], xmT[:, jj, 0:TB], cw(0), pp[:, P_CB + jj:P_CB + jj + 1], ALU.mult, ALU.add, XMK + ['pp'], ['mA'])
                          for tap in (1, 2, 3):
                              stt('dve', mA[:], xmT[:, jj, tap:tap + TB], cw(tap), mA[:], ALU.mult, ALU.add, XMK + ['pp', 'mA'], ['mA'])
                          act(mB[:], mA[:], AF.Tanh, ['mA'], ['mB'], scale=0.5)
                          fix05(mB[:], mB[:], ['mB'], ['mB'])
                          tt('dve', xcT[:, jj, :], mA[:], mB[:], ALU.mult, ['mA', 'mB'], [('xcT', jj)])
                          cp('act', xcT_bf[:, jj, :], xcT[:, jj, :], [('xcT', jj)], [('xcTbf', jj)])
                          yield
                      XCB = [('xcTbf', jj) for jj in range(4)]
                      for hh in range(4):
                          bi = nbM()
                          mm(ps[bi][:], wq_bf[:, hh, :], xcT_bf[:, hh, :], True, True, ['wq_bf'] + XCB, [psk[bi]])
                          act(qT_bf[:, hh, :], ps[bi][:], AF.Copy, [psk[bi]], [('qTbf', hh)], scale=128 ** -0.5)
                          bi = nbM()
                          mm(ps[bi][:], wk_bf[:, hh, :], xcT_bf[:, hh, :], True, True, ['wk_bf'] + XCB, [psk[bi]])
                          cp('dve', kT_bf[:, hh, :], ps[bi][:], [psk[bi]], [('kTbf', hh)])
                          yield
                      for c in range(4):
                          bi = nbM()
                          for hh in range(4):
                              mm(ps[bi][:, hh * 128:(hh + 1) * 128], xcT_bf[:, hh, c * 128:(c + 1) * 128], wk_bf[:, hh, :], True, True, ['wk_bf'] + XCB, [psk[bi]])
                          cp('act', k_tok[:, c, :], ps[bi][:], [psk[bi]], [('ktok', c)])
                          yield
                      wg, wgk = w_get('gt')
                      b_i, b_f = nbM(), nbM()
                      for kc in range(8):
                          mm(ps[b_i][0:4, :], wg[:, kc, 0:4], xnT[:, kc, :], kc == 0, kc == 7, [wgk] + XN, [psk[b_i]])
                      for kc in range(8):
                          mm(ps[b_f][0:4, :], wg[:, kc, 4:8], xnT[:, kc, :], kc == 0, kc == 7, [wgk] + XN, [psk[b_f]])
                      w_done()
                      yield
                      act(gT_i[:], ps[b_i][0:4, :], AF.Exp, [psk[b_i], 'gbf'], ['gTi'], bias=gbf[:, 0:1])
                      act(gT_s[:], ps[b_f][0:4, :], AF.Tanh, [psk[b_f], 'hfb'], ['gTs'], bias=hfb[:, 0:1], scale=0.5)
                      ts('dve', gT_s[:], gT_s[:], 0.5, 0.5, ALU.mult, ALU.add, ['gTs'], ['gTs'])
                      for c in range(4):
                          sl_ = slice(c * 128, (c + 1) * 128)
                          S.op('dve', lambda en, sl_=sl_: en.tensor_tensor_scan(out=gT_cp[:, sl_], data0=gT_s[:, sl_], data1=zeros_f[0:4, :], initial=1.0,
                                                                       op0=ALU.mult, op1=ALU.add), reads=['gTs', 'zeros_f'], writes=[('gTcp', c)])
                      CPK = [('gTcp', c) for c in range(4)]
                      S.op('dve', lambda en: en.tensor_tensor_scan(out=gT_M[:], data0=gT_s[:], data1=gT_i[:], initial=Mst[:, 0:1],
                                                                   op0=ALU.mult, op1=ALU.max), reads=['gTs', 'gTi', 'Mst'], writes=['gTM'])
                      cp('dve', Mst[:, 0:1], gT_M[:, TB - 1:TB], ['gTM'], ['Mst'])
                      S.op('dve', lambda en: en.reciprocal(out=gT_eb[:], in_=gT_cp[:]), reads=CPK, writes=['gTeb'])
                      tt('dve', gT_eb[:], gT_eb[:], gT_i[:], ALU.mult, ['gTeb', 'gTi'], ['gTeb'])
                      b_g1 = nbM()
                      for c in range(4):
                          sl_ = slice(c * 128, (c + 1) * 128)
                          tr(ps[b_g1][:, c * 8:c * 8 + 4], gT_cp[:, sl_], ident_f[0:4, 0:4], CPK + ['ident_f'], [psk[b_g1]])
                          tr(ps[b_g1][:, c * 8 + 4:c * 8 + 8], gT_eb[:, sl_], ident_f[0:4, 0:4], ['gTeb', 'ident_f'], [psk[b_g1]])
                      cp('dve', gtk[:].rearrange("p c a h -> p (c a h)"), ps[b_g1][:, 0:32], [psk[b_g1]], ['gtk'])
                      for c in range(4):
                          ts('dve', dg4[:, c, :], ident_f[0:4, 0:4], gT_cp[:, c * 128 + 127:c * 128 + 128], None, ALU.mult, None, CPK + ['ident_f'], [('dg4', c)])
                      b_g2 = nbM()
                      mm(ps[b_g2][:, 0:16], ones_f[0:4, :], dg4[:].rearrange("p c h -> p (c h)"), True, True, ['ones_f'] + [('dg4', c) for c in range(4)], [psk[b_g2]])
                      cp('dve', cpLb[:].rearrange("p c h -> p (c h)"), ps[b_g2][:, 0:16], [psk[b_g2]], ['cpLb'])
                      wv, wvk = w_get('mv')
                      for c in range(4):
                          bi = nbM()
                          for kc in range(8):
                              mm(ps[bi][:], xnT[:, kc, c * 128:(c + 1) * 128], wv[:, kc, :], kc == 0, kc == 7, [wvk] + XN, [psk[bi]])
                          tt('dve', Vp[:, c, :, 0:128], ps[bi][:].rearrange("p (h e) -> p h e", h=4), bc_in(gtk[:, c, 1, :], 128), ALU.mult, [psk[bi], 'gtk'], [('Vp', c)])
                          cp('act', Vp[:, c, :, 128], gtk[:, c, 1, :], ['gtk', ('Vp', c)], [('Vp', c)])
                          yield
                      w_done()
                      yield
                      wo_, wok = w_get('o')
                      for jj in range(4):
                          bi = nbM()
                          for kc in range(8):
                              mm(ps[bi][:], wo_[:, kc, jj * 128:(jj + 1) * 128], xnT[:, kc, :], kc == 0, kc == 7, [wok] + XN, [psk[bi]])
                          act(soT[:, jj, :], ps[bi][:], AF.Tanh, [psk[bi]], [('soT', jj)], scale=0.5)
                          fix05(soT[:, jj, :], soT[:, jj, :], [('soT', jj)], [('soT', jj)])
                          yield
                      w_done()
                      yield
                      QTK = [('qTbf', hh) for hh in range(4)]
                      KTK = [('kTbf', hh) for hh in range(4)]
                      for c in range(4):
                          csl = slice(c * 128, (c + 1) * 128)
                          b_s = nbM()
                          for hh in range(4):
                              mm(ps[b_s][:, hh * 128:(hh + 1) * 128], kT_bf[:, hh, csl], qT_bf[:, hh, csl], True, True, QTK + KTK, [psk[b_s]])
                          tt('dve', ST_bf[:].rearrange("p (a b) -> p a b", a=4), ps[b_s][:].rearrange("p (a b) -> p a b", a=4), bc_mid(ui_f[:], 4), ALU.mult, [psk[b_s], 'ui_f'], ['STbf'])
                          yield
                          b_n = [nbM(), nbM()]
                          for hh in range(4):
                              o_ = ps[b_n[hh // 2]][:, (hh % 2) * 129:(hh % 2) * 129 + 129]
                              mm(o_, ST_bf[:, hh * 128:(hh + 1) * 128], Vp[:, c, hh, :], True, False, ['STbf', ('Vp', c)], [psk[b_n[hh // 2]]])
                              mm(o_, qT_bf[:, hh, csl], Caug_bf[:, hh, :], False, True, QTK + ['Caug_bf'], [psk[b_n[hh // 2]]])
                          for g in range(2):
                              tt('dve', numS[:, 2 * g:2 * g + 2, :], ps[b_n[g]][:, 0:258].rearrange("p (h e) -> p h e", h=2), bc_in(gtk[:, c, 0, 2 * g:2 * g + 2], 129), ALU.mult,
                                 [psk[b_n[g]], 'gtk'], [('numS', g)])
                          NK = [('numS', 0), ('numS', 1)]
                          stt('dve', den4[:], numS[:, :, 128], -1.0, numS[:, :, 128], ALU.mult, ALU.max, NK, ['den4'])
                          ts('dve', den4[:], den4[:], 1.0, None, ALU.max, None, ['den4'], ['den4'])
                          S.op('dve', lambda en: en.reciprocal(out=den4[:], in_=den4[:]), reads=['den4'], writes=['den4'])
                          tt('dve', hS[:].rearrange("p (h e) -> p h e", h=4), numS[:, :, 0:128], bc_in(den4[:], 128), ALU.mult, NK + ['den4'], ['hS'])
                          yield
                          if blk == 0 and c == 0:
                              dbg("hS_c0", hS[:], [128, 512], ['hS'])
                          b_c = [nbM(), nbM()]
                          for hh in range(4):
                              o_ = ps[b_c[hh // 2]][:, (hh % 2) * 129:(hh % 2) * 129 + 129]
                              mm(o_, k_tok[:, c, hh * 128:(hh + 1) * 128], Vp[:, c, hh, :], True, True, [('ktok', c), ('Vp', c)], [psk[b_c[hh // 2]]])
                          for g in range(2):
                              tt('dve', Caug[:, 2 * g:2 * g + 2, :], Caug[:, 2 * g:2 * g + 2, :], ps[b_c[g]][:, 0:258].rearrange("p (h e) -> p h e", h=2), ALU.add,
                                 ['Caug', psk[b_c[g]]], ['Caug'])
                          tt('dve', Caug[:], Caug[:], bc_in(cpLb[:, c, :], 129), ALU.mult, ['Caug', 'cpLb'], ['Caug'])
                          cp('act', Caug_bf[:], Caug[:], ['Caug'], ['Caug_bf'])
                          yield
                          S.op('dve', lambda en: en.tensor_reduce(out=ms1[:], in_=hS[:].rearrange("p (h e) -> p h e", h=4), axis=AX.X, op=ALU.add), reads=['hS'], writes=['ms1'])
                          act(hsq[:], hS[:], AF.Square, ['hS'], ['hsq'])
                          S.op('dve', lambda en: en.tensor_reduce(out=ms2[:], in_=hsq[:].rearrange("p (h e) -> p h e", h=4), axis=AX.X, op=ALU.add), reads=['hsq'], writes=['ms2'])
                          ts('dve', ms1[:], ms1[:], 1.0 / 128, None, ALU.mult, None, ['ms1'], ['ms1'])
                          tt('dve', rsB_in[:, 0:4], ms1[:], ms1[:], ALU.mult, ['ms1', 'rsB_out'], ['rsB_in'])
                          stt('dve', rsB_in[:, 0:4], ms2[:], 1.0 / 128, rsB_in[:, 0:4], ALU.mult, ALU.subtract, ['ms2', 'rsB_in'], ['rsB_in'])
                          ts('dve', rsB_in[:, 0:4], rsB_in[:, 0:4], 1e-5, None, ALU.add, None, ['rsB_in'], ['rsB_in'])
                          rsqrt(4, [], bgset=True)
                          tt('dve', hsq[:].rearrange("p (h e) -> p h e", h=4), hS[:].rearrange("p (h e) -> p h e", h=4), bc_in(ms1[:], 128), ALU.subtract, ['hS', 'ms1', 'hsq'], ['hsq'])
                          tt('dve', hn_bf[:].rearrange("p (h e) -> p h e", h=4), hsq[:].rearrange("p (h e) -> p h e", h=4), bc_in(rsB_out[:, 0:4], 128), ALU.mult, ['hsq', 'rsB_out'], ['hnbf'])
                          yield
                          b_t = nbM()
                          psb = ps[b_t][:].bitcast(BF16)
                          for hh in range(4):
                              tr(psb[:, hh * 128:(hh + 1) * 128], hn_bf[:, hh * 128:(hh + 1) * 128], ident_bf[:], ['hnbf', 'ident_bf'], [psk[b_t]])
                          v4 = lambda ap: ap.rearrange("p (j t) -> p j t", j=4)
                          tt('dve', v4(mA[:]), v4(psb[:, 0:512]), bc_in(pp[:, P_MGN:P_MGN + 4], 128), ALU.mult, [psk[b_t], 'pp'], ['mA'])
                          tt('dve', v4(mB[:]), xcT[:, :, csl], bc_in(pp[:, P_MSK:P_MSK + 4], 128), ALU.mult, [('xcT', jj) for jj in range(4)] + ['pp'], ['mB'])
                          tt('dve', mA[:], mA[:], mB[:], ALU.add, ['mA', 'mB'], ['mA'])
                          tt('dve', ymT[:, :, csl], v4(mA[:]), soT[:, :, csl], ALU.mult, ['mA'] + [('soT', jj) for jj in range(4)], [('ymT', c)])
                          yield
                      yield

                  gM = [m_gen()]

                  with contextlib.ExitStack() as sm:
                      MTs, MTp = {}, {}
                      for hf in range(2):
                          for nm in ('PA0', 'PA1', 'PB0', 'PB1', 'T0', 'T1'):
                              MTs[(nm, hf)] = SB(sm, "%s%d_%d" % (nm, hf, blk), [128, 512], BF16)
                          for par in range(2):
                              for nm in ('AkvT', 'ArkT', 'nArbT', 'Tf'):
                                  MTp[(nm, hf, par)] = SB(sm, "%s%d%d_%d" % (nm, hf, par, blk), [128, 512], BF16)
                      RHS_sb = SB(sm, "RHSsb_%d" % blk, [128, 512], BF16)
                      U_sb = SB(sm, "Usb_%d" % blk, [128, 512], BF16)
                      Htmp = SB(sm, "Htmp_%d" % blk, [128, 256])
                      y_sb = SB(sm, "ysb_%d" % blk, [128, 512])
                      yn_bf = SB(sm, "ynbf_%d" % blk, [128, 512], BF16)
                      st1 = SB(sm, "st1_%d" % blk, [128, 8])
                      st2 = SB(sm, "st2_%d" % blk, [128, 8])
                      y2 = SB(sm, "y2_%d" % blk, [128, 512])
                      y3 = SB(sm, "y3_%d" % blk, [128, 512])
                      rrA, rrB = [0], [0]

                      def nbA():
                          rrA[0] = (rrA[0] + 1) % 2
                          return rrA[0]

                      def nbB():
                          rrB[0] = (rrB[0] + 1) % 3
                          return 2 + rrB[0]

                      def stepM():
                          if gM[0] is not None:
                              try:
                                  next(gM[0])
                              except StopIteration:
                                  gM[0] = None

                      def gen_mat(c):
                          par = c % 2
                          csl = slice(c * 128, (c + 1) * 128)

                          def hv(tile_, h):
                              jj, base = h // 2, (h % 2) * 64
                              return tile_[base:base + 64, jj, csl]
                          for hf in range(2):
                              heads = [2 * i + hf for i in range(4)]
                              specs = (('PB0', nBtT, QKT, su_f, 'su_f', BT_K + QK_K),
                                       ('PA0', QKT, nBtT, sl_f, 'sl_f', BT_K + QK_K),
                                       ('AkvT', KtT, QKT, su_f, 'su_f', KT_K + QK_K),
                                       ('ArkT', KtT, RtT, ui_f, 'ui_f', KT_K + RT_K),
                                       ('nArbT', nBtT, RtT, ui_f, 'ui_f', BT_K + RT_K))
                              for (nm, lt, rt, mask, mkk, rk_) in specs:
                                  if nm in ('PA0', 'PB0'):
                                      dst, dk = MTs[(nm, hf)], ('M', nm, hf)
                                  else:
                                      dst, dk = MTp[(nm, hf, par)], ('M', nm, hf, par)
                                  bi = nbA()
                                  for i, h in enumerate(heads):
                                      mm(ps[bi][:, i * 128:(i + 1) * 128], hv(lt, h), hv(rt, h), True, True, rk_, [psk[bi]])
                                  tt('dve', dst[:].rearrange("p (a b) -> p a b", a=4), ps[bi][:].rearrange("p (a b) -> p a b", a=4),
                                     bc_mid(mask[:], 4), ALU.mult, [psk[bi], mkk], [dk])
                                  yield
                              tt('dve', MTs[('T0', hf)][:].rearrange("p (a b) -> p a b", a=4), MTs[('PB0', hf)][:].rearrange("p (a b) -> p a b", a=4),
                                 bc_mid(ident_f[:], 4), ALU.add, [('M', 'PB0', hf), 'ident_f'], [('M', 'T0', hf)])
                          cur = 0
                          for lvl in range(6):
                              nxt = 1 - cur
                              lastl = (lvl == 5)
                              for hf in range(2):
                                  mk = lambda nm: ('M', nm, hf)
                                  PAc, PBc = MTs[('PA%d' % cur, hf)], MTs[('PB%d' % cur, hf)]
                                  ba_ = nbA()
                                  for i in range(4):
                                      sl_ = slice(i * 128, (i + 1) * 128)
                                      mm(ps[ba_][:, sl_], PBc[:, sl_], PAc[:, sl_], True, True, [mk('PA%d' % cur), mk('PB%d' % cur)], [psk[ba_]])
                                  cp('act', MTs[('PA%d' % nxt, hf)][:], ps[ba_][:], [psk[ba_]], [mk('PA%d' % nxt)])
                                  if not lastl:
                                      bb_ = nbA()
                                      for i in range(4):
                                          sl_ = slice(i * 128, (i + 1) * 128)
                                          mm(ps[bb_][:, sl_], PAc[:, sl_], PBc[:, sl_], True, True, [mk('PA%d' % cur), mk('PB%d' % cur)], [psk[bb_]])
                                      cp('act', MTs[('PB%d' % nxt, hf)][:], ps[bb_][:], [psk[bb_]], [mk('PB%d' % nxt)])
                                  yield
                              for hf in range(2):
                                  mk = lambda nm: ('M', nm, hf)
                                  PAn = MTs[('PA%d' % nxt, hf)]
                                  Tc = MTs[('T%d' % cur, hf)]
                                  if lastl:
                                      Tn, tnk = MTp[('Tf', hf, par)], ('M', 'Tf', hf, par)
                                  else:
                                      Tn, tnk = MTs[('T%d' % nxt, hf)], mk('T%d' % nxt)
                                  bt_ = nbA()
                                  for i in range(4):
                                      sl_ = slice(i * 128, (i + 1) * 128)
                                      mm(ps[bt_][:, sl_], PAn[:, sl_], Tc[:, sl_], True, True, [mk('PA%d' % nxt), mk('T%d' % cur)], [psk[bt_]])
                                  tt('dve', Tn[:], ps[bt_][:], Tc[:], ALU.add, [psk[bt_], mk('T%d' % cur)], [tnk])
                                  yield
                              cur = nxt

                      def gen_seq(c):
                          par = c % 2
                          cg = blk * 4 + c
                          csl = slice(c * 128, (c + 1) * 128)
                          MP = lambda nm, hf: MTp[(nm, hf, par)]
                          MK = lambda nm, hf: ('M', nm, hf, par)
                          hb_cur = Hbf[cg % 2]
                          hk_cur = 'Hbf%d' % (cg % 2)
                          tt('dve', hb_cur[:].rearrange("p (j v) -> p j v", j=4), Hst[:].rearrange("p (j v) -> p j v", j=4),
                             bc_in(emid[:, :, c], 64), ALU.mult, ['Hst'] + [('emid', j) for j in range(4)], [hk_cur])
                          b_rhs = nbB()
                          for h in range(8):
                              jj, base, hf, i = h // 2, (h % 2) * 64, h % 2, h // 2
                              osl = slice(h * 64, (h + 1) * 64)
                              mm(ps[b_rhs][:, osl], MP('AkvT', hf)[:, i * 128:(i + 1) * 128], V_tok[:, c, osl], True, False, [MK('AkvT', hf), ('Vtok', c)], [psk[b_rhs]])
                              mm(ps[b_rhs][:, osl], QKT[base:base + 64, jj, csl], hb_cur[base:base + 64, jj * 64:(jj + 1) * 64], False, True, QK_K + [hk_cur], [psk[b_rhs]])
                          cp('act', RHS_sb[:], ps[b_rhs][:], [psk[b_rhs]], ['RHSsb'])
                          yield
                          b_u = nbB()
                          for h in range(8):
                              hf, i = h % 2, h // 2
                              osl = slice(h * 64, (h + 1) * 64)
                              mm(ps[b_u][:, osl], MP('Tf', hf)[:, i * 128:(i + 1) * 128], RHS_sb[:, osl], True, True, [MK('Tf', hf), 'RHSsb'], [psk[b_u]])
                          cp('dve', U_sb[:], ps[b_u][:], [psk[b_u]], ['Usb'])
                          yield
                          b_y = nbB()
                          for h in range(8):
                              jj, base, hf, i = h // 2, (h % 2) * 64, h % 2, h // 2
                              osl = slice(h * 64, (h + 1) * 64)
                              mm(ps[b_y][:, osl], RtT[base:base + 64, jj, csl], hb_cur[base:base + 64, jj * 64:(jj + 1) * 64], True, False, RT_K + [hk_cur], [psk[b_y]])
                              mm(ps[b_y][:, osl], MP('ArkT', hf)[:, i * 128:(i + 1) * 128], V_tok[:, c, osl], False, False, [MK('ArkT', hf), ('Vtok', c)], [psk[b_y]])
                              mm(ps[b_y][:, osl], MP('nArbT', hf)[:, i * 128:(i + 1) * 128], U_sb[:, osl], False, True, [MK('nArbT', hf), 'Usb'], [psk[b_y]])
                          b_h = nbB()
                          for h in range(8):
                              jj, base = h // 2, (h % 2) * 64
                              osl = slice(h * 64, (h + 1) * 64)
                              mm(ps[b_h][base:base + 64, jj * 64:(jj + 1) * 64], Kt_tok[:, c, osl], V_tok[:, c, osl], True, False, [('Kttok', c), ('Vtok', c)], [psk[b_h]])
                              mm(ps[b_h][base:base + 64, jj * 64:(jj + 1) * 64], nBt_tok[:, c, osl], U_sb[:, osl], False, True, [('nBttok', c), 'Usb'], [psk[b_h]])
                          tt('dve', Htmp[:].rearrange("p (j v) -> p j v", j=4), ps[b_h][:, 0:256].rearrange("p (j v) -> p j v", j=4),
                             bc_in(eLp[:, :, c], 64), ALU.mult, [psk[b_h]] + [('eLp', j) for j in range(4)], ['Htmp'])
                          tt('dve', Hst[:].rearrange("p (j v) -> p j v", j=4), Hst[:].rearrange("p (j v) -> p j v", j=4),
                             bc_in(eL[:, :, c], 64), ALU.mult, ['Hst'] + [('eL', j) for j in range(4)], ['Hst'])
                          tt('dve', Hst[:], Hst[:], Htmp[:], ALU.add, ['Hst', 'Htmp'], ['Hst'])
                          cp('act', y_sb[:], ps[b_y][:], [psk[b_y]], ['ysb'])
                          yield
                          S.op('dve', lambda en: en.tensor_reduce(out=st1[:], in_=y_sb[:].rearrange("p (h v) -> p h v", h=8), axis=AX.X, op=ALU.add), reads=['ysb'], writes=['st1'])
                          act(y3[:], y_sb[:], AF.Square, ['ysb'], ['y3'])
                          S.op('dve', lambda en: en.tensor_reduce(out=st2[:], in_=y3[:].rearrange("p (h v) -> p h v", h=8), axis=AX.X, op=ALU.add), reads=['y3'], writes=['st2'])
                          ts('dve', st1[:], st1[:], 1.0 / 64, None, ALU.mult, None, ['st1'], ['st1'])
                          tt('dve', rs_in[:, 0:8], st1[:], st1[:], ALU.mult, ['st1', 'rs_out'], ['rs_in'])
                          stt('dve', rs_in[:, 0:8], st2[:], 1.0 / 64, rs_in[:, 0:8], ALU.mult, ALU.subtract, ['st2', 'rs_in'], ['rs_in'])
                          ts('dve', rs_in[:, 0:8], rs_in[:, 0:8], 64e-5, None, ALU.add, None, ['rs_in'], ['rs_in'])
                          rsqrt(8, [])
                          tt('dve', y2[:].rearrange("p (h v) -> p h v", h=8), y_sb[:].rearrange("p (h v) -> p h v", h=8), bc_in(st1[:], 64), ALU.subtract, ['ysb', 'st1'], [('y2', j) for j in range(4)])
                          tt('dve', yn_bf[:].rearrange("p (h v) -> p h v", h=8), y2[:].rearrange("p (h v) -> p h v", h=8), bc_in(rs_out[:, 0:8], 64), ALU.mult, [('y2', j) for j in range(4)] + ['rs_out'], ['ynbf'])
                          yield
                          b_t = nbB()
                          psb = ps[b_t][:].bitcast(BF16)
                          for j in range(4):
                              tr(psb[:, j * 128:(j + 1) * 128], yn_bf[:, j * 128:(j + 1) * 128], ident_bf[:], ['ynbf', 'ident_bf'], [psk[b_t]])
                          b_g = nbB()
                          for j in range(4):
                              mm(ps[b_g][:, j * 128:(j + 1) * 128], g2_bf[:, j * 128:(j + 1) * 128], sgx_bf[:, csl], True, True, ['g2_bf', 'sgxbf'], [psk[b_g]])
                          b_b = nbB()
                          for j in range(4):
                              mm(ps[b_b][:, j * 128:(j + 1) * 128], bones[:], rkT_bf[:, j, csl], True, True, ['bones', ('rkT', j)], [psk[b_b]])
                          v4 = lambda ap: ap.rearrange("p (j t) -> p j t", j=4)
                          for j in range(4):
                              act(y2[:, j * 128:(j + 1) * 128], psb[:, j * 128:(j + 1) * 128], AF.Identity, [psk[b_t], 'pp'], [('y2', j)],
                                  bias=pp[:, P_GNB + j:P_GNB + j + 1], scale=pp[:, P_GNW + j:P_GNW + j + 1])
                          tt('dve', v4(y3[:]), v4(ps[b_b][:]), vT_bf[:, :, csl], ALU.mult, [psk[b_b]] + [('vTbf', j) for j in range(4)], ['y3'])
                          yield
                          tt('dve', y3[:], y3[:], y2[:], ALU.add, ['y3'] + [('y2', j) for j in range(4)], ['y3'])
                          tt('dve', yrT[:, :, csl], v4(y3[:]), v4(ps[b_g][:]), ALU.mult, ['y3', psk[b_g]], [('yrT', c)])

                      for c in range(5):
                          gl = []
                          if c < 4:
                              gl.append(gen_mat(c))
                          if c >= 1:
                              gl.append(gen_seq(c - 1))
                          while gl:
                              for g_ in list(gl):
                                  try:
                                      next(g_)
                                  except StopIteration:
                                      gl.remove(g_)
                              stepM()
                      while gM[0] is not None:
                          stepM()
                      S.barrier()
                  S.barrier()
                  smm.close()
              sr.close()
              YR = [('yrT', c) for c in range(4)]
              chk('R%d' % blk)
              if blk == 0:
                  dbg("yrT", yrT[:, :, :], [128, 4, TB], YR)
                  dbg("Hst", Hst[:], [128, 256], ['Hst'])

              YM = [('ymT', c) for c in range(4)]
              chk('M%d' % blk)
              if blk == 0:
                  dbg("ymT", ymT[:, :, :], [128, 4, TB], YM)

              with contextlib.ExitStack() as sf:
                  mrgT = SB(sf, "mrgT_%d" % blk, [128, 8, TB], BF16)
                  x1 = SB(sf, "x1_%d" % blk, [128, 4, D])
                  xre = [SB(sf, "xre%d_%d" % (i, blk), [128, D]) for i in range(2)]
                  thr2 = [SB(sf, "thr%d_%d" % (i, blk), [128, TB]) for i in range(2)]
                  thm2 = [SB(sf, "thm%d_%d" % (i, blk), [128, TB]) for i in range(2)]
                  uu2 = [SB(sf, "uu%d_%d" % (i, blk), [128, TB]) for i in range(2)]
                  ww2 = [SB(sf, "ww%d_%d" % (i, blk), [128, TB]) for i in range(2)]
                  h1T = SB(sf, "h1T_%d" % blk, [128, 32, TB], BF16)
                  rl = [SB(sf, "rl%d_%d" % (i, blk), [128, TB]) for i in range(2)]
                  xs_bf = [SB(sf, "xsbfF%d_%d" % (i, blk), [128, D], BF16) for i in range(2)]
                  junk = SB(sf, "junkF_%d" % blk, [128, D], BF16)
                  ssF = SB(sf, "ssF_%d" % blk, [128, 4])
                  yo = [SB(sf, "yo%d_%d" % (i, blk), [128, D]) for i in range(2)]
                  for hh in range(2):
                      wgr, kgr = w_get('gr%d' % hh)
                      wgm, kgm = w_get('gm%d' % hh)
                      wru, kru = w_get('ru%d' % hh)
                      wmu, kmu = w_get('mu%d' % hh)
                      for p4 in range(4):
                          pc = hh * 4 + p4
                          csl_ = slice(p4 * 128, (p4 + 1) * 128)
                          b1, b2, b3, b4 = nb(), nb(), nb(), nb()
                          for kc in range(8):
                              mm(ps[b1][:], wgr[:, kc, csl_], xnT[:, kc, :], kc == 0, kc == 7, [kgr] + XN, [psk[b1]])
                          for kc in range(8):
                              mm(ps[b2][:], wgm[:, kc, csl_], xnT[:, kc, :], kc == 0, kc == 7, [kgm] + XN, [psk[b2]])
                          for kc in range(4):
                              mm(ps[b3][:], wru[:, kc, csl_], yrT[:, kc, :], kc == 0, kc == 3, [kru] + YR, [psk[b3]])
                          for kc in range(4):
                              mm(ps[b4][:], wmu[:, kc, csl_], ymT[:, kc, :], kc == 0, kc == 3, [kmu] + YM, [psk[b4]])
                          q2 = pc % 2
                          thr, thm, uu, ww = thr2[q2], thm2[q2], uu2[q2], ww2[q2]
                          act(thr[:], ps[b1][:], AF.Tanh, [psk[b1], 'pq'], [('thr', q2)], bias=pq[:, Q_HGTB + pc:Q_HGTB + pc + 1], scale=0.5)
                          act(thm[:], ps[b2][:], AF.Tanh, [psk[b2], 'pq'], [('thm', q2)], bias=pq[:, Q_HGTB + 8 + pc:Q_HGTB + 8 + pc + 1], scale=0.5)
                          stt('dve', uu[:], thr[:], 1.0, ps[b3][:], ALU.add, ALU.mult, [('thr', q2), psk[b3]], [('uu', q2)])
                          stt('dve', ww[:], thm[:], 1.0, ps[b4][:], ALU.add, ALU.mult, [('thm', q2), psk[b4]], [('ww', q2)])
                          tt('dve', mrgT[:, pc, :], uu[:], ww[:], ALU.add, [('uu', q2), ('ww', q2)], [('mrgT', pc)])
                      w_done(4)
                  MG = [('mrgT', pc) for pc in range(8)]
                  wo0, ko0 = w_get('wo0')
                  wo1, ko1 = w_get('wo1')
                  for tt_ in range(4):
                      xr_ = xre[tt_ % 2]
                      xrk = 'xre%d' % (tt_ % 2)
                      S.dma('sp', xr_[:], x_d[t0 + tt_ * 128:t0 + (tt_ + 1) * 128, :], writes=[xrk])
                      for hh, (wo_, ko_) in enumerate(((wo0, ko0), (wo1, ko1))):
                          bi = nb()
                          for kc in range(8):
                              mm(ps[bi][:], mrgT[:, kc, tt_ * 128:(tt_ + 1) * 128], wo_[:, kc, :], kc == 0, kc == 7, [ko_] + MG, [psk[bi]])
                          stt('dve', x1[:, tt_, hh * 512:(hh + 1) * 512], ps[bi][:], 0.5, xr_[:, hh * 512:(hh + 1) * 512], ALU.mult, ALU.add, [psk[bi], xrk], [('x1', tt_, hh)])
                  w_done(2)
                  S.op('pool', lambda en: en.memset(ssF[:], 0.0), writes=[('ssF', i) for i in range(4)])
                  for tt_ in range(4):
                      act(junk[:], x1[:, tt_, :], AF.Square, [('x1', tt_, 0), ('x1', tt_, 1)], ['junkF', ('ssF', tt_)], accum=ssF[:, tt_:tt_ + 1])
                  ts('dve', rs_in[:, 0:4], ssF[:], 1.0 / D, 1e-6, ALU.mult, ALU.add, [('ssF', i) for i in range(4)] + ['rs_out'], ['rs_in'])
                  rsqrt(4, [])
                  for tt_ in range(4):
                      xb_ = xs_bf[tt_ % 2]
                      xk = 'xsbfF%d' % (tt_ % 2)
                      act(xb_[:], x1[:, tt_, :], AF.Copy, [('x1', tt_, 0), ('x1', tt_, 1), 'rs_out'], [xk], scale=rs_out[:, tt_:tt_ + 1])
                      bi = nb()
                      psb = ps[bi][:].bitcast(BF16)
                      for kc in range(8):
                          tr(psb[:, kc * 128:(kc + 1) * 128], xb_[:, kc * 128:(kc + 1) * 128], ident_bf[:], [xk, 'ident_bf'], [psk[bi]])
                      tt('dve', xnT[:, :, tt_ * 128:(tt_ + 1) * 128], psb.rearrange("p (k t) -> p k t", k=8),
                         bc_in(pp[:, P_GFFN:P_GFFN + 8], 128), ALU.mult, [psk[bi], 'pp'], [('xnT', tt_)])
                  pig = (blk == 0 and with_sample)
                  if pig:
                      BUP, BDN = 4, [5, 6]
                      reserved.update([4, 5, 6])
                      rl_e = SB(sf, "rl_e", [128, 512])
                      h1T_e = SB(sf, "h1T_e", [128, 512], BF16)
                      junk_e = SB(sf, "junk_e", [16, D], BF16)
                      ss_e = SB(sf, "ss_e", [16, 1])
                      y_e = SB(sf, "y_e", [16, D])
                  for g in range(8):
                      w1g, k1g = w_get('w1_%d' % g)
                      if pig:
                          for p4 in range(4):
                              pc = g * 4 + p4
                              for kc in range(8):
                                  mm(ps[BUP][:, pc * 16:(pc + 1) * 16], w1g[:, kc, p4 * 128:(p4 + 1) * 128], hnT_sP[:, kc, :], kc == 0, kc == 7, [k1g, 'hnT_sP'], [psk[BUP]])
                      for p4 in range(4):
                          pc = g * 4 + p4
                          bi = nb()
                          for kc in range(8):
                              mm(ps[bi][:], w1g[:, kc, p4 * 128:(p4 + 1) * 128], xnT[:, kc, :], kc == 0, kc == 7, [k1g] + XN, [psk[bi]])
                          r_ = rl[pc % 2]
                          rk_ = 'rl%d' % (pc % 2)
                          act(r_[:], ps[bi][:], AF.Relu, [psk[bi]], [rk_])
                          act(h1T[:, pc, :], r_[:], AF.Square, [rk_], [('h1T', pc)])
                      w_done()
                  if pig:
                      act(rl_e[:], ps[BUP][:], AF.Relu, [psk[BUP]], ['rl_e'])
                      act(h1T_e[:], rl_e[:], AF.Square, ['rl_e'], ['h1T_e'])
                  for hh in range(2):
                      banks = [0, 1, 2, 3]
                      for kg in range(4):
                          w2g, k2g = w_get('w2_%d_%d' % (hh, kg))
                          if pig:
                              for kc in range(8):
                                  kk_ = kg * 8 + kc
                                  mm(ps[BDN[hh]][0:16, :], h1T_e[:, kk_ * 16:(kk_ + 1) * 16], w2g[:, kc, :], kk_ == 0, kk_ == 31, [k2g, 'h1T_e'], [psk[BDN[hh]]])
                          for tt_ in range(4):
                              for kc in range(8):
                                  kk_ = kg * 8 + kc
                                  mm(ps[banks[tt_]][:], h1T[:, kk_, tt_ * 128:(tt_ + 1) * 128], w2g[:, kc, :], kk_ == 0, kk_ == 31, [k2g, ('h1T', kk_)], [psk[banks[tt_]]])
                          w_done()
                      for tt_ in range(4):
                          tt('dve', x1[:, tt_, hh * 512:(hh + 1) * 512], x1[:, tt_, hh * 512:(hh + 1) * 512], ps[banks[tt_]][:], ALU.add,
                             [('x1', tt_, hh), psk[banks[tt_]]], [('x1', tt_, hh)])
                  bank_rr[0] = 4
                  if pig:
                      for hh in range(2):
                          tt('dve', x1_sP[:, hh * 512:(hh + 1) * 512], x1_sP[:, hh * 512:(hh + 1) * 512], ps[BDN[hh]][0:16, :], ALU.add, [('x1_s', hh), psk[BDN[hh]]], [('x1_s', hh)])
                      reserved.difference_update([4, 5, 6])
                      S.op('pool', lambda en: en.memset(ss_e[:], 0.0), writes=['ss_e'])
                      act(junk_e[:], x1_sP[:], AF.Square, [('x1_s', 0), ('x1_s', 1)], ['junk_e', 'ss_e'], accum=ss_e[:, 0:1])
                      ts('dve', rs_in[0:16, 0:1], ss_e[:], 1.0 / D, 1e-6, ALU.mult, ALU.add, ['ss_e', 'rs_out'], ['rs_in'])
                      rsqrt(1, [])
                      stt('dve', y_e[:], x1_sP[:], rs_out[0:16, 0:1], gfin_bc[0:16, :], ALU.mult, ALU.mult, [('x1_s', 0), ('x1_s', 1), 'rs_out', 'gfin_bc'], ['y_e'])
                      S.dma('sp', ys_d[:, :], y_e[:], reads=['y_e'], writes=['o_ys'])
                  S.op('pool', lambda en: en.memset(ssF[:], 0.0), reads=['rs_in'], writes=[('ssF', i) for i in range(4)])
                  for tt_ in range(4):
                      act(junk[:], x1[:, tt_, :], AF.Square, [('x1', tt_, 0), ('x1', tt_, 1)], ['junkF', ('ssF', tt_)], accum=ssF[:, tt_:tt_ + 1])
                  ts('dve', rs_in[:, 0:4], ssF[:], 1.0 / D, 1e-6, ALU.mult, ALU.add, [('ssF', i) for i in range(4)] + ['rs_out'], ['rs_in'])
                  rsqrt(4, [])
                  for tt_ in range(4):
                      yo_ = yo[tt_ % 2]
                      yk = 'yo%d' % (tt_ % 2)
                      stt('dve', yo_[:], x1[:, tt_, :], rs_out[:, tt_:tt_ + 1], gfin_bc[:], ALU.mult, ALU.mult, [('x1', tt_, 0), ('x1', tt_, 1), 'rs_out', 'gfin_bc'], [yk])
                      S.dma('sp', y_d[t0 + tt_ * 128:t0 + (tt_ + 1) * 128, :], yo_[:], reads=[yk], writes=[('o_y', blk, tt_)])
                  S.barrier()

        except _Stop:
            pass
        if nofinal:
            S.finish('sp')
        with contextlib.ExitStack() as se:
            if nofinal:
                se.close()
                return nc, dbg_outs
            So = SB(se, "So", [64, 8, 64])
            Co = SB(se, "Co", [128, 4, 129])
            rM = SB(se, "rM", [4, 1])
            dgM = SB(se, "dgM", [4, 4])
            rMb = SB(se, "rMb", [128, 4])
            mo = SB(se, "mo", [4, 1])
            bis = [nb(), nb()]
            for par in range(2):
                for jj in range(4):
                    h, base = 2 * jj + par, par * 64
                    tr(ps[bis[par]][0:64, jj * 64:(jj + 1) * 64], Hst[base:base + 64, jj * 64:(jj + 1) * 64], ident_f[base:base + 64, base:base + 64], ['Hst', 'ident_f'], [psk[bis[par]]])
            for par in range(2):
                cp('dve', So[:].rearrange("p (j q) k -> p j q k", q=2)[:, :, par, :], ps[bis[par]][0:64, 0:256].rearrange("p (j k) -> p j k", j=4), [psk[bis[par]]], [('So', par)])
            S.dma('sp', wkv_d.rearrange("h v k -> v h k"), So[:], reads=[('So', 0), ('So', 1)], writes=['o_wkv'])
            S.op('dve', lambda en: en.reciprocal(out=rM[:], in_=Mst[:]), reads=['Mst'], writes=['rM'])
            ts('dve', dgM[:], ident_f[0:4, 0:4], rM[:, 0:1], None, ALU.mult, None, ['rM', 'ident_f'], ['dgM'])
            b2 = nb()
            mm(ps[b2][:, 0:4], ones_f[0:4, :], dgM[:], True, True, ['ones_f', 'dgM'], [psk[b2]])
            cp('dve', rMb[:], ps[b2][:, 0:4], [psk[b2]], ['rMb'])
            tt('dve', Co[:], Caug[:], bc_in(rMb[:], 129), ALU.mult, ['Caug', 'rMb'], ['Co'])
            S.dma('sp', C_d.rearrange("h d e -> d h e"), Co[:, :, 0:128], reads=['Co'], writes=['o_C'])
            S.dma('sp', n_d.rearrange("h d -> d h"), Co[:, :, 128], reads=['Co'], writes=['o_n'])
            act(mo[:], Mst[:], AF.Ln, ['Mst'], ['mo'])
            S.dma('sp', m_d[:, :], mo[:], reads=['mo'], writes=['o_m'])
            S.finish('sp')
        print("sched: ninst", S.ninst, "nwaits", S.nwaits, {e: S.cnt[e] for e in S.cnt})
    return nc, dbg_outs


def _prep_shared(inp):
    f = lambda a: np.ascontiguousarray(np.asarray(a, dtype=np.float32))
    w_in = f(inp['w_in'][0])
    sh = {}
    sh['w_in_l'] = f(w_in[:, PERM].reshape(8, 128, 5384).transpose(1, 0, 2))
    sh['r_up_l'] = f(inp['r_up'][0].reshape(4, 128, 1024).transpose(1, 0, 2))
    sh['m_up_l'] = f(inp['m_up'][0].reshape(4, 128, 1024).transpose(1, 0, 2))
    sh['w_out_l'] = f(inp['w_out'][0].reshape(8, 128, 1024).transpose(1, 0, 2))
    sh['w1_l'] = f(inp['ffn_w1'][0].reshape(8, 128, 4096).transpose(1, 0, 2))
    sh['w2_l'] = f(inp['ffn_w2'][0].reshape(32, 128, 1024).transpose(1, 0, 2))
    sh['w2a2'] = f(np.concatenate([inp['r_w2'][0], inp['r_a2'][0]], 0))
    sh['g2'] = f(inp['r_g2'][0])
    sh['wq_l'] = f(inp['m_wq'][0].transpose(1, 0, 2))
    sh['wk_l'] = f(inp['m_wk'][0].transpose(1, 0, 2))
    pp = np.zeros((128, NPC), np.float32)
    mu = np.asarray(inp['r_mu'][0], np.float32)
    for pi, s in enumerate(PIECE_ORIG):
        pp[:, P_MU + pi] = mu[s:s + 128]
    col = lambda v, n: np.asarray(v, np.float32).reshape(n, 128).T
    pp[:, P_W0:P_W0 + 4] = col(inp['r_w0'][0], 4)
    pp[:, P_A0:P_A0 + 4] = col(inp['r_a0'][0], 4)
    pp[:, P_RKK:P_RKK + 4] = col(inp['r_kk'][0], 4)
    pp[:, P_KA:P_KA + 4] = col(inp['r_ka'][0], 4)
    pp[:, P_RRK:P_RRK + 4] = col(inp['r_rk'][0].reshape(512), 4)
    pp[:, P_GNW:P_GNW + 4] = col(inp['r_gn_w'][0], 4)
    pp[:, P_GNB:P_GNB + 4] = col(inp['r_gn_b'][0], 4)
    for tap in range(4):
        pp[:, P_CW + tap * 4:P_CW + tap * 4 + 4] = col(inp['m_conv_w'][0][tap], 4)
    pp[:, P_CB:P_CB + 4] = col(inp['m_conv_b'][0], 4)
    pp[:, P_MGN:P_MGN + 4] = col(inp['m_gn_w'][0], 4)
    pp[:, P_MSK:P_MSK + 4] = col(inp['m_skip'][0], 4)
    pp[:, P_GTB:P_GTB + 16] = col(inp['gate_b'][0], 16)
    pp[:, P_GMIX:P_GMIX + 8] = col(inp['norm_mix_g'][0], 8)
    pp[:, P_GFFN:P_GFFN + 8] = col(inp['norm_ffn_g'][0], 8)
    sh['ppack'] = pp
    sh['gbias_t'] = f(np.concatenate([inp['m_i_b'][0], inp['m_f_b'][0]])[None, :])
    sh['gbias_f'] = f(np.stack([inp['m_i_b'][0], inp['m_f_b'][0]], 1))
    sh['g_final'] = f(np.asarray(inp['norm_final_g'])[None, :])
    sh['g_mix_row'] = f(inp['norm_mix_g'][0][None, :])
    v = lambda a: np.asarray(a, np.float32).reshape(-1)
    mu_perm = np.concatenate([mu[s_:s_ + 128] for s_ in PIECE_ORIG])
    sh['prow1'] = f(np.concatenate([mu_perm, v(inp['r_w0'][0]), v(inp['r_a0'][0]), v(inp['r_kk'][0]), v(inp['r_ka'][0]),
                                    v(inp['r_rk'][0]), v(inp['r_gn_w'][0]), v(inp['r_gn_b'][0])])[None, :])
    sh['prow2'] = f(np.concatenate([v(inp['m_conv_w'][0]), v(inp['m_conv_b'][0]), v(inp['m_gn_w'][0]), v(inp['m_skip'][0])])[None, :])
    sh['prow3'] = f(np.concatenate([v(inp['gate_b'][0]), v(inp['norm_mix_g'][0]), v(inp['norm_ffn_g'][0])])[None, :])
    return sh


_CACHE = {}


def kernel(**inputs):
    debug = bool(inputs.pop('_debug', False))
    stop = inputs.pop('_stop', None)
    key = ('nc', debug, stop)
    if key not in _CACHE:
        _CACHE[key] = build_program(debug=debug, stop=stop)
    nc, dbg_outs = _CACHE[key]
    sh = _prep_shared(inputs)
    xp = np.asarray(inputs['x_prompt'], np.float32)
    f32 = lambda a: np.ascontiguousarray(np.asarray(a, np.float32))
    xs_all = f32(inputs['x_sample'])[:, 0, :]
    sh0_all = f32(inputs['state_rwkv_shift'])[0]
    wkv_all = f32(inputs['state_rwkv_wkv'])[0]
    C_all = f32(inputs['state_mlstm_C'])[0]
    n_all = f32(inputs['state_mlstm_n'])[0]
    m_all = f32(inputs['state_mlstm_m'])[0]
    cv_all = f32(inputs['state_mlstm_conv'])[0]
    in_maps = []
    for c in range(NCORES):
        m = dict(sh)
        m['x'] = np.ascontiguousarray(xp[c])
        sl = slice(16 * c, 16 * (c + 1))
        m['xs'] = np.ascontiguousarray(xs_all[sl])
        m['sh0'] = np.ascontiguousarray(sh0_all[sl])
        m['s_wkv0'] = np.ascontiguousarray(wkv_all[sl].reshape(128, 4096))
        m['s_C0'] = np.ascontiguousarray(C_all[sl].reshape(64, 128, 128))
        m['s_n0'] = np.ascontiguousarray(n_all[sl].reshape(64, 128))
        m['s_m0'] = np.ascontiguousarray(m_all[sl])
        m['s_conv0'] = np.ascontiguousarray(cv_all[sl])
        in_maps.append(m)
    res = run_bass_kernel_spmd(nc, in_maps, core_ids=list(range(NCORES)))
    R = res.results
    B = NCORES
    y_prompt = np.stack([R[c]['y'] for c in range(B)], 0)
    p_shift = np.stack([R[c]['p_shift'].reshape(D) for c in range(B)], 0)[None]
    p_wkv = np.stack([R[c]['p_wkv'] for c in range(B)], 0)[None]
    p_C = np.stack([R[c]['p_C'] for c in range(B)], 0)[None]
    p_n = np.stack([R[c]['p_n'] for c in range(B)], 0)[None]
    p_m = np.stack([R[c]['p_m'].reshape(4) for c in range(B)], 0)[None]
    p_conv = np.stack([R[c]['p_conv'] for c in range(B)], 0)[None]
    cat = lambda k: np.concatenate([R[c][k] for c in range(B)], 0)
    y_sample = cat('ys').reshape(128, 1, D)
    s_shift = cat('s_shift').reshape(1, 128, D)
    s_wkv = cat('s_wkv').reshape(1, 128, 8, 64, 64)
    s_C = cat('s_C').reshape(1, 128, 4, 128, 128)
    s_n = cat('s_n').reshape(1, 128, 4, 128)
    s_m = cat('s_m').reshape(1, 128, 4)
    s_conv = cat('s_conv').reshape(1, 128, 3, 512)
    out = (y_prompt, y_sample, p_shift, p_wkv, p_C, p_n, p_m, p_conv, s_shift, s_wkv, s_C, s_n, s_m, s_conv)
    if debug:
        return out, {k: [R[c]["dbg_" + k] for c in range(B)] for k in dbg_outs}
    return out
```
